# Optimizing a Trainium2 kernel written in Bass

```python
import jax, jax.numpy as jnp
from jax import lax
import numpy as np

D_MODEL = 2048
BATCH = 2
SEQ = 4096
DEPTH = 4

N_MIXERS = 2
N_A = (DEPTH + 1) // 2
N_B = DEPTH // 2
MEM_LEN = 256
EPS = 1e-6
CONV_W = 4

D_RNN = ((4 * D_MODEL // 3 + 128) // 256) * 256
LRU_BLOCKS = 16
LRU_BLOCK_DIM = D_RNN // LRU_BLOCKS
LRU_C = 8.0

GDN_QK_HEADS = 16
GDN_V_HEADS = 32
GDN_HEAD_DIM = 128
GDN_D_QK = GDN_QK_HEADS * GDN_HEAD_DIM
GDN_D_V = GDN_V_HEADS * GDN_HEAD_DIM
GDN_CHUNK = 64

X_HEADS = 4
X_HEAD_DIM = D_MODEL // X_HEADS

D_FF = ((8 * D_MODEL // 3 + 255) // 256) * 256

LRU_IN = 2 * D_RNN + D_MODEL
LRU_OUT_IN = D_RNN + D_MODEL
GDN_IN = 2 * GDN_D_QK + 2 * GDN_D_V + 2 * GDN_V_HEADS + D_MODEL
GDN_OUT_IN = GDN_D_V + D_MODEL

kernel_name = "hybrid_rglru_gated_deltanet_memxattn_swiglu"


def rms_norm(x, g):
    xf = x.astype(jnp.float32)
    y = xf * lax.rsqrt(jnp.mean(xf * xf, axis=-1, keepdims=True) + EPS)
    return (y * g.astype(jnp.float32)).astype(x.dtype)


def causal_depthwise_conv(x, w):
    k_w, c = w.shape
    return lax.conv_general_dilated(
        x, w[:, None, :].astype(x.dtype), window_strides=(1,), padding=[(k_w - 1, 0)],
        dimension_numbers=("NWC", "WIO", "NWC"), feature_group_count=c)


def l2_normalize(t):
    return t * lax.rsqrt(jnp.sum(t * t, axis=-1, keepdims=True) + EPS)


def memory_cross_attention(q, mem_n, w_mem_kv):
    b, s, _ = q.shape
    m = mem_n.shape[1]
    k, v = jnp.split(mem_n @ w_mem_kv, 2, axis=-1)
    q = q.reshape(b, s, X_HEADS, X_HEAD_DIM)
    k = k.reshape(b, m, X_HEADS, X_HEAD_DIM)
    v = v.reshape(b, m, X_HEADS, X_HEAD_DIM)
    scores = jnp.einsum("bshd,bmhd->bhsm", q, k).astype(jnp.float32) * (X_HEAD_DIM ** -0.5)
    p = jax.nn.softmax(scores, axis=-1).astype(v.dtype)
    o = jnp.einsum("bhsm,bmhd->bshd", p, v)
    return o.reshape(b, s, D_MODEL)


def rglru_mixer(u, conv_w, conv_b, gate_a_w, gate_a_b, gate_x_w, gate_x_b, lam):
    b, s, _ = u.shape
    xb, gb = jnp.split(u, 2, axis=-1)
    xb = causal_depthwise_conv(xb, conv_w) + conv_b
    xblk = xb.reshape(b, s, LRU_BLOCKS, LRU_BLOCK_DIM)
    r = jax.nn.sigmoid(jnp.einsum("bshi,hij->bshj", xblk, gate_a_w).reshape(b, s, D_RNN) + gate_a_b)
    i = jax.nn.sigmoid(jnp.einsum("bshi,hij->bshj", xblk, gate_x_w).reshape(b, s, D_RNN) + gate_x_b)
    log_a = -LRU_C * r.astype(jnp.float32) * jax.nn.softplus(-lam.astype(jnp.float32))
    a = jnp.exp(log_a)
    b_in = jnp.sqrt(-jnp.expm1(2.0 * log_a)) * (i * xb).astype(jnp.float32)

    def combine(left, right):
        a1, b1 = left
        a2, b2 = right
        return a1 * a2, a2 * b1 + b2

    _, h = lax.associative_scan(combine, (a, b_in), axis=1)
    return (h * jax.nn.gelu(gb.astype(jnp.float32))).astype(u.dtype)


def chunk_gated_delta_rule(q, k, v, g, beta):
    b, s, h, dk = q.shape
    dv = v.shape[-1]
    c = GDN_CHUNK
    n = s // c

    def to_chunks(t):
        return t.reshape(b, n, c, h, *t.shape[3:]).swapaxes(2, 3)

    q, k, v, g, beta = (to_chunks(t) for t in (q, k, v, g, beta))
    g = jnp.cumsum(g, axis=-1)
    kb = k * beta[..., None]
    vb = v * beta[..., None]
    causal = jnp.tril(jnp.ones((c, c), dtype=bool))
    strict = jnp.tril(jnp.ones((c, c), dtype=bool), k=-1)
    diff = g[..., :, None] - g[..., None, :]
    decay = jnp.where(causal, jnp.exp(jnp.where(causal, diff, 0.0)), 0.0)
    a_mat = jnp.where(strict, jnp.einsum("bnhid,bnhjd->bnhij", kb, k) * decay, 0.0)
    eye = jnp.eye(c, dtype=jnp.float32)
    t_inv = lax.linalg.triangular_solve(eye + a_mat, jnp.broadcast_to(eye, a_mat.shape),
                                        left_side=True, lower=True, unit_diagonal=True)
    w_vals = jnp.einsum("bnhij,bnhjd->bnhid", t_inv, vb)
    k_cum = jnp.einsum("bnhij,bnhjd->bnhid", t_inv, kb * jnp.exp(g)[..., None])
    attn_intra = jnp.where(causal, jnp.einsum("bnhid,bnhjd->bnhij", q, k) * decay, 0.0)
    g_last = g[..., -1]
    k_state = k * jnp.exp(g_last[..., None] - g)[..., None]
    q_dec = q * jnp.exp(g)[..., None]

    def step(state, xs):
        qd, kc, wv, att, ks, gl = xs
        v_new = wv - jnp.einsum("bhcd,bhde->bhce", kc, state)
        o = jnp.einsum("bhcd,bhde->bhce", qd, state) + jnp.einsum("bhij,bhje->bhie", att, v_new)
        state = state * jnp.exp(gl)[..., None, None] + jnp.einsum("bhcd,bhce->bhde", ks, v_new)
        return state, o

    xs = tuple(jnp.moveaxis(t, 1, 0) for t in (q_dec, k_cum, w_vals, attn_intra, k_state, g_last))
    s0 = jnp.zeros((b, h, dk, dv), jnp.float32)
    _, o = lax.scan(step, s0, xs)
    return jnp.moveaxis(o, 0, 1).swapaxes(2, 3).reshape(b, s, h, dv)


def gated_deltanet_mixer(u, conv_w, a_log, dt_bias, o_norm_g):
    b, s, _ = u.shape
    split_at = [GDN_D_QK + GDN_D_QK + GDN_D_V, GDN_D_QK + GDN_D_QK + 2 * GDN_D_V,
                GDN_D_QK + GDN_D_QK + 2 * GDN_D_V + GDN_V_HEADS]
    qkv, z, beta_in, alpha_in = jnp.split(u, split_at, axis=-1)
    qkv = jax.nn.silu(causal_depthwise_conv(qkv, conv_w))
    q, k, v = jnp.split(qkv, [GDN_D_QK, 2 * GDN_D_QK], axis=-1)
    rep = GDN_V_HEADS // GDN_QK_HEADS
    q = l2_normalize(q.astype(jnp.float32).reshape(b, s, GDN_QK_HEADS, GDN_HEAD_DIM))
    k = l2_normalize(k.astype(jnp.float32).reshape(b, s, GDN_QK_HEADS, GDN_HEAD_DIM))
    q = jnp.repeat(q, rep, axis=2) * (GDN_HEAD_DIM ** -0.5)
    k = jnp.repeat(k, rep, axis=2)
    v = v.astype(jnp.float32).reshape(b, s, GDN_V_HEADS, GDN_HEAD_DIM)
    beta = jax.nn.sigmoid(beta_in.astype(jnp.float32))
    g = -jnp.exp(a_log.astype(jnp.float32)) * jax.nn.softplus(
        alpha_in.astype(jnp.float32) + dt_bias.astype(jnp.float32))
    o = chunk_gated_delta_rule(q, k, v, g, beta)
    z = z.astype(jnp.float32).reshape(b, s, GDN_V_HEADS, GDN_HEAD_DIM)
    o = rms_norm(o, o_norm_g) * jax.nn.silu(z)
    return o.reshape(b, s, GDN_D_V).astype(u.dtype)


def setup_inputs(seed: int = 0) -> dict:
    key = jax.random.key(seed)
    ks = jax.random.split(key, 26)
    f32 = jnp.float32

    def nrm(k, shape, fan_in):
        return jax.random.normal(k, shape, f32) * (fan_in ** -0.5)

    def gain(k, shape):
        return 1.0 + 0.02 * jax.random.normal(k, shape, f32)

    u = jax.random.uniform(ks[16], (N_A, D_RNN), f32, 0.9, 0.999)
    sig = u ** (1.0 / LRU_C)
    lru_lambda = jnp.log(sig) - jnp.log1p(-sig)
    dt = jnp.exp(jax.random.uniform(ks[21], (N_B, GDN_V_HEADS), f32, np.log(1e-3), np.log(1e-1)))
    gdn_dt_bias = dt + jnp.log(-jnp.expm1(-dt))
    return {
        "x": jax.random.normal(ks[0], (BATCH, SEQ, D_MODEL), f32),
        "mem": jax.random.normal(ks[1], (BATCH, MEM_LEN, D_MODEL), f32),
        "norm_mix_g": gain(ks[2], (DEPTH, D_MODEL)),
        "norm_mem_g": gain(ks[3], (DEPTH, D_MODEL)),
        "mem_kv_w": nrm(ks[4], (DEPTH, D_MODEL, 2 * D_MODEL), D_MODEL),
        "norm_ffn_g": gain(ks[5], (DEPTH, D_MODEL)),
        "ffn_w_in": nrm(ks[6], (DEPTH, D_MODEL, 2 * D_FF), D_MODEL),
        "ffn_w_out": nrm(ks[7], (DEPTH, D_FF, D_MODEL), D_FF),
        "lru_w_in": nrm(ks[8], (N_A, D_MODEL, LRU_IN), D_MODEL),
        "lru_w_out": nrm(ks[9], (N_A, LRU_OUT_IN, D_MODEL), LRU_OUT_IN),
        "lru_conv_w": nrm(ks[10], (N_A, CONV_W, D_RNN), CONV_W),
        "lru_conv_b": 0.01 * jax.random.normal(ks[11], (N_A, D_RNN), f32),
        "lru_gate_a_w": nrm(ks[12], (N_A, LRU_BLOCKS, LRU_BLOCK_DIM, LRU_BLOCK_DIM), LRU_BLOCK_DIM),
        "lru_gate_a_b": 0.01 * jax.random.normal(ks[13], (N_A, D_RNN), f32),
        "lru_gate_x_w": nrm(ks[14], (N_A, LRU_BLOCKS, LRU_BLOCK_DIM, LRU_BLOCK_DIM), LRU_BLOCK_DIM),
        "lru_gate_x_b": 0.01 * jax.random.normal(ks[15], (N_A, D_RNN), f32),
        "lru_lambda": lru_lambda,
        "gdn_w_in": nrm(ks[17], (N_B, D_MODEL, GDN_IN), D_MODEL),
        "gdn_w_out": nrm(ks[18], (N_B, GDN_OUT_IN, D_MODEL), GDN_OUT_IN),
        "gdn_conv_w": nrm(ks[19], (N_B, CONV_W, 2 * GDN_D_QK + GDN_D_V), CONV_W),
        "gdn_a_log": jnp.log(jax.random.uniform(ks[20], (N_B, GDN_V_HEADS), f32, 1.0, 16.0)),
        "gdn_dt_bias": gdn_dt_bias,
        "gdn_norm_g": gain(ks[22], (N_B, GDN_HEAD_DIM)),
        "final_norm_g": gain(ks[23], (D_MODEL,)),
    }


def reference(x, mem, norm_mix_g, norm_mem_g, mem_kv_w, norm_ffn_g, ffn_w_in, ffn_w_out,
              lru_w_in, lru_w_out, lru_conv_w, lru_conv_b, lru_gate_a_w, lru_gate_a_b,
              lru_gate_x_w, lru_gate_x_b, lru_lambda,
              gdn_w_in, gdn_w_out, gdn_conv_w, gdn_a_log, gdn_dt_bias, gdn_norm_g,
              final_norm_g):
    h = x
    for i in range(DEPTH):
        hn = rms_norm(h, norm_mix_g[i])
        mem_n = rms_norm(mem, norm_mem_g[i])
        j = i // N_MIXERS
        if i % N_MIXERS == 0:
            proj = hn @ lru_w_in[j]
            mix_in, xq = proj[..., :2 * D_RNN], proj[..., 2 * D_RNN:]
            y = rglru_mixer(mix_in, lru_conv_w[j], lru_conv_b[j], lru_gate_a_w[j], lru_gate_a_b[j],
                            lru_gate_x_w[j], lru_gate_x_b[j], lru_lambda[j])
            w_out = lru_w_out[j]
        else:
            proj = hn @ gdn_w_in[j]
            mix_in, xq = proj[..., :GDN_IN - D_MODEL], proj[..., GDN_IN - D_MODEL:]
            y = gated_deltanet_mixer(mix_in, gdn_conv_w[j], gdn_a_log[j], gdn_dt_bias[j], gdn_norm_g[j])
            w_out = gdn_w_out[j]
        xo = memory_cross_attention(xq, mem_n, mem_kv_w[i])
        h = h + jnp.concatenate([y, xo], axis=-1) @ w_out
        hn = rms_norm(h, norm_ffn_g[i])
        gate, up = jnp.split(hn @ ffn_w_in[i], 2, axis=-1)
        h = h + (jax.nn.silu(gate) * up) @ ffn_w_out[i]
    return rms_norm(h, final_norm_g)
```

```python
import numpy as np
from contextlib import ExitStack
import ml_dtypes
import concourse.bass as bass
import concourse.mybir as mybir
from concourse.bass_utils import run_bass_kernel_spmd

F32 = mybir.dt.float32
BF16 = mybir.dt.bfloat16
ALU = mybir.AluOpType
AF = mybir.ActivationFunctionType
AX = mybir.AxisListType
NPBF = ml_dtypes.bfloat16

D = 2048
KD = 16
EPS = 1e-6
MEM = 256
D_RNN = 2816
D_FF = 5632
XH = 4
GQK = 2048
GV = 4096
GDN_IN_MIX = 2 * GQK + 2 * GV + 64
ENGS = ("pe", "dve", "act", "pool", "sp")


class Res:
    __slots__ = ("name", "w", "r")

    def __init__(self, name=""):
        self.name = name
        self.w = None
        self.r = {}


class Prog:
    def __init__(self, nc, es):
        self.nc = nc
        self.es = es
        self.sems = {}
        self.cnt = {}
        self.known = {e: {} for e in ENGS}
        self.streams = {e: [] for e in ENGS}
        for e in ENGS:
            self._newsem(e)
        self.ndma = 0
        self.named = {}

    def _newsem(self, key):
        self.sems[key] = self.es.enter_context(self.nc.semaphore("s%d" % len(self.sems)))
        self.cnt[key] = 0

    def chan(self, name=None):
        if name is not None and name in self.named:
            return self.named[name]
        key = ("dma", self.ndma)
        self.ndma += 1
        self._newsem(key)
        if name is not None:
            self.named[name] = key
        return key

    def cc_chan(self, name):
        if name in self.named:
            return self.named[name]
        key = ("cc", self.ndma)
        self.ndma += 1
        self._newsem(key)
        self.named[name] = key
        return key

    def _deps(self, eng, reads, writes):
        deps = {}

        def add(d):
            if d is not None and deps.get(d[0], 0) < d[1]:
                deps[d[0]] = d[1]
        for r in reads:
            add(r.w)
        for w in writes:
            add(w.w)
            for k, n in w.r.items():
                add((k, n))
        waits = []
        for k, n in deps.items():
            if k == eng and eng == "pe":
                continue
            if isinstance(k, tuple) and k[0] == "dma":
                n = self.cnt[k]
            if self.known[eng].get(k, 0) < n:
                self.known[eng][k] = n
                waits.append((k, n))
        return waits

    def op(self, eng, fn, reads=(), writes=()):
        waits = self._deps(eng, reads, writes)
        self.cnt[eng] += 1
        n = self.cnt[eng]
        for r in reads:
            r.r[eng] = n
        for w in writes:
            w.w = (eng, n)
            w.r = {}
        self.streams[eng].append((waits, fn, (eng, 1)))

    def dma(self, queue, ch, fn, reads=(), writes=(), ndma=1):
        waits = self._deps(queue, reads, writes)
        self.cnt[ch] += 16 * ndma
        n = self.cnt[ch]
        for r in reads:
            r.r[ch] = n
        for w in writes:
            w.w = (ch, n)
            w.r = {}
        self.streams[queue].append((waits, fn, (ch, 16)))

    def coll(self, key, fn, reads=(), writes=()):
        waits = self._deps("pool", reads, writes)
        self.cnt[key] += 1
        n = self.cnt[key]
        for r in reads:
            r.r[key] = n
        for w in writes:
            w.w = (key, n)
            w.r = {}
        self.streams["pool"].append((waits, fn, (key, None)))

    def barrier(self):
        for e in ENGS:
            waits = []
            for k, n in self.cnt.items():
                if n > 0 and self.known[e].get(k, 0) < n and k != e:
                    self.known[e][k] = n
                    waits.append((k, n))
            if waits:
                self.streams[e].append((waits, None, None))

    def emit(self):
        sems = self.sems

        def replay(stream):
            def run(e):
                for waits, fn, inc in stream:
                    for k, n in waits:
                        e.wait_ge(sems[k], n)
                    if fn is None:
                        continue
                    ins = fn(e)
                    if inc[1] is None:
                        ins.then_inc(sems[inc[0]])
                    elif isinstance(ins, (list, tuple)):
                        for i in ins:
                            i.then_inc(sems[inc[0]], inc[1])
                    else:
                        ins.then_inc(sems[inc[0]], inc[1])
            return run
        with self.nc.Block() as block:
            block.tensor(replay(self.streams["pe"]))
            block.vector(replay(self.streams["dve"]))
            block.scalar(replay(self.streams["act"]))
            block.gpsimd(replay(self.streams["pool"]))
            block.sync(replay(self.streams["sp"]))


class Ctx:
    pass


_UID = [0]


def _uid():
    _UID[0] += 1
    return _UID[0]


def make_ctx(nc, es, P, nslots=3):
    c = Ctx()
    c.nc, c.es, c.P = nc, es, P
    sb = lambda name, shape, dt=F32: es.enter_context(nc.sbuf_tensor("sb_" + name, shape, dt))
    c.sb = sb
    c.ones_f = sb("ones_f", [128, 128], F32)
    c.ones_b = sb("ones_b", [128, 128], BF16)
    c.ident = sb("ident", [128, 128], F32)
    c.r_const = Res("const")
    P.op("pool", lambda e: e.memset(c.ones_f[:], 1.0), writes=[c.r_const])
    P.op("pool", lambda e: e.memset(c.ones_b[:], 1.0), writes=[c.r_const])
    P.op("pool", lambda e: e.memset(c.ident[:], 0.0), writes=[c.r_const])
    P.op("pool", lambda e: e.affine_select(out=c.ident[:], in_=c.ident[:], pattern=[[-1, 128]],
                                           compare_op=ALU.not_equal, fill=1.0, base=0,
                                           channel_multiplier=1),
         reads=[c.r_const], writes=[c.r_const])
    c.nslots = nslots
    c.wslots = [sb("wslot%d" % i, [128, 32, 128], BF16) for i in range(nslots)]
    c.wres = [Res("w%d" % i) for i in range(nslots)]
    c.wch = [P.chan() for _ in range(nslots)]
    c.wnext = 0
    c.ps = [es.enter_context(nc.psum_tensor("ps%d" % i, [128, 512], F32)) for i in range(8)]
    c.psr = [Res("ps%d" % i) for i in range(8)]
    c.psn = 0
    c.progress = None
    return c


def next_ps(c, lo=0, hi=4):
    i = lo + (c.psn % (hi - lo))
    c.psn += 1
    return c.ps[i], c.psr[i]


class Prep:
    pass


def wprep(c, blocks, kctot, tasks=None):
    P, nc = c.P, c.nc
    pr = Prep()
    pr.n = len(blocks)
    pr.kctot = kctot
    pr.blocks = blocks
    pr.ap = nc.dram_tensor("wb%d" % _uid(), [len(blocks), 128, kctot * 128], BF16).ap()
    pr.res = [Res("prep") for _ in blocks]
    for i in range(len(blocks)):
        if tasks is None:
            prep_issue(c, pr, i)
        else:
            tasks.append((pr, i))
    return pr


def prep_issue(c, pr, i, gate=()):
    segs = pr.blocks[i]

    def fn(e, segs=segs, i=i):
        out = []
        dst = pr.ap[i].rearrange("p (kc m) -> p kc m", m=128)
        for (W, row0, KC, rpc, col0, MB, off) in segs:
            src = W[row0:row0 + KC * rpc, col0:col0 + MB].rearrange("(kc p) m -> p kc m", p=rpc)
            out.append(e.dma_start(out=dst[0:rpc, off:off + KC, 0:MB], in_=src))
        return out
    c.P.dma("pool", c.P.chan("prep"), fn, reads=list(gate), writes=[pr.res[i]], ndma=len(segs))


def wfetch(c, pr, i):
    P = c.P
    k = c.wnext
    c.wnext = (c.wnext + 1) % c.nslots
    slot, res, ch = c.wslots[k], c.wres[k], c.wch[k]
    P.dma("sp", ch, lambda e, slot=slot, i=i: e.dma_start(out=slot[:, 0:pr.kctot, :].rearrange("p kc m -> p (kc m)"), in_=pr.ap[i]),
          reads=[pr.res[i]], writes=[res])
    return slot, res


def prep_layer_weights(c, w_xq, w_kv, w_out, w_fi, w_fo, nyc, rpc_y, tasks=None):
    d = {}
    d["p_kv"] = wprep(c, [[(w_kv, 0, KD, 128, mo * 128, 128, 0)] for mo in range(2 * KD)], KD, tasks)
    d["p_xq"] = wprep(c, [[(w_xq, 0, KD, 128, b * 128, 128, 0)] for b in range(KD)], KD, tasks)
    ob = []
    for mo in range(KD):
        ob.append([(w_out, 0, 24, rpc_y, mo * 128, 128, 0)])
        ob.append([(w_out, 24 * rpc_y, nyc - 24, rpc_y, mo * 128, 128, 0), (w_out, nyc * rpc_y, KD, 128, mo * 128, 128, nyc - 24)])
    d["p_out"] = wprep(c, ob, 24, tasks)
    d["p_fi"] = wprep(c, [[(w_fi, 0, KD, 128, j * 128, 128, 0), (w_fi, 0, KD, 128, D_FF + j * 128, 128, KD)] for j in range(D_FF // 128)], 2 * KD, tasks)
    d["p_fo"] = wprep(c, [[(w_fo, half * 22 * 128, 22, 128, mo * 128, 128, 0)] for mo in range(KD) for half in range(2)], 22, tasks)
    return d


def prep_gdn_weights(c, w_ap, tasks=None):
    blocks = [[(w_ap, 0, KD, 128, f * 128, 128, 0)] for f in range(24)] + [[(w_ap, 0, KD, 128, 3072, 16, 0)]]
    return wprep(c, blocks, KD, tasks)


def emit_norm(c, h, rh, tsl, T, gcol, hn, rhn, tmp, inplace=False):
    P = c.P
    ps, rps = c.ps[7], c.psr[7]
    for kc in range(KD):
        sq, rsq = tmp.sq[kc % 2], tmp.rsq[kc % 2]
        P.op("act", lambda e, kc=kc, sq=sq: e.activation(out=sq[:, 0:T], in_=h[:, kc, tsl], func=AF.Square),
             reads=[rh], writes=[rsq])
        P.op("pe", lambda e, kc=kc, sq=sq: e.matmul(ps[:, 0:T], lhsT=c.ones_f[:], rhs=sq[:, 0:T],
                                                     start=(kc == 0), stop=(kc == KD - 1)),
             reads=[rsq, c.r_const], writes=[rps])
    P.op("act", lambda e: e.activation(out=tmp.rstd[:, 0:T], in_=ps[:, 0:T], func=AF.Sqrt, bias=EPS, scale=1.0 / D),
         reads=[rps], writes=[tmp.rrstd])
    P.op("dve", lambda e: e.reciprocal(out=tmp.rstd[:, 0:T], in_=tmp.rstd[:, 0:T]),
         reads=[tmp.rrstd], writes=[tmp.rrstd])
    for kc in range(KD):
        P.op("dve", lambda e, kc=kc: e.scalar_tensor_tensor(
            out=(h[:, kc, tsl] if inplace else hn[:, kc, 0:T]), in0=h[:, kc, tsl], scalar=c.vecs[:, gcol + kc:gcol + kc + 1],
            in1=tmp.rstd[:, 0:T], op0=ALU.mult, op1=ALU.mult),
            reads=[rh, tmp.rrstd, c.r_vecs], writes=[rh if inplace else rhn])


class Tmp:
    pass


def make_tmp(c, T):
    t = Tmp()
    t.sq = [c.sb("sq%d" % i, [128, T], F32) for i in range(2)]
    t.rsq = [Res("sq%d" % i) for i in range(2)]
    t.rstd = c.sb("rstd", [128, T], F32)
    t.rrstd = Res("rstd")
    return t


def emit_kv(c, L, mem_ap, p_kv, gcol_mem):
    P, nc = c.P, c.nc
    KT_, V_ = c.KT, c.V
    with ExitStack() as es2:
        u = _uid()
        sb2 = lambda name, shape, dt=F32: es2.enter_context(nc.sbuf_tensor("sk%d_%s" % (u, name), shape, dt))
        memt = sb2("memt", [128, 2, D], F32)
        rmem = Res("memt")
        sqj = sb2("sqj", [128, D], F32)
        rsqj = Res("sqj")
        ssq = sb2("ssq", [128, 2], F32)
        rssq = Res("ssq")
        memT = sb2("memT", [128, KD, MEM], BF16)
        rmemT = Res("memT")
        ch = P.chan("kvmem")
        P.dma("sp", ch, lambda e: e.dma_start(out=memt[:], in_=mem_ap.rearrange("(mc p) d -> p mc d", p=128)),
              writes=[rmem])
        for mc in range(2):
            P.op("act", lambda e, mc=mc: e.activation(out=sqj[:], in_=memt[:, mc, :], func=AF.Square),
                 reads=[rmem], writes=[rsqj])
            P.op("dve", lambda e, mc=mc: e.reduce_sum(out=ssq[:, mc:mc + 1], in_=sqj[:], axis=AX.X),
                 reads=[rsqj], writes=[rssq])
        P.op("act", lambda e: e.activation(out=ssq[:], in_=ssq[:], func=AF.Sqrt, bias=EPS, scale=1.0 / D),
             reads=[rssq], writes=[rssq])
        P.op("dve", lambda e: e.reciprocal(out=ssq[:], in_=ssq[:]), reads=[rssq], writes=[rssq])
        for mc in range(2):
            P.op("dve", lambda e, mc=mc: e.tensor_scalar(out=memt[:, mc, :], in0=memt[:, mc, :],
                                                           scalar1=ssq[:, mc:mc + 1], scalar2=None, op0=ALU.mult),
                 reads=[rmem, rssq], writes=[rmem])
        for kc in range(KD):
            for mc in range(2):
                ps, rps = next_ps(c)
                P.op("pe", lambda e, kc=kc, mc=mc, ps=ps: e.transpose(ps[:, 0:128], memt[:, mc, kc * 128:(kc + 1) * 128], c.ident[:]),
                     reads=[rmem, c.r_const], writes=[rps])
                P.op("act", lambda e, kc=kc, mc=mc, ps=ps: e.activation(
                    out=memT[:, kc, mc * 128:(mc + 1) * 128], in_=ps[:, 0:128], func=AF.Copy,
                    scale=c.vecs[:, gcol_mem + kc:gcol_mem + kc + 1]),
                    reads=[rps, c.r_vecs], writes=[rmemT])
        for mo in range(KD):
            slot, rw = wfetch(c, p_kv, mo)
            ps, rps = next_ps(c)

            def mm(e, slot=slot, ps=ps):
                r = None
                for kc in range(KD):
                    r = e.matmul(ps[:, 0:MEM], lhsT=slot[:, kc, 0:128], rhs=memT[:, kc, :], start=(kc == 0), stop=(kc == KD - 1))
                return r
            P.op("pe", mm, reads=[rw, rmemT], writes=[rps])
            P.op("act", lambda e, mo=mo, ps=ps: e.copy(out=KT_[:, mo, :], in_=ps[:, 0:MEM]), reads=[rps], writes=[c.rKT])
        for cb in range(KD):
            slot, rw = wfetch(c, p_kv, KD + cb)
            for mc in range(2):
                ps, rps = next_ps(c)

                def mm(e, slot=slot, ps=ps, mc=mc):
                    r = None
                    for kc in range(KD):
                        r = e.matmul(ps[:, 0:128], lhsT=memT[:, kc, mc * 128:(mc + 1) * 128], rhs=slot[:, kc, 0:128],
                                     start=(kc == 0), stop=(kc == KD - 1))
                    return r
                P.op("pe", mm, reads=[rw, rmemT], writes=[rps])
                P.op("dve", lambda e, cb=cb, mc=mc, ps=ps: e.tensor_copy(out=V_[:, mc, cb * 128:(cb + 1) * 128], in_=ps[:, 0:128]),
                     reads=[rps], writes=[c.rV])
        P.barrier()


def emit_tok_pass(c, lay, tsl, T, hn_dst, out_dst, do_main=True):
    P, nc = c.P, c.nc
    h, rh = c.h, c.rh
    hn, rhn = c.hn, c.rhn
    big, rbig = c.big, c.rbig
    tmp = c.tmp
    nyc, rpc_y = lay["nyc"], lay["rpc_y"]
    scale = float(512 ** -0.5)

    if do_main:
        emit_tok_main(c, lay, tsl, T)
    if hn_dst is not None:
        emit_norm(c, h, rh, tsl, T, lay["g_next"], hn, rhn, tmp)
        hn_dst(tsl)
    if out_dst is not None:
        emit_norm(c, h, rh, tsl, T, lay["g_next"], None, None, tmp, inplace=True)
        P.dma("sp", c.och, lambda e: e.dma_start(out=out_dst(tsl), in_=h[:, :, tsl]), reads=[rh])


def emit_tok_main(c, lay, tsl, T):
    P, nc = c.P, c.nc
    h, rh = c.h, c.rh
    hn, rhn = c.hn, c.rhn
    big, rbig = c.big, c.rbig
    tmp = c.tmp
    nyc, rpc_y = lay["nyc"], lay["rpc_y"]
    scale = float(512 ** -0.5)
    KT_, V_, qh_, ex_, rden_ = c.KT, c.V, c.qh, c.ex, c.rden
    emit_norm(c, h, rh, tsl, T, lay["g_mix"], hn, rhn, tmp)
    ych = c.ych
    P.dma("sp", ych, lambda e: e.dma_start(out=big[0:rpc_y, 0:nyc, 0:T], in_=lay["y_ap"](tsl, e)), reads=list(lay.get("y_res", ())), writes=[rbig])
    for hh in range(XH):
        for dc in range(4):
            slot, rw = wfetch(c, lay["p_xq"], hh * 4 + dc)
            ps, rps = next_ps(c)

            def mm(e, slot=slot, ps=ps):
                r = None
                for kc in range(KD):
                    r = e.matmul(ps[:, 0:T], lhsT=slot[:, kc, 0:128], rhs=hn[:, kc, 0:T], start=(kc == 0), stop=(kc == KD - 1))
                return r
            P.op("pe", mm, reads=[rw, rhn], writes=[rps])
            P.op("act", lambda e, dc=dc, ps=ps: e.copy(out=qh_[:, dc, 0:T], in_=ps[:, 0:T]), reads=[rps], writes=[c.rqh])
        for mc in range(2):
            ps, rps = next_ps(c, 4, 6)

            def mm(e, ps=ps, mc=mc, hh=hh):
                r = None
                for dc in range(4):
                    r = e.matmul(ps[:, 0:T], lhsT=KT_[:, hh * 4 + dc, mc * 128:(mc + 1) * 128], rhs=qh_[:, dc, 0:T],
                                 start=(dc == 0), stop=(dc == 3))
                return r
            P.op("pe", mm, reads=[c.rKT, c.rqh], writes=[rps])
            P.op("act", lambda e, ps=ps, mc=mc: e.activation(out=ex_[:, mc, 0:T], in_=ps[:, 0:T], func=AF.Exp, scale=scale),
                 reads=[rps], writes=[c.rex])
        ps, rps = c.ps[6], c.psr[6]

        def mmd(e, ps=ps):
            r = None
            for mc in range(2):
                r = e.matmul(ps[:, 0:T], lhsT=c.ones_b[:], rhs=ex_[:, mc, 0:T], start=(mc == 0), stop=(mc == 1))
            return r
        P.op("pe", mmd, reads=[c.rex, c.r_const], writes=[rps])
        P.op("dve", lambda e, ps=ps: e.reciprocal(out=rden_[:, 0:T], in_=ps[:, 0:T]), reads=[rps], writes=[c.rrden])
        for dc in range(4):
            ps, rps = next_ps(c, 4, 6)

            def mmo(e, ps=ps, dc=dc, hh=hh):
                r = None
                for mc in range(2):
                    r = e.matmul(ps[:, 0:T], lhsT=V_[:, mc, hh * 512 + dc * 128: hh * 512 + (dc + 1) * 128],
                                 rhs=ex_[:, mc, 0:T], start=(mc == 0), stop=(mc == 1))
                return r
            P.op("pe", mmo, reads=[c.rV, c.rex], writes=[rps])
            P.op("dve", lambda e, ps=ps, dc=dc, hh=hh: e.tensor_tensor(out=big[:, nyc + hh * 4 + dc, 0:T], in0=ps[:, 0:T],
                                                                       in1=rden_[:, 0:T], op=ALU.mult),
                 reads=[rps, c.rrden], writes=[rbig])
    for mo in range(KD):
        ps, rps = next_ps(c)
        slot, rw = wfetch(c, lay["p_out"], 2 * mo)

        def mm0(e, slot=slot, ps=ps):
            r = None
            for kc in range(24):
                r = e.matmul(ps[:, 0:T], lhsT=slot[0:rpc_y, kc, 0:128], rhs=big[0:rpc_y, kc, 0:T], start=(kc == 0), stop=False)
            return r
        P.op("pe", mm0, reads=[rw, rbig], writes=[rps])
        slot, rw = wfetch(c, lay["p_out"], 2 * mo + 1)

        def mm1(e, slot=slot, ps=ps):
            r = None
            for kc in range(24, nyc):
                r = e.matmul(ps[:, 0:T], lhsT=slot[0:rpc_y, kc - 24, 0:128], rhs=big[0:rpc_y, kc, 0:T], start=False, stop=False)
            for kc in range(nyc, nyc + KD):
                r = e.matmul(ps[:, 0:T], lhsT=slot[:, kc - 24, 0:128], rhs=big[:, kc, 0:T], start=False, stop=(kc == nyc + KD - 1))
            return r
        P.op("pe", mm1, reads=[rw, rbig], writes=[rps])
        P.op("dve", lambda e, mo=mo, ps=ps: e.tensor_tensor(out=h[:, mo, tsl], in0=ps[:, 0:T], in1=h[:, mo, tsl], op=ALU.add),
             reads=[rps, rh], writes=[rh])

    emit_norm(c, h, rh, tsl, T, lay["g_ffn"], hn, rhn, tmp)
    NJ = D_FF // 128
    for j in range(NJ):
        slot, rw = wfetch(c, lay["p_fi"], j)
        psg, rpsg = next_ps(c)
        psu, rpsu = next_ps(c)

        def mm(e, slot=slot, psg=psg, psu=psu):
            r = None
            for kc in range(KD):
                r = e.matmul(psg[:, 0:T], lhsT=slot[:, kc, 0:128], rhs=hn[:, kc, 0:T], start=(kc == 0), stop=(kc == KD - 1))
            for kc in range(KD):
                r = e.matmul(psu[:, 0:T], lhsT=slot[:, KD + kc, 0:128], rhs=hn[:, kc, 0:T], start=(kc == 0), stop=(kc == KD - 1))
            return r
        P.op("pe", mm, reads=[rw, rhn], writes=[rpsg, rpsu])
        sg, rsg = tmp.sq[j % 2], tmp.rsq[j % 2]
        P.op("act", lambda e, psg=psg, sg=sg: e.activation(out=sg[:, 0:T], in_=psg[:, 0:T], func=AF.Silu), reads=[rpsg], writes=[rsg])
        P.op("dve", lambda e, j=j, psu=psu, sg=sg: e.tensor_tensor(out=big[:, j, 0:T], in0=psu[:, 0:T], in1=sg[:, 0:T], op=ALU.mult),
             reads=[rpsu, rsg], writes=[rbig])
    for mo in range(KD):
        ps, rps = next_ps(c)
        for half in range(2):
            slot, rw = wfetch(c, lay["p_fo"], mo * 2 + half)

            def mm(e, slot=slot, ps=ps, half=half):
                r = None
                for kc in range(22):
                    r = e.matmul(ps[:, 0:T], lhsT=slot[:, kc, 0:128], rhs=big[:, half * 22 + kc, 0:T],
                                 start=(half == 0 and kc == 0), stop=(half == 1 and kc == 21))
                return r
            P.op("pe", mm, reads=[rw, rbig], writes=[rps])
        P.op("dve", lambda e, mo=mo, ps=ps: e.tensor_tensor(out=h[:, mo, tsl], in0=ps[:, 0:T], in1=h[:, mo, tsl], op=ALU.add),
             reads=[rps, rh], writes=[rh])


def emit_lru(c, S, TT, hn_src, wx_ap, wg_ap, ga_ap, gx_ap, lvec_ap, y_dst, tile_done=None, rhn_d=(), ry_d=(), after_setup=None):
    P, nc = c.P, c.nc
    NT = S // TT
    with ExitStack() as es2:
        cnt = [0]
        u = _uid()

        def sb2(name, shape, dt=F32):
            cnt[0] += 1
            return es2.enter_context(nc.sbuf_tensor("sl%d_%s_%d" % (u, name, cnt[0]), shape, dt))
        Wx = sb2("Wx", [128, KD, 704], BF16)
        Wg = sb2("Wg", [128, KD, 704], BF16)
        Ga = sb2("Ga", [88, 4, 2, 176], BF16)
        Gx = sb2("Gx", [88, 4, 2, 176], BF16)
        lv = sb2("lv", [88, 8, 8], F32)
        nsp = sb2("nsp", [88, 8], F32)
        rWl = []
        ch = P.chan("lruw")

        def ld(queue, fn):
            r = Res("lw")
            rWl.append(r)
            P.dma(queue, ch, fn, writes=[r])
        for q in range(4):
            ld("pool", lambda e, q=q: e.dma_start(
                out=Wx[:, q * 4:(q + 1) * 4, :], in_=wx_ap[q * 512:(q + 1) * 512, :].rearrange("(kc p) m -> p kc m", p=128)))
            ld("pool", lambda e, q=q: e.dma_start(
                out=Wg[:, q * 4:(q + 1) * 4, :], in_=wg_ap[q * 512:(q + 1) * 512, :].rearrange("(kc p) m -> p kc m", p=128)))
        ld("pool", lambda e: e.dma_start(out=Ga[:], in_=ga_ap.rearrange("b (kh p) n -> p b kh n", p=88)))
        ld("pool", lambda e: e.dma_start(out=Gx[:], in_=gx_ap.rearrange("b (kh p) n -> p b kh n", p=88)))
        ld("sp", lambda e: e.dma_start(out=lv[:], in_=lvec_ap))
        if after_setup is not None:
            after_setup()
        rW = Res("lruW")
        P.op("act", lambda e: e.activation(out=nsp[:], in_=lv[:, :, 7], func=AF.Exp, scale=-1.0), reads=rWl, writes=[rW])
        P.op("act", lambda e: e.activation(out=nsp[:], in_=nsp[:], func=AF.Ln, bias=1.0), reads=[rW], writes=[rW])
        P.op("dve", lambda e: e.tensor_scalar(out=nsp[:], in0=nsp[:], scalar1=-8.0, scalar2=None, op0=ALU.mult), reads=[rW], writes=[rW])

        hnb = [sb2("hnb", [128, KD, TT], BF16) for _ in range(2)]
        rhnb = [Res("hnb0"), Res("hnb1")]
        hch = [P.chan("hch0"), P.chan("hch1")]
        xpad = sb2("xpad", [88, 8, TT + 3], F32)
        rxpad = [Res("xpad%d" % i) for i in range(8)]
        xc = sb2("xc", [88, 8, TT], F32)
        rxc = [Res("xc%d" % i) for i in range(8)]
        xcb = sb2("xcb", [88, 8, TT], BF16)
        rxcb = [Res("xcb%d" % i) for i in range(8)]
        hst = sb2("hst", [88, 8], F32)
        rhst = [Res("hst%d" % i) for i in range(8)]
        P.op("dve", lambda e: e.memset(xpad[:], 0.0), writes=rxpad)
        P.op("dve", lambda e: e.memset(hst[:], 0.0), writes=rhst)
        roles = ["bx", "hs", "gl"]
        tb = {r: [sb2(r, [88, TT], F32) for _ in range(2)] for r in roles}
        tr = {r: [Res(r + "0"), Res(r + "1")] for r in roles}
        tA = sb2("tA", [88, TT], F32); rtA = Res("tA")
        tI = sb2("tI", [88, 4, TT], F32); rtI = [Res("tI%d" % i) for i in range(4)]
        aB = sb2("aB", [88, 4, TT], F32); raB = [Res("aB%d" % i) for i in range(4)]
        sB = sb2("sB", [88, 4, TT], F32); rsB = Res("sB")
        hb2 = sb2("hb2", [88, 8, 2], F32)
        nsph = sb2("nsph", [88, 8], F32)
        P.op("dve", lambda e: e.tensor_scalar(out=hb2[:], in0=lv[:, :, 5:7], scalar1=0.5, scalar2=None, op0=ALU.mult), reads=rWl, writes=[rW])
        P.op("dve", lambda e: e.tensor_scalar(out=nsph[:], in0=nsp[:], scalar1=0.5, scalar2=None, op0=ALU.mult), reads=[rW], writes=[rW])
        ytl = sb2("yt", [88, 8, TT], BF16)
        ryt = [Res("yt%d" % i) for i in range(8)]
        ych = [P.chan("ych0"), P.chan("ych1")]

        def lvs(cc, k):
            return lv[:, cc, k:k + 1]

        for t in range(NT):
            hb, rhb = hnb[t % 2], rhnb[t % 2]
            tsl = slice(t * TT, (t + 1) * TT)
            P.dma("sp", hch[t % 2], lambda e, hb=hb, t=t: e.dma_start(out=hb[:], in_=hn_src(t)), reads=list(rhn_d), writes=[rhb])
            for cc in range(8):
                ps, rps = next_ps(c)

                def mm(e, ps=ps, cc=cc, hb=hb):
                    r = None
                    for kc in range(KD):
                        r = e.matmul(ps[0:88, 0:TT], lhsT=Wx[:, kc, cc * 88:(cc + 1) * 88], rhs=hb[:, kc, :], start=(kc == 0), stop=(kc == KD - 1))
                    return r
                P.op("pe", mm, reads=[rW, rhb], writes=[rps])
                P.op("act", lambda e, ps=ps, cc=cc: e.copy(out=xpad[:, cc, 3:3 + TT], in_=ps[0:88, 0:TT]), reads=[rps], writes=[rxpad[cc]])
                P.op("dve", lambda e, cc=cc: e.tensor_scalar(out=xc[:, cc, :], in0=xpad[:, cc, 0:TT], scalar1=lvs(cc, 0), scalar2=lvs(cc, 4),
                                                             op0=ALU.mult, op1=ALU.add), reads=[rxpad[cc], rW], writes=[rxc[cc]])
                for k in range(1, 4):
                    P.op("dve", lambda e, cc=cc, k=k: e.scalar_tensor_tensor(out=xc[:, cc, :], in0=xpad[:, cc, k:k + TT], scalar=lvs(cc, k),
                                                                             in1=xc[:, cc, :], op0=ALU.mult, op1=ALU.add),
                         reads=[rxpad[cc], rxc[cc], rW], writes=[rxc[cc]])
                P.op("dve", lambda e, cc=cc: e.tensor_copy(out=xpad[:, cc, 0:3], in_=xpad[:, cc, TT:TT + 3]), reads=[rxpad[cc]], writes=[rxpad[cc]])
                P.op("act", lambda e, cc=cc: e.copy(out=xcb[:, cc, :], in_=xc[:, cc, :]), reads=[rxc[cc]], writes=[rxcb[cc]])
            for g4 in range(2):
                for cc in range(g4 * 4, g4 * 4 + 4):
                    bl, oh = cc // 2, cc % 2
                    ci4 = cc % 4
                    for (G, dstt, rdst_, bcol) in ((Ga, tA, rtA, 0), (Gx, tI, rtI[ci4], 1)):
                        ps, rps = next_ps(c, 4, 7)

                        def mmg(e, ps=ps, G=G, bl=bl, oh=oh):
                            r = None
                            for kh in range(2):
                                r = e.matmul(ps[0:88, 0:TT], lhsT=G[:, bl, kh, oh * 88:(oh + 1) * 88], rhs=xcb[:, 2 * bl + kh, :], start=(kh == 0), stop=(kh == 1))
                            return r
                        P.op("pe", mmg, reads=[rW, rxcb[2 * bl], rxcb[2 * bl + 1]], writes=[rps])
                        dst_ap = tA[:] if bcol == 0 else tI[:, ci4, :]
                        P.op("act", lambda e, ps=ps, dst_ap=dst_ap, bcol=bcol, cc=cc: e.activation(out=dst_ap, in_=ps[0:88, 0:TT], func=AF.Tanh, bias=hb2[:, cc, bcol:bcol + 1], scale=0.5),
                             reads=[rps, rW], writes=[rdst_])
                    P.op("act", lambda e, cc=cc, ci4=ci4: e.activation(out=aB[:, ci4, :], in_=tA[:], func=AF.Exp, scale=nsph[:, cc:cc + 1], bias=nsph[:, cc:cc + 1]),
                         reads=[rtA, rW], writes=[raB[ci4]])
                    P.op("act", lambda e, ci4=ci4: e.activation(out=sB[:, ci4, :], in_=aB[:, ci4, :], func=AF.Square), reads=[raB[ci4]], writes=[rsB])
                P.op("act", lambda e: e.activation(out=sB[:].rearrange("p c t -> p (c t)"), in_=sB[:].rearrange("p c t -> p (c t)"), func=AF.Sqrt, bias=0.25, scale=-0.25),
                     reads=[rsB], writes=[rsB])
                for cc in range(g4 * 4, g4 * 4 + 4):
                    ci4 = cc % 4
                    pb = cc % 2
                    B = {r: tb[r][pb] for r in roles}
                    R = {r: tr[r][pb] for r in roles}
                    P.op("dve", lambda e, cc=cc, ci4=ci4, B=B: e.scalar_tensor_tensor(out=B["bx"][:], in0=tI[:, ci4, :], scalar=1.0, in1=xc[:, cc, :], op0=ALU.add, op1=ALU.mult),
                         reads=[rtI[ci4], rxc[cc]], writes=[R["bx"]])
                    P.op("dve", lambda e, ci4=ci4, B=B: e.tensor_tensor(out=B["bx"][:], in0=B["bx"][:], in1=sB[:, ci4, :], op=ALU.mult), reads=[R["bx"], rsB], writes=[R["bx"]])
                    P.op("dve", lambda e, cc=cc, ci4=ci4, B=B: e.tensor_tensor_scan(out=B["hs"][:], data0=aB[:, ci4, :], data1=B["bx"][:], initial=hst[:, cc:cc + 1],
                                                                                     op0=ALU.mult, op1=ALU.add), reads=[raB[ci4], R["bx"], rhst[cc]], writes=[R["hs"]])
                    P.op("dve", lambda e, cc=cc, B=B: e.tensor_copy(out=hst[:, cc:cc + 1], in_=B["hs"][:, TT - 1:TT]), reads=[R["hs"]], writes=[rhst[cc]])
                    ps, rps = next_ps(c)

                    def mmgb(e, ps=ps, cc=cc, hb=hb):
                        r = None
                        for kc in range(KD):
                            r = e.matmul(ps[0:88, 0:TT], lhsT=Wg[:, kc, cc * 88:(cc + 1) * 88], rhs=hb[:, kc, :], start=(kc == 0), stop=(kc == KD - 1))
                        return r
                    P.op("pe", mmgb, reads=[rW, rhb], writes=[rps])
                    P.op("act", lambda e, ps=ps, B=B: e.activation(out=B["gl"][:], in_=ps[0:88, 0:TT], func=AF.Square), reads=[rps], writes=[R["gl"]])
                    P.op("dve", lambda e, B=B: e.tensor_scalar(out=B["gl"][:], in0=B["gl"][:], scalar1=0.044715, scalar2=1.0, op0=ALU.mult, op1=ALU.add), reads=[R["gl"]], writes=[R["gl"]])
                    P.op("dve", lambda e, ps=ps, B=B: e.tensor_tensor(out=B["gl"][:], in0=ps[0:88, 0:TT], in1=B["gl"][:], op=ALU.mult), reads=[rps, R["gl"]], writes=[R["gl"]])
                    P.op("act", lambda e, B=B: e.activation(out=B["gl"][:], in_=B["gl"][:], func=AF.Tanh, scale=0.7978845608028654), reads=[R["gl"]], writes=[R["gl"]])
                    P.op("dve", lambda e, ps=ps, B=B: e.scalar_tensor_tensor(out=B["gl"][:], in0=B["gl"][:], scalar=1.0, in1=ps[0:88, 0:TT], op0=ALU.add, op1=ALU.mult),
                         reads=[rps, R["gl"]], writes=[R["gl"]])
                    P.op("dve", lambda e, B=B, cc=cc: e.scalar_tensor_tensor(out=ytl[:, cc, :], in0=B["hs"][:], scalar=0.5, in1=B["gl"][:], op0=ALU.mult, op1=ALU.mult),
                         reads=[R["hs"], R["gl"]], writes=[ryt[cc]])
                    P.dma("sp", ych[pb], lambda e, cc=cc, t=t: e.dma_start(out=y_dst(t, cc * 88, (cc + 1) * 88), in_=ytl[:, cc, :]), reads=[ryt[cc]],
                          writes=list(ry_d(t)) if callable(ry_d) else [])
                    if c.progress is not None:
                        c.progress([ryt[cc]])
            if tile_done is not None:
                tile_done(t, ryt[7])
        P.barrier()


def _pass_bufs(c, es2, T, tag):
    nc = c.nc
    sb2 = lambda name, shape, dt=F32: es2.enter_context(nc.sbuf_tensor("sp_%s_%s" % (name, tag), shape, dt))
    c.hn = sb2("hn", [128, KD, T], BF16); c.rhn = Res("hn")
    c.big = sb2("big", [128, 48, T], BF16); c.rbig = Res("big")
    c.qh = sb2("qh", [128, 4, T], BF16); c.rqh = Res("qh")
    c.ex = sb2("ex", [128, 2, T], BF16); c.rex = Res("ex")
    c.rden = sb2("rden", [128, T], F32); c.rrden = Res("rden")
    t = Tmp()
    t.sq = [sb2("sq%d" % i, [128, T], F32) for i in range(2)]
    t.rsq = [Res("sq%d" % i) for i in range(2)]
    t.rstd = sb2("rstd", [128, T], F32)
    t.rrstd = Res("rstd")
    c.tmp = t


def build_tok_program(Sc, T, kind, first, final, do_main=True):
    nc = bass.Bass("TRN2", target_bir_lowering=False)
    dt = lambda name, shape, dtype, kind_: nc.dram_tensor(name, shape, dtype, kind=kind_).ap()
    hin = dt("hin", [D, Sc], F32, "ExternalInput")
    vecs = dt("vecs", [128, 64], F32, "ExternalInput")
    nyc, rpc_y = (32, 88) if kind == "lru" else (32, 128)
    if do_main:
        y = dt("y", [nyc * rpc_y, Sc], BF16, "ExternalInput")
        mem = dt("mem", [MEM, D], F32, "ExternalInput")
        w_xq = dt("w_xq", [D, D], F32, "ExternalInput")
        w_kv = dt("w_kv", [D, 2 * D], F32, "ExternalInput")
        w_out = dt("w_out", [nyc * rpc_y + D, D], F32, "ExternalInput")
        w_fi = dt("w_fi", [D, 2 * D_FF], F32, "ExternalInput")
        w_fo = dt("w_fo", [D_FF, D], F32, "ExternalInput")
    if final:
        outd = dt("out", [D, Sc], F32, "ExternalOutput")
    else:
        hout = dt("hout", [D, Sc], F32, "ExternalOutput")
        hnn = dt("hnn", [D, Sc], BF16, "ExternalOutput")
    with ExitStack() as es:
        P = Prog(nc, es)
        c = make_ctx(nc, es, P)
        c.vecs = c.sb("vecs", [128, 64], F32)
        c.r_vecs = Res("vecs")
        c.h = c.sb("h", [128, KD, Sc], F32)
        c.rh = Res("h")
        c.KT = c.sb("KT", [128, KD, MEM], BF16); c.rKT = Res("KT")
        c.V = c.sb("V", [128, 2, D], BF16); c.rV = Res("V")
        c.ych = P.chan()
        c.och = P.chan()
        P.dma("sp", P.chan(), lambda e: e.dma_start(out=c.vecs[:], in_=vecs), writes=[c.r_vecs])
        P.dma("sp", P.chan(), lambda e: e.dma_start(out=c.h[:], in_=hin.rearrange("(kc p) t -> p kc t", p=128)), writes=[c.rh])
        lay = dict(g_mix=0, g_mem=16, g_ffn=32, g_next=48, nyc=nyc, rpc_y=rpc_y)
        if do_main:
            lay.update(y_ap=lambda tsl, e: y[:, tsl].rearrange("(kc p) t -> p kc t", p=rpc_y))
            lay.update(prep_layer_weights(c, w_xq, w_kv, w_out, w_fi, w_fo, nyc, rpc_y))
            emit_kv(c, lay, mem, lay["p_kv"], lay["g_mem"])
        with ExitStack() as es2:
            _pass_bufs(c, es2, T, "p")
            for p0 in range(0, Sc, T):
                tsl = slice(p0, p0 + T)
                emit_tok_pass(c, lay, tsl, T,
                              hn_dst=None if final else (lambda tsl: P.dma("sp", c.och, lambda e: e.dma_start(
                                  out=hnn[:, tsl].rearrange("(kc p) t -> p kc t", p=128), in_=c.hn[:, :, 0:T]), reads=[c.rhn])),
                              out_dst=(lambda tsl: outd[:, tsl].rearrange("(kc p) t -> p kc t", p=128)) if final else None,
                              do_main=do_main)
            if not final:
                P.dma("sp", c.och, lambda e: e.dma_start(out=hout.rearrange("(kc p) t -> p kc t", p=128), in_=c.h[:]), reads=[c.rh])
            P.barrier()
        P.emit()
    return nc


def build_lru_program(S, TT):
    nc = bass.Bass("TRN2", target_bir_lowering=False)
    dt = lambda name, shape, dtype, kind_: nc.dram_tensor(name, shape, dtype, kind=kind_).ap()
    hn = dt("hn", [D, S], BF16, "ExternalInput")
    wx = dt("wx", [D, 704], F32, "ExternalInput")
    wg = dt("wg", [D, 704], F32, "ExternalInput")
    ga = dt("ga", [4, 176, 176], F32, "ExternalInput")
    gx = dt("gx", [4, 176, 176], F32, "ExternalInput")
    lvec = dt("lvec", [88, 8, 8], F32, "ExternalInput")
    y = dt("y", [704, S], BF16, "ExternalOutput")
    with ExitStack() as es:
        P = Prog(nc, es)
        c = make_ctx(nc, es, P, nslots=1)
        emit_lru(c, S, TT, lambda t: hn[:, t * TT:(t + 1) * TT].rearrange("(kc p) t -> p kc t", p=128), wx, wg, ga, gx, lvec,
                 lambda t, r0, r1: y[r0:r1, t * TT:(t + 1) * TT])
        P.emit()
    return nc


def emit_gdn(c, S, TT, hn_src, p_w, cvec_ap, gvec_ap, pvec_ap, y_dst, tile_done=None, rhn_d=(), ry_d=()):
    P, nc = c.P, c.nc
    NT = S // TT
    NCH = TT // 64
    C = 64
    H = 8
    with ExitStack() as es2:
        cnt = [0]
        u = _uid()

        def sb2(name, shape, dt=F32):
            cnt[0] += 1
            return es2.enter_context(nc.sbuf_tensor("sg%d_%s_%d" % (u, name, cnt[0]), shape, dt))
        rK = Res("gconst")
        sel = sb2("sel", [8, H, 128])
        nsel = sb2("nsel", [8, H, 128])
        mB = sb2("mB", [C, H, C])
        mBT = sb2("mBT", [C, H, C])
        mAtt = sb2("mAtt", [C, H, C])
        eye8 = sb2("eye8", [C, H, C])
        lastsel = sb2("lastsel", [C, C])
        cmask = sb2("cmask", [8, TT])

        gk = c.gdn_consts
        for nm, tile_ in (("sel", sel), ("nsel", nsel), ("mB", mB), ("mBT", mBT), ("mAtt", mAtt), ("eye8", eye8)):
            P.dma("sp", P.chan("gk_" + nm), lambda e, nm=nm, tile_=tile_: e.dma_start(out=tile_[:].rearrange("p h c -> p (h c)"), in_=gk[nm]), reads=[c.r_gk], writes=[rK])
        P.dma("sp", P.chan("gk_lastsel"), lambda e: e.dma_start(out=lastsel[:], in_=gk["lastsel"]), reads=[c.r_gk], writes=[rK])
        P.dma("sp", P.chan("gk_cmask"), lambda e: e.dma_start(out=cmask[:], in_=gk["cmask"]), reads=[c.r_gk], writes=[rK])
        cv = sb2("cv", [128, 16, 4])
        gn = sb2("gn", [128, 1])
        pv = sb2("pv", [8, 2])
        nalog = sb2("nalog", [8, 1])
        rcv = Res("cv")
        P.dma("sp", P.chan("gcv"), lambda e: e.dma_start(out=cv[:], in_=cvec_ap), writes=[rcv])
        rgn = Res("gn")
        P.dma("sp", P.chan("ggn"), lambda e: e.dma_start(out=gn[:], in_=gvec_ap), writes=[rgn])
        rpv = Res("pv")
        P.dma("sp", P.chan("gpv"), lambda e: e.dma_start(out=pv[:], in_=pvec_ap), writes=[rpv])
        P.op("act", lambda e: e.activation(out=nalog[:], in_=pv[:, 0:1], func=AF.Exp), reads=[rpv], writes=[rpv])
        P.op("dve", lambda e: e.tensor_scalar(out=nalog[:], in0=nalog[:], scalar1=-1.0, scalar2=None, op0=ALU.mult), reads=[rpv], writes=[rpv])

        hnb = sb2("hnb", [128, KD, TT], BF16); rhnb = Res("hnb"); hch = P.chan("hch0")
        carry = sb2("carry", [128, 16, 3]); rcar = [Res("car%d" % i) for i in range(16)]
        P.op("dve", lambda e: e.memset(carry[:], 0.0), writes=rcar)
        pad = [sb2("pad", [128, TT + 3]) for _ in range(2)]; rpad = [Res("pad0"), Res("pad1")]
        xcv = [sb2("xcv", [128, TT]) for _ in range(2)]; rxcv = [Res("xcv0"), Res("xcv1")]
        sqb = [sb2("sqb", [128, TT]) for _ in range(2)]; rsqb = [Res("sqb0"), Res("sqb1")]
        rsb = [sb2("rsb", [128, TT]) for _ in range(2)]; rrsb = [Res("rsb0"), Res("rsb1")]
        qT = sb2("qT", [128, 4, TT], BF16); rqT = [Res("qT%d" % i) for i in range(4)]
        kT = sb2("kT", [128, 4, TT], BF16); rkT = [Res("kT%d" % i) for i in range(4)]
        vT = sb2("vT", [128, H, TT], BF16); rvT = [Res("vT%d" % i) for i in range(H)]
        o_t = sb2("o_t", [128, H, TT]); ro_t = Res("o_t")
        betaT = sb2("betaT", [8, TT]); rbetaT = Res("betaT")
        lbT = sb2("lbT", [8, TT]); rlbT = Res("lbT")
        spT = sb2("spT", [8, TT]); rspT = Res("spT")
        gcT = sb2("gcT", [8, TT]); rgcT = Res("gcT")
        g2T = sb2("g2T", [8, TT]); rg2T = Res("g2T")
        Sst = sb2("Sst", [128, H, 128]); rS = Res("S")
        P.op("dve", lambda e: e.memset(Sst[:], 0.0), writes=[rS])
        colt = sb2("colt", [C, 16]); rcolt = Res("colt")
        cbg = sb2("cbg", [C, H]); rcbg = Res("cbg")
        ksc = sb2("ksc", [C, H]); rksc = Res("ksc")
        E = sb2("E", [128, H, C]); rE = Res("E")
        qd = sb2("qd", [128, H, C]); rqd = Res("qd")
        tt_ = [sb2("tt", [C, H, C]) for _ in range(3)]; rtt = [Res("tt%d" % i) for i in range(3)]
        Pm = [sb2("Pm", [C, H, C], BF16) for _ in range(2)]; rPm = [Res("Pm0"), Res("Pm1")]
        PTm = [sb2("PTm", [C, H, C], BF16) for _ in range(2)]; rPTm = [Res("PTm0"), Res("PTm1")]
        Um = [sb2("Um", [C, H, C], BF16) for _ in range(2)]; rUm = [Res("Um0"), Res("Um1")]
        attT = sb2("attT", [C, H, C], BF16); rattT = Res("attT")
        kbg = sb2("kbg", [C, H, 128], BF16); rkbg = Res("kbg")
        kst = sb2("kst", [C, H, 128], BF16); rkst = Res("kst")
        vb = sb2("vb", [C, H, 128], BF16); rvb = Res("vb")
        wv = sb2("wv", [C, H, 128]); rwv = Res("wv")
        kcT = sb2("kcT", [128, H, C]); rkcT = Res("kcT")
        vnew = sb2("vnew", [C, H, 128], BF16); rvnew = Res("vnew")
        identb = sb2("identb", [128, 128], BF16)
        P.op("act", lambda e: e.copy(out=identb[:], in_=c.ident[:]), reads=[c.r_const], writes=[rK])
        ytg = sb2("yt", [128, H, TT], BF16); ryt = [Res("yt%d" % i) for i in range(H)]
        ych = [P.chan("ych0"), P.chan("ych1")]
        ps, psr = c.ps, c.psr
        pn = [0]

        def bank():
            i = pn[0] % 8
            pn[0] += 1
            return ps[i], psr[i]

        def inproj(col0, M, TTn):
            slot, rw = wfetch(c, p_w, col0 // 128)
            pb, rpb = bank()

            def mm(e, slot=slot, pb=pb):
                r = None
                for kc in range(KD):
                    r = e.matmul(pb[0:M, 0:TTn], lhsT=slot[:, kc, 0:M], rhs=hnb[:, kc, :], start=(kc == 0), stop=(kc == KD - 1))
                return r
            P.op("pe", mm, reads=[rw, rhnb], writes=[rpb])
            return pb, rpb

        for t in range(NT):
            tsl = slice(t * TT, (t + 1) * TT)
            P.dma("sp", hch, lambda e, t=t: e.dma_start(out=hnb[:], in_=hn_src(t)), reads=list(rhn_d), writes=[rhnb])
            slot, rw = wfetch(c, p_w, 24)
            pbb, rpbb = bank()
            pba, rpba = bank()

            def mmba(e, slot=slot, pbb=pbb, pba=pba):
                r = None
                for kc in range(KD):
                    r = e.matmul(pbb[0:8, 0:TT], lhsT=slot[:, kc, 0:8], rhs=hnb[:, kc, :], start=(kc == 0), stop=(kc == KD - 1))
                for kc in range(KD):
                    r = e.matmul(pba[0:8, 0:TT], lhsT=slot[:, kc, 8:16], rhs=hnb[:, kc, :], start=(kc == 0), stop=(kc == KD - 1))
                return r
            P.op("pe", mmba, reads=[rw, rhnb], writes=[rpbb, rpba])
            P.op("act", lambda e, pbb=pbb: e.activation(out=betaT[:], in_=pbb[0:8, 0:TT], func=AF.Sigmoid), reads=[rpbb], writes=[rbetaT])
            P.op("act", lambda e, pbb=pbb: e.activation(out=lbT[:], in_=pbb[0:8, 0:TT], func=AF.Exp, scale=-1.0), reads=[rpbb], writes=[rlbT])
            P.op("act", lambda e: e.activation(out=lbT[:], in_=lbT[:], func=AF.Ln, bias=1.0), reads=[rlbT], writes=[rlbT])
            P.op("act", lambda e, pba=pba: e.activation(out=spT[:], in_=pba[0:8, 0:TT], func=AF.Exp, bias=pv[:, 1:2]), reads=[rpba, rpv], writes=[rspT])
            P.op("act", lambda e: e.activation(out=spT[:], in_=spT[:], func=AF.Ln, bias=1.0), reads=[rspT], writes=[rspT])
            P.op("dve", lambda e: e.tensor_scalar(out=spT[:], in0=spT[:], scalar1=nalog[:, 0:1], scalar2=None, op0=ALU.mult), reads=[rspT, rpv], writes=[rspT])
            P.op("dve", lambda e: e.tensor_tensor_scan(out=gcT[:], data0=cmask[:], data1=spT[:], initial=0.0, op0=ALU.mult, op1=ALU.add),
                 reads=[rspT, rK], writes=[rgcT])
            P.op("dve", lambda e: e.tensor_tensor(out=g2T[:], in0=gcT[:], in1=lbT[:], op=ALU.subtract), reads=[rgcT, rlbT], writes=[rg2T])
            for f in range(16):
                pb, rpb = inproj(f * 128, 128, TT)
                pd, rpd = pad[f % 2], rpad[f % 2]
                xc_, rxc_ = xcv[f % 2], rxcv[f % 2]
                P.op("act", lambda e, pb=pb, pd=pd: e.copy(out=pd[:, 3:3 + TT], in_=pb[:, 0:TT]), reads=[rpb], writes=[rpd])
                P.op("dve", lambda e, pd=pd, f=f: e.tensor_copy(out=pd[:, 0:3], in_=carry[:, f, :]), reads=[rcar[f], rpd], writes=[rpd])
                P.op("dve", lambda e, pd=pd, xc_=xc_, f=f: e.tensor_scalar(out=xc_[:], in0=pd[:, 0:TT], scalar1=cv[:, f, 0:1], scalar2=None, op0=ALU.mult),
                     reads=[rpd, rcv], writes=[rxc_])
                for k in range(1, 4):
                    P.op("dve", lambda e, pd=pd, xc_=xc_, f=f, k=k: e.scalar_tensor_tensor(out=xc_[:], in0=pd[:, k:k + TT], scalar=cv[:, f, k:k + 1], in1=xc_[:],
                                                                                           op0=ALU.mult, op1=ALU.add), reads=[rpd, rxc_, rcv], writes=[rxc_])
                P.op("dve", lambda e, pd=pd, f=f: e.tensor_copy(out=carry[:, f, :], in_=pd[:, TT:TT + 3]), reads=[rpd], writes=[rcar[f]])
                if f >= 8:
                    hv = f - 8
                    P.op("act", lambda e, xc_=xc_, hv=hv: e.activation(out=vT[:, hv, :], in_=xc_[:], func=AF.Silu), reads=[rxc_], writes=[rvT[hv]])
                else:
                    dstT, rdst, hq = (qT, rqT, f) if f < 4 else (kT, rkT, f - 4)
                    qscale = float(128 ** -0.5) if f < 4 else 1.0
                    P.op("act", lambda e, xc_=xc_: e.activation(out=xc_[:], in_=xc_[:], func=AF.Silu), reads=[rxc_], writes=[rxc_])
                    sq_, rsq_ = sqb[f % 2], rsqb[f % 2]
                    rs_, rrs_ = rsb[f % 2], rrsb[f % 2]
                    P.op("act", lambda e, xc_=xc_, sq_=sq_: e.activation(out=sq_[:], in_=xc_[:], func=AF.Square), reads=[rxc_], writes=[rsq_])
                    pn_, rpn_ = bank()
                    P.op("pe", lambda e, sq_=sq_, pn_=pn_: e.matmul(pn_[:, 0:TT], lhsT=c.ones_f[:], rhs=sq_[:], start=True, stop=True), reads=[rsq_, c.r_const], writes=[rpn_])
                    P.op("act", lambda e, rs_=rs_, pn_=pn_: e.activation(out=rs_[:], in_=pn_[:, 0:TT], func=AF.Sqrt, bias=EPS), reads=[rpn_], writes=[rrs_])
                    P.op("dve", lambda e, rs_=rs_: e.reciprocal(out=rs_[:], in_=rs_[:]), reads=[rrs_], writes=[rrs_])
                    P.op("dve", lambda e, xc_=xc_, rs_=rs_, dstT=dstT, hq=hq, qscale=qscale: e.scalar_tensor_tensor(
                        out=dstT[:, hq, :], in0=xc_[:], scalar=qscale, in1=rs_[:], op0=ALU.mult, op1=ALU.mult), reads=[rxc_, rrs_], writes=[rdst[hq]])
            for ci in range(NCH):
                cs = slice(ci * C, (ci + 1) * C)
                pc, rpc = bank()

                def trc(e, pc=pc, cs=cs):
                    e.transpose(pc[0:C, 0:8], gcT[0:8, cs], c.ident[0:8, 0:8])
                    return e.transpose(pc[0:C, 8:16], betaT[0:8, cs], c.ident[0:8, 0:8])
                P.op("pe", trc, reads=[rgcT, rbetaT, c.r_const], writes=[rpc])
                P.op("act", lambda e, pc=pc: e.copy(out=colt[:], in_=pc[0:C, 0:16]), reads=[rpc], writes=[rcolt])
                pl, rpl = bank()
                P.op("pe", lambda e, pl=pl: e.matmul(pl[0:C, 0:8], lhsT=lastsel[:], rhs=colt[:, 0:8], start=True, stop=True), reads=[rcolt, rK], writes=[rpl])
                P.op("act", lambda e: e.activation(out=cbg[:], in_=colt[:, 0:8], func=AF.Exp), reads=[rcolt], writes=[rcbg])
                P.op("dve", lambda e: e.tensor_tensor(out=cbg[:], in0=cbg[:], in1=colt[:, 8:16], op=ALU.mult), reads=[rcbg, rcolt], writes=[rcbg])
                P.op("dve", lambda e, pl=pl: e.tensor_tensor(out=ksc[:], in0=pl[0:C, 0:8], in1=colt[:, 0:8], op=ALU.subtract), reads=[rpl, rcolt], writes=[rksc])
                P.op("act", lambda e: e.activation(out=ksc[:], in_=ksc[:], func=AF.Exp), reads=[rksc], writes=[rksc])
                pe_, rpe_ = bank()

                def mmE(e, pe_=pe_, cs=cs):
                    r = None
                    for h in range(H):
                        r = e.matmul(pe_[:, h * C:(h + 1) * C], lhsT=sel[:, h, :], rhs=gcT[0:8, cs], start=True, stop=True)
                    return r
                P.op("pe", mmE, reads=[rgcT, rK], writes=[rpe_])
                P.op("act", lambda e, pe_=pe_: e.activation(out=E[:].rearrange("p h c -> p (h c)"), in_=pe_[:, 0:H * C], func=AF.Exp), reads=[rpe_], writes=[rE])
                for rep in range(2):
                    P.op("dve", lambda e, rep=rep, cs=cs: e.tensor_tensor(
                        out=qd[:].rearrange("p (q r) c -> p q r c", r=2)[:, :, rep, :], in0=qT[:, :, cs],
                        in1=E[:].rearrange("p (q r) c -> p q r c", r=2)[:, :, rep, :], op=ALU.mult), reads=rqT + [rE], writes=[rqd])
                pkk, rpkk = bank()
                pqk, rpqk = bank()

                def mmkk(e, pkk=pkk, pqk=pqk, cs=cs):
                    r = None
                    for h in range(H):
                        r = e.matmul(pkk[0:C, h * C:(h + 1) * C], lhsT=kT[:, h // 2, cs], rhs=kT[:, h // 2, cs], start=True, stop=True)
                    for h in range(H):
                        r = e.matmul(pqk[0:C, h * C:(h + 1) * C], lhsT=kT[:, h // 2, cs], rhs=qT[:, h // 2, cs], start=True, stop=True)
                    return r
                P.op("pe", mmkk, reads=rkT + rqT, writes=[rpkk, rpqk])
                pd1, rpd1 = bank()
                pd2, rpd2 = bank()
                pd3, rpd3 = bank()

                def mmd(e, pd1=pd1, pd2=pd2, pd3=pd3, cs=cs):
                    r = None
                    for h in range(H):
                        hs = slice(h * C, (h + 1) * C)
                        e.matmul(pd1[0:C, hs], lhsT=g2T[0:8, cs], rhs=sel[:, h, 0:C], start=True, stop=False)
                        e.matmul(pd1[0:C, hs], lhsT=nsel[:, h, 0:C], rhs=gcT[0:8, cs], start=False, stop=True)
                        e.matmul(pd2[0:C, hs], lhsT=sel[:, h, 0:C], rhs=g2T[0:8, cs], start=True, stop=False)
                        e.matmul(pd2[0:C, hs], lhsT=gcT[0:8, cs], rhs=nsel[:, h, 0:C], start=False, stop=True)
                        e.matmul(pd3[0:C, hs], lhsT=sel[:, h, 0:C], rhs=gcT[0:8, cs], start=True, stop=False)
                        r = e.matmul(pd3[0:C, hs], lhsT=gcT[0:8, cs], rhs=nsel[:, h, 0:C], start=False, stop=True)
                    return r
                P.op("pe", mmd, reads=[rg2T, rgcT, rK], writes=[rpd1, rpd2, rpd3])
                p0, rp0 = Pm[0], rPm[0]
                pt0, rpt0 = PTm[0], rPTm[0]
                fl = lambda a: a[:].rearrange("p h c -> p (h c)")
                for (pdx, rpdx, tmp_, rtmp_, msk, pmat, rpmat, dst, rdst_) in (
                        (pd1, rpd1, tt_[0], rtt[0], mB, pkk, rpkk, p0, rp0),
                        (pd2, rpd2, tt_[1], rtt[1], mBT, pkk, rpkk, pt0, rpt0),
                        (pd3, rpd3, tt_[2], rtt[2], mAtt, pqk, rpqk, attT, rattT)):
                    P.op("dve", lambda e, pdx=pdx, tmp_=tmp_: e.tensor_scalar(out=fl(tmp_), in0=pdx[0:C, 0:H * C], scalar1=0.0, scalar2=None, op0=ALU.min),
                         reads=[rpdx], writes=[rtmp_])
                    P.op("act", lambda e, tmp_=tmp_: e.activation(out=fl(tmp_), in_=fl(tmp_), func=AF.Exp), reads=[rtmp_], writes=[rtmp_])
                    P.op("dve", lambda e, tmp_=tmp_, msk=msk: e.tensor_tensor(out=fl(tmp_), in0=fl(tmp_), in1=fl(msk), op=ALU.mult), reads=[rtmp_, rK], writes=[rtmp_])
                    P.op("dve", lambda e, tmp_=tmp_, pmat=pmat, dst=dst: e.tensor_tensor(out=fl(dst), in0=pmat[0:C, 0:H * C], in1=fl(tmp_), op=ALU.mult),
                         reads=[rpmat, rtmp_], writes=[rdst_])
                P.op("dve", lambda e: e.tensor_tensor(out=fl(Um[0]), in0=fl(pt0), in1=fl(eye8), op=ALU.add), reads=[rpt0, rK], writes=[rUm[0]])
                cur = 0
                for r in range(0, 6):
                    nxt = 1 - cur
                    Pc, rPc, PTc, rPTc, Uc, rUc = Pm[cur], rPm[cur], PTm[cur], rPTm[cur], Um[cur], rUm[cur]
                    Pn, rPn, PTn, rPTn, Un, rUn = Pm[nxt], rPm[nxt], PTm[nxt], rPTm[nxt], Um[nxt], rUm[nxt]
                    if r >= 1:
                        pu, rpu = bank()

                        def mmu(e, pu=pu, Pc=Pc, Uc=Uc):
                            rr = None
                            for h in range(H):
                                rr = e.matmul(pu[0:C, h * C:(h + 1) * C], lhsT=Pc[:, h, :], rhs=Uc[:, h, :], start=True, stop=True)
                            return rr
                        P.op("pe", mmu, reads=[rPc, rUc], writes=[rpu])
                    if r < 5:
                        pp, rpp = bank()

                        def mmp(e, pp=pp, Pc=Pc, PTc=PTc):
                            rr = None
                            for h in range(H):
                                rr = e.matmul(pp[0:C, h * C:(h + 1) * C], lhsT=PTc[:, h, :], rhs=Pc[:, h, :], start=True, stop=True)
                            return rr
                        P.op("pe", mmp, reads=[rPc, rPTc], writes=[rpp])
                        if r < 4:
                            ppt, rppt = bank()

                            def mmpt(e, ppt=ppt, Pc=Pc, PTc=PTc):
                                rr = None
                                for h in range(H):
                                    rr = e.matmul(ppt[0:C, h * C:(h + 1) * C], lhsT=Pc[:, h, :], rhs=PTc[:, h, :], start=True, stop=True)
                                return rr
                            P.op("pe", mmpt, reads=[rPc, rPTc], writes=[rppt])
                    if r >= 1:
                        P.op("dve", lambda e, pu=pu, Uc=Uc, Un=Un: e.tensor_tensor(out=fl(Un), in0=pu[0:C, 0:H * C], in1=fl(Uc), op=ALU.add),
                             reads=[rpu, rUc], writes=[rUn])
                    else:
                        P.op("act", lambda e, Uc=Uc, Un=Un: e.copy(out=fl(Un), in_=fl(Uc)), reads=[rUc], writes=[rUn])
                    if r < 5:
                        P.op("act", lambda e, pp=pp, Pn=Pn: e.copy(out=fl(Pn), in_=pp[0:C, 0:H * C]), reads=[rpp], writes=[rPn])
                        if r < 4:
                            P.op("act", lambda e, ppt=ppt, PTn=PTn: e.copy(out=fl(PTn), in_=ppt[0:C, 0:H * C]), reads=[rppt], writes=[rPTn])
                    cur = nxt
                U, rU = Um[cur], rUm[cur]
                pkt, rpkt = bank()

                def trk(e, pkt=pkt, cs=cs):
                    rr = None
                    for hq in range(4):
                        rr = e.matmul(pkt[0:C, hq * 128:(hq + 1) * 128], lhsT=kT[:, hq, cs], rhs=identb[:], start=True, stop=True)
                    return rr
                P.op("pe", trk, reads=rkT + [rK], writes=[rpkt])
                for rep in range(2):
                    kv_ = lambda a: a[:].rearrange("p (q r) d -> p q r d", r=2)[:, :, rep, :]
                    P.op("dve", lambda e, pkt=pkt, rep=rep: e.tensor_tensor(
                        out=kbg[:].rearrange("p (q r) d -> p q r d", r=2)[:, :, rep, :], in0=pkt[0:C, 0:512].rearrange("p (q d) -> p q d", d=128),
                        in1=cbg[:].rearrange("p (q r) -> p q r", r=2)[:, :, rep].unsqueeze(2).to_broadcast([C, 4, 128]), op=ALU.mult),
                        reads=[rpkt, rcbg], writes=[rkbg])
                    P.op("dve", lambda e, pkt=pkt, rep=rep: e.tensor_tensor(
                        out=kst[:].rearrange("p (q r) d -> p q r d", r=2)[:, :, rep, :], in0=pkt[0:C, 0:512].rearrange("p (q d) -> p q d", d=128),
                        in1=ksc[:].rearrange("p (q r) -> p q r", r=2)[:, :, rep].unsqueeze(2).to_broadcast([C, 4, 128]), op=ALU.mult),
                        reads=[rpkt, rksc], writes=[rkst])
                for half in range(2):
                    pvt, rpvt = bank()

                    def trv(e, pvt=pvt, cs=cs, half=half):
                        rr = None
                        for hh in range(4):
                            rr = e.matmul(pvt[0:C, hh * 128:(hh + 1) * 128], lhsT=vT[:, half * 4 + hh, cs], rhs=identb[:], start=True, stop=True)
                        return rr
                    P.op("pe", trv, reads=rvT + [rK], writes=[rpvt])
                    P.op("dve", lambda e, pvt=pvt, half=half: e.tensor_tensor(
                        out=vb[:, half * 4:(half + 1) * 4, :], in0=pvt[0:C, 0:512].rearrange("p (q d) -> p q d", d=128),
                        in1=colt[:, 8 + half * 4:8 + (half + 1) * 4].unsqueeze(2).to_broadcast([C, 4, 128]), op=ALU.mult),
                        reads=[rpvt, rcolt], writes=[rvb])
                for half in range(2):
                    pw, rpw = bank()

                    def mmw(e, pw=pw, half=half, U=U):
                        rr = None
                        for hh in range(4):
                            h = half * 4 + hh
                            rr = e.matmul(pw[0:C, hh * 128:(hh + 1) * 128], lhsT=U[:, h, :], rhs=vb[:, h, :], start=True, stop=True)
                        return rr
                    P.op("pe", mmw, reads=[rU, rvb], writes=[rpw])
                    P.op("act", lambda e, pw=pw, half=half: e.copy(out=wv[:, half * 4:(half + 1) * 4, :].rearrange("p q d -> p (q d)"), in_=pw[0:C, 0:512]),
                         reads=[rpw], writes=[rwv])
                pkc, rpkc = bank()

                def mmkc(e, pkc=pkc, U=U):
                    rr = None
                    for h in range(H):
                        rr = e.matmul(pkc[:, h * C:(h + 1) * C], lhsT=kbg[:, h, :], rhs=U[:, h, :], start=True, stop=True)
                    return rr
                P.op("pe", mmkc, reads=[rU, rkbg], writes=[rpkc])
                P.op("act", lambda e, pkc=pkc: e.copy(out=fl(kcT), in_=pkc[:, 0:H * C]), reads=[rpkc], writes=[rkcT])
                for half in range(2):
                    pv_, rpv_ = bank()

                    def mmv(e, pv_=pv_, half=half):
                        rr = None
                        for hh in range(4):
                            h = half * 4 + hh
                            rr = e.matmul(pv_[0:C, hh * 128:(hh + 1) * 128], lhsT=kcT[:, h, :], rhs=Sst[:, h, :], start=True, stop=True)
                        return rr
                    P.op("pe", mmv, reads=[rkcT, rS], writes=[rpv_])
                    P.op("dve", lambda e, pv_=pv_, half=half: e.tensor_tensor(
                        out=vnew[:, half * 4:(half + 1) * 4, :].rearrange("p q d -> p (q d)"),
                        in0=wv[:, half * 4:(half + 1) * 4, :].rearrange("p q d -> p (q d)"), in1=pv_[0:C, 0:512], op=ALU.subtract),
                        reads=[rpv_, rwv], writes=[rvnew])
                po, rpo = bank()

                def mmo(e, po=po):
                    rr = None
                    for h in range(H):
                        e.matmul(po[:, h * C:(h + 1) * C], lhsT=Sst[:, h, :], rhs=qd[:, h, :], start=True, stop=False)
                        rr = e.matmul(po[:, h * C:(h + 1) * C], lhsT=vnew[:, h, :], rhs=attT[:, h, :], start=False, stop=True)
                    return rr
                P.op("pe", mmo, reads=[rS, rqd, rvnew, rattT], writes=[rpo])
                P.op("act", lambda e, po=po, cs=cs: e.copy(out=o_t[:, :, cs], in_=po[:, 0:H * C].rearrange("p (h c) -> p h c", c=C)), reads=[rpo], writes=[ro_t])
                for half in range(2):
                    pss, rpss = bank()

                    def mms(e, pss=pss, half=half):
                        rr = None
                        for hh in range(4):
                            h = half * 4 + hh
                            rr = e.matmul(pss[:, hh * 128:(hh + 1) * 128], lhsT=kst[:, h, :], rhs=vnew[:, h, :], start=True, stop=True)
                        return rr
                    P.op("pe", mms, reads=[rkst, rvnew], writes=[rpss])
                    for hh in range(4):
                        h = half * 4 + hh
                        P.op("dve", lambda e, pss=pss, h=h, hh=hh: e.scalar_tensor_tensor(
                            out=Sst[:, h, :], in0=Sst[:, h, :], scalar=E[:, h, C - 1:C], in1=pss[:, hh * 128:(hh + 1) * 128], op0=ALU.mult, op1=ALU.add),
                            reads=[rS, rE, rpss], writes=[rS])
                if c.progress is not None and ci < NCH - 1:
                    c.progress([rS])
            for h in range(H):
                pz, rpz = inproj(2048 + h * 128, 128, TT)
                zs, rzs = xcv[h % 2], rxcv[h % 2]
                P.op("act", lambda e, pz=pz, zs=zs: e.activation(out=zs[:], in_=pz[:, 0:TT], func=AF.Silu), reads=[rpz], writes=[rzs])
                sq_, rsq_ = sqb[h % 2], rsqb[h % 2]
                rs_, rrs_ = rsb[h % 2], rrsb[h % 2]
                P.op("act", lambda e, sq_=sq_, h=h: e.activation(out=sq_[:], in_=o_t[:, h, :], func=AF.Square), reads=[ro_t], writes=[rsq_])
                pn_, rpn_ = bank()
                P.op("pe", lambda e, sq_=sq_, pn_=pn_: e.matmul(pn_[:, 0:TT], lhsT=c.ones_f[:], rhs=sq_[:], start=True, stop=True), reads=[rsq_, c.r_const], writes=[rpn_])
                P.op("act", lambda e, rs_=rs_, pn_=pn_: e.activation(out=rs_[:], in_=pn_[:, 0:TT], func=AF.Sqrt, bias=EPS, scale=1.0 / 128), reads=[rpn_], writes=[rrs_])
                P.op("dve", lambda e, rs_=rs_: e.reciprocal(out=rs_[:], in_=rs_[:]), reads=[rrs_], writes=[rrs_])
                P.op("dve", lambda e, rs_=rs_, h=h: e.scalar_tensor_tensor(out=rs_[:], in0=o_t[:, h, :], scalar=gn[:, 0:1], in1=rs_[:], op0=ALU.mult, op1=ALU.mult),
                     reads=[ro_t, rgn, rrs_], writes=[rrs_])
                P.op("dve", lambda e, rs_=rs_, zs=zs, h=h: e.tensor_tensor(out=ytg[:, h, :], in0=rs_[:], in1=zs[:], op=ALU.mult), reads=[rrs_, rzs], writes=[ryt[h]])
                P.dma("sp", ych[h % 2], lambda e, h=h, t=t: e.dma_start(out=y_dst(t, h * 128, (h + 1) * 128), in_=ytg[:, h, :]), reads=[ryt[h]],
                      writes=list(ry_d(t)) if callable(ry_d) else [])
            if tile_done is not None:
                tile_done(t, ryt[H - 1])
        P.barrier()


def build_gdn_consts(c, TT):
    P, nc = c.P, c.nc
    C, H = 64, 8
    gk = {}
    shapes = dict(sel=[8, H * 128], nsel=[8, H * 128], mB=[C, H * C], mBT=[C, H * C], mAtt=[C, H * C], eye8=[C, H * C], lastsel=[C, C], cmask=[8, TT])
    for nm, sh in shapes.items():
        gk[nm] = nc.dram_tensor("gk_%s" % nm, sh, F32).ap()
    c.gdn_consts = gk
    c.r_gk = Res("gk")
    with ExitStack() as es2:
        u = _uid()
        sb2 = lambda name, shape: es2.enter_context(nc.sbuf_tensor("sgk%d_%s" % (u, name), shape, F32))
        rK = Res("gkbuild")
        sel = sb2("sel", [8, H, 128]); nsel = sb2("nsel", [8, H, 128])
        mB = sb2("mB", [C, H, C]); mBT = sb2("mBT", [C, H, C]); mAtt = sb2("mAtt", [C, H, C]); eye8 = sb2("eye8", [C, H, C])
        lastsel = sb2("lastsel", [C, C]); cmask = sb2("cmask", [8, TT])

        def cst(fn):
            P.op("pool", fn, reads=[rK], writes=[rK])
        cst(lambda e: e.memset(sel[:], 0.0))
        cst(lambda e: e.affine_select(out=sel[:], in_=sel[:], pattern=[[-1, H], [0, 128]], compare_op=ALU.not_equal, fill=1.0, base=0, channel_multiplier=1))
        cst(lambda e: e.memset(nsel[:], 0.0))
        cst(lambda e: e.affine_select(out=nsel[:], in_=nsel[:], pattern=[[-1, H], [0, 128]], compare_op=ALU.not_equal, fill=-1.0, base=0, channel_multiplier=1))
        cst(lambda e: e.memset(mB[:], -1.0))
        cst(lambda e: e.affine_select(out=mB[:], in_=mB[:], pattern=[[0, H], [-1, C]], compare_op=ALU.is_gt, fill=0.0, base=0, channel_multiplier=1))
        cst(lambda e: e.memset(mBT[:], -1.0))
        cst(lambda e: e.affine_select(out=mBT[:], in_=mBT[:], pattern=[[0, H], [1, C]], compare_op=ALU.is_gt, fill=0.0, base=0, channel_multiplier=-1))
        cst(lambda e: e.memset(mAtt[:], 1.0))
        cst(lambda e: e.affine_select(out=mAtt[:], in_=mAtt[:], pattern=[[0, H], [1, C]], compare_op=ALU.is_ge, fill=0.0, base=0, channel_multiplier=-1))
        cst(lambda e: e.memset(eye8[:], 0.0))
        cst(lambda e: e.affine_select(out=eye8[:], in_=eye8[:], pattern=[[0, H], [-1, C]], compare_op=ALU.not_equal, fill=1.0, base=0, channel_multiplier=1))
        cst(lambda e: e.memset(lastsel[:], 0.0))
        cst(lambda e: e.affine_select(out=lastsel[:], in_=lastsel[:], pattern=[[0, C]], compare_op=ALU.not_equal, fill=1.0, base=-(C - 1), channel_multiplier=1))
        cst(lambda e: e.memset(cmask[:], 1.0))
        cst(lambda e: e.memset(cmask[:].rearrange("p (n c) -> p n c", c=C)[:, :, 0:1], 0.0))
        ch = P.chan("gkst")
        for nm, t in (("sel", sel), ("nsel", nsel), ("mB", mB), ("mBT", mBT), ("mAtt", mAtt), ("eye8", eye8)):
            P.dma("sp", ch, lambda e, nm=nm, t=t: e.dma_start(out=gk[nm], in_=t[:].rearrange("p h c -> p (h c)")), reads=[rK], writes=[c.r_gk])
        P.dma("sp", ch, lambda e: e.dma_start(out=gk["lastsel"], in_=lastsel[:]), reads=[rK], writes=[c.r_gk])
        P.dma("sp", ch, lambda e: e.dma_start(out=gk["cmask"], in_=cmask[:]), reads=[rK], writes=[c.r_gk])
        P.barrier()


def build_gdn_program(S, TT):
    nc = bass.Bass("TRN2", target_bir_lowering=False)
    dt = lambda name, shape, dtype, kind_: nc.dram_tensor(name, shape, dtype, kind=kind_).ap()
    hn = dt("hn", [D, S], BF16, "ExternalInput")
    w = dt("w", [D, 3088], F32, "ExternalInput")
    cvec = dt("cvec", [128, 16, 4], F32, "ExternalInput")
    gvec = dt("gvec", [128, 1], F32, "ExternalInput")
    pvec = dt("pvec", [8, 2], F32, "ExternalInput")
    y = dt("y", [1024, S], BF16, "ExternalOutput")
    with ExitStack() as es:
        P = Prog(nc, es)
        c = make_ctx(nc, es, P, nslots=3)
        build_gdn_consts(c, TT)
        emit_gdn(c, S, TT, lambda t: hn[:, t * TT:(t + 1) * TT].rearrange("(kc p) t -> p kc t", p=128), prep_gdn_weights(c, w), cvec, gvec, pvec,
                 lambda t, r0, r1: y[r0:r1, t * TT:(t + 1) * TT])
        P.emit()
    return nc


def build_fused_program(S=4096, depth=4, NB=2, debug=False):
    GROUPS = [[4 * b + q for q in range(4)] for b in range(NB)]
    Sc = S // 4
    T = 512
    TT = 256
    NQ = Sc // 256
    NYK = S // 512
    nc = bass.Bass("TRN2", target_bir_lowering=False)
    dt = lambda name, shape, dtype, kind_: nc.dram_tensor(name, shape, dtype, kind=kind_).ap()
    xin = dt("x", [D, Sc], F32, "ExternalInput")
    mem = dt("mem", [MEM, D], F32, "ExternalInput")
    vecs = dt("vecs", [128, 64 * depth], F32, "ExternalInput")
    outd = dt("out", [D, Sc], F32, "ExternalOutput")
    W = []
    for i in range(depth):
        kind = "lru" if i % 2 == 0 else "gdn"
        ychan = 704 if kind == "lru" else 1024
        d = dict(kind=kind, ychan=ychan)
        d["w_xq"] = dt("w_xq%d" % i, [D, D], F32, "ExternalInput")
        d["w_kv"] = dt("w_kv%d" % i, [D, 2 * D], F32, "ExternalInput")
        d["w_out"] = dt("w_out%d" % i, [4 * ychan + D, D], F32, "ExternalInput")
        d["w_fi"] = dt("w_fi%d" % i, [D, 2 * D_FF], F32, "ExternalInput")
        d["w_fo"] = dt("w_fo%d" % i, [D_FF, D], F32, "ExternalInput")
        if kind == "lru":
            d["wx"] = dt("wx%d" % i, [D, 704], F32, "ExternalInput")
            d["wg"] = dt("wg%d" % i, [D, 704], F32, "ExternalInput")
            d["ga"] = dt("ga%d" % i, [4, 176, 176], F32, "ExternalInput")
            d["gx"] = dt("gx%d" % i, [4, 176, 176], F32, "ExternalInput")
            d["lvec"] = dt("lvec%d" % i, [88, 8, 8], F32, "ExternalInput")
        else:
            d["gw"] = dt("gw%d" % i, [D, 3088], F32, "ExternalInput")
            d["cvec"] = dt("cvec%d" % i, [128, 16, 4], F32, "ExternalInput")
            d["gvec"] = dt("gvec%d" % i, [128, 1], F32, "ExternalInput")
            d["pvec"] = dt("pvec%d" % i, [8, 2], F32, "ExternalInput")
        W.append(d)
    hn_loc = nc.dram_tensor("hn_loc", [NQ, D, 256], BF16).ap()
    hn_all = nc.dram_tensor("hn_all", [NQ, 4 * D, 256], BF16).ap()
    r_hn_loc = [Res("hn_loc%d" % q) for q in range(NQ)]
    r_hn_all = [Res("hn_all%d" % q) for q in range(NQ)]
    ybuf = {}
    for kind, ychan in (("lru", 704), ("gdn", 1024)):
        ybuf[kind] = (nc.dram_tensor("y_loc_" + kind, [NYK, ychan, 512], BF16).ap(),
                      nc.dram_tensor("y_all_" + kind, [NYK, 4 * ychan, 512], BF16).ap(),
                      [Res("yl%d" % k) for k in range(NYK)], [Res("ya%d" % k) for k in range(NYK)])
    if debug:
        dbg_hn = dt("dbg_hn", [NQ, 4 * D, 256], BF16, "ExternalOutput")
        dbg_y = dt("dbg_y", [NYK, 4 * 704, 512], BF16, "ExternalOutput")
        dbg_yl = dt("dbg_yl", [NYK, 704, 512], BF16, "ExternalOutput")
    with ExitStack() as es:
        P = Prog(nc, es)
        c = make_ctx(nc, es, P, nslots=4)
        c.vecs = c.sb("vecs", [128, 64 * depth], F32)
        c.r_vecs = Res("vecs")
        c.h = c.sb("h", [128, KD, Sc], F32)
        c.rh = Res("h")
        c.ych = P.chan()
        c.och = P.chan()
        P.dma("sp", P.chan(), lambda e: e.dma_start(out=c.vecs[:], in_=vecs), writes=[c.r_vecs])
        P.dma("sp", P.chan(), lambda e: e.dma_start(out=c.h[:], in_=xin.rearrange("(kc p) t -> p kc t", p=128)), writes=[c.rh])

        hn_ch = [P.chan() for _ in range(NQ)]
        pid_cache = {}
        build_gdn_consts(c, TT)

        def hn_exchange(tsl):
            hn_ = c.hn
            for s0 in range(0, T, 256):
                q = (tsl.start + s0) // 256
                P.dma("sp", hn_ch[q], lambda e, q=q, s0=s0, hn_=hn_: e.dma_start(out=hn_loc[q].rearrange("(kc p) t -> p kc t", p=128),
                                                                              in_=hn_[:, :, s0:s0 + 256]),
                      reads=[c.rhn], writes=[r_hn_loc[q]])
                P.coll(P.cc_chan("hn%d" % q), lambda e, q=q: e.collective_compute("AllGather", ALU.bypass, replica_groups=GROUPS,
                                                                               ins=[hn_loc[q].opt()], outs=[hn_all[q].opt()]),
                       reads=[r_hn_loc[q]], writes=[r_hn_all[q]])

        with ExitStack() as es2:
            _pass_bufs(c, es2, T, "a")
            for p0 in range(0, Sc, T):
                emit_tok_pass(c, dict(g_next=0, nyc=0, rpc_y=128), slice(p0, p0 + T), T, hn_dst=hn_exchange, out_dst=None, do_main=False)
            P.barrier()
        preps = {}
        gpreps = {}
        for i in range(depth):
            d = W[i]
            kind, ychan = d["kind"], d["ychan"]
            final = (i == depth - 1)
            y_loc, y_all, r_yl, r_ya = ybuf[kind]
            tasks = []
            if kind == "gdn" and i not in gpreps:
                gpreps[i] = prep_gdn_weights(c, d["gw"])
            preps[i] = prep_layer_weights(c, d["w_xq"], d["w_kv"], d["w_out"], d["w_fi"], d["w_fo"], 32, 88 if kind == "lru" else 128, tasks)
            if i + 1 < depth and W[i + 1]["kind"] == "gdn":
                gpreps[i + 1] = prep_gdn_weights(c, W[i + 1]["gw"], tasks)
            NTB = S // TT
            ncalls = [NTB * 8 if kind == "lru" else NTB * (TT // 64 - 1)]

            def progress(gate, tasks=tasks, ncalls=ncalls):
                k = -(-len(tasks) // max(1, ncalls[0]))
                ncalls[0] -= 1
                for _ in range(k):
                    if tasks:
                        pr_, bi_ = tasks.pop(0)
                        prep_issue(c, pr_, bi_, gate)
            c.progress = progress

            def hn_src(t):
                r, q = t // NQ, t % NQ
                return hn_all[q][r * D:(r + 1) * D, :].rearrange("(kc p) t -> p kc t", p=128)

            def y_dst(t, r0, r1, y_loc=y_loc):
                return y_loc[t // 2][r0:r1, (t % 2) * 256:(t % 2 + 1) * 256]

            ystores = [[] for _ in range(NYK)]

            def tile_done(t, gate, y_loc=y_loc, y_all=y_all, ystores=ystores, r_ya=r_ya):
                if t % 2 == 1:
                    k = t // 2
                    r_yl = ystores
                    P.coll(P.cc_chan("y%d" % k), lambda e, k=k: e.collective_compute("AllGather", ALU.bypass, replica_groups=GROUPS,
                                                                                  ins=[y_loc[k].opt()], outs=[y_all[k].opt()]),
                           reads=r_yl[k], writes=[r_ya[k]])

            def ry_d(t, ystores=ystores):
                r = Res("ys")
                ystores[t // 2].append(r)
                return [r]
            if kind == "lru":
                emit_lru(c, S, TT, hn_src, d["wx"], d["wg"], d["ga"], d["gx"], d["lvec"], y_dst, tile_done, rhn_d=r_hn_all, ry_d=ry_d)
            else:
                emit_gdn(c, S, TT, hn_src, gpreps[i], d["cvec"], d["gvec"], d["pvec"], y_dst, tile_done, rhn_d=r_hn_all, ry_d=ry_d)
            if debug and i == 0:
                dch = P.chan()
                P.dma("sp", dch, lambda e: e.dma_start(out=dbg_hn, in_=hn_all), reads=r_hn_all)
                P.dma("sp", dch, lambda e: e.dma_start(out=dbg_y, in_=y_all), reads=r_ya)
                P.dma("sp", dch, lambda e: e.dma_start(out=dbg_yl, in_=y_loc), reads=r_yl)
            c.progress = None
            while tasks:
                pr_, bi_ = tasks.pop(0)
                prep_issue(c, pr_, bi_)
            with ExitStack() as es3:
                c.KT = es3.enter_context(nc.sbuf_tensor("sb_KT%d" % i, [128, KD, MEM], BF16)); c.rKT = Res("KT")
                c.V = es3.enter_context(nc.sbuf_tensor("sb_V%d" % i, [128, 2, D], BF16)); c.rV = Res("V")
                rpc_y = 88 if kind == "lru" else 128
                lay = dict(g_mix=64 * i, g_mem=64 * i + 16, g_ffn=64 * i + 32, g_next=64 * i + 48, nyc=32, rpc_y=rpc_y, y_res=r_ya)
                lay.update(preps[i])

                def y_ap(tsl, e, y_all=y_all, rpc_y=rpc_y):
                    if "cidx" not in pid_cache:
                        pid_cache["cidx"] = e.partition_id() % 4
                    cidx = pid_cache["cidx"]
                    ya4 = y_all.rearrange("(c q) r t -> c q r t", c=4)
                    return ya4[bass.ds(cidx, 1), tsl.start // 512].rearrange("o (kc p) t -> p (o kc) t", p=rpc_y)
                lay["y_ap"] = y_ap
                emit_kv(c, lay, mem, lay["p_kv"], lay["g_mem"])
                with ExitStack() as es2:
                    _pass_bufs(c, es2, T, "c%d" % i)
                    for p0 in range(0, Sc, T):
                        tsl = slice(p0, p0 + T)
                        emit_tok_pass(c, lay, tsl, T, hn_dst=None if final else hn_exchange,
                                      out_dst=(lambda tsl: outd[:, tsl].rearrange("(kc p) t -> p kc t", p=128)) if final else None)
                    P.barrier()
        P.emit()
    return nc


_PROGS = {}


def _prog(key, fn):
    if key not in _PROGS:
        _PROGS[key] = fn()
    return _PROGS[key]


def _vecs(I, i, gnext):
    v = np.zeros((128, 64), np.float32)
    for k, g in enumerate([I["norm_mix_g"][i], I["norm_mem_g"][i], I["norm_ffn_g"][i], gnext]):
        v[:, 16 * k:16 * (k + 1)] = np.asarray(g, np.float32).reshape(16, 128).T
    return v


def _lru_inputs(I, j, g, hn):
    ch = np.arange(g * 704, (g + 1) * 704)
    cw = I["lru_conv_w"][j]
    lvec = np.stack([cw[0, ch], cw[1, ch], cw[2, ch], cw[3, ch], I["lru_conv_b"][j][ch], I["lru_gate_a_b"][j][ch],
                     I["lru_gate_x_b"][j][ch], I["lru_lambda"][j][ch]], -1)
    lvec = np.ascontiguousarray(lvec.reshape(8, 88, 8).transpose(1, 0, 2))
    return {"hn": hn, "wx": np.ascontiguousarray(I["lru_w_in"][j][:, ch]), "wg": np.ascontiguousarray(I["lru_w_in"][j][:, D_RNN + ch]),
            "ga": np.ascontiguousarray(I["lru_gate_a_w"][j][4 * g:4 * g + 4]), "gx": np.ascontiguousarray(I["lru_gate_x_w"][j][4 * g:4 * g + 4]),
            "lvec": lvec}


def _gdn_inputs(I, j, g, hn):
    Wi = I["gdn_w_in"][j]
    hq = np.arange(4 * g * 128, (4 * g + 4) * 128)
    hv = np.arange(8 * g * 128, (8 * g + 8) * 128)
    cols = np.concatenate([hq, GQK + hq, 2 * GQK + hv, 2 * GQK + GV + hv, 2 * GQK + 2 * GV + np.arange(8 * g, 8 * g + 8),
                           2 * GQK + 2 * GV + 32 + np.arange(8 * g, 8 * g + 8)])
    W = np.ascontiguousarray(Wi[:, cols])
    cch = np.concatenate([hq, GQK + hq, 2 * GQK + hv])
    cvec = np.ascontiguousarray(I["gdn_conv_w"][j][:, cch].T.reshape(16, 128, 4).transpose(1, 0, 2))
    gvec = np.ascontiguousarray(np.asarray(I["gdn_norm_g"][j], np.float32).reshape(128, 1))
    pvec = np.ascontiguousarray(np.stack([I["gdn_a_log"][j][8 * g:8 * g + 8], I["gdn_dt_bias"][j][8 * g:8 * g + 8]], -1).astype(np.float32))
    return {"hn": hn, "w": W, "cvec": cvec, "gvec": gvec, "pvec": pvec}


def kernel_unfused(I, NB=2, S=4096, depth=4):
    I = {k: np.asarray(v) for k, v in I.items()}
    Sc = S // 4
    T = min(512, Sc)
    TT = min(512, S)
    ncores = 4 * NB
    cores = list(range(ncores))
    x = I["x"]
    ncA = _prog(("tokA", Sc, T), lambda: build_tok_program(Sc, T, "lru", True, False, do_main=False))
    hs = [np.ascontiguousarray(x[i // 4, (i % 4) * Sc:(i % 4 + 1) * Sc, :].T) for i in cores]
    res = run_bass_kernel_spmd(ncA, [{"hin": hs[i], "vecs": _vecs(I, 0, I["norm_mix_g"][0])} for i in cores], core_ids=cores).results
    hn = [np.concatenate([res[b * 4 + c]["hnn"] for c in range(4)], axis=1) for b in range(NB)]
    out = None
    for i in range(depth):
        j = i // 2
        kind = "lru" if i % 2 == 0 else "gdn"
        final = (i == depth - 1)
        if kind == "lru":
            ncB = _prog(("lru", S, TT), lambda: build_lru_program(S, TT))
            ins = [_lru_inputs(I, j, c % 4, hn[c // 4]) for c in cores]
        else:
            ncB = _prog(("gdn", S, TT), lambda: build_gdn_program(S, TT))
            ins = [_gdn_inputs(I, j, c % 4, hn[c // 4]) for c in cores]
        res = run_bass_kernel_spmd(ncB, ins, core_ids=cores).results
        y = [np.concatenate([res[b * 4 + g]["y"] for g in range(4)], axis=0) for b in range(NB)]
        ncC = _prog(("tokC", Sc, T, kind, final), lambda: build_tok_program(Sc, T, kind, False, final))
        w_in = I["lru_w_in"][j] if kind == "lru" else I["gdn_w_in"][j]
        w_xq = np.ascontiguousarray(w_in[:, -D:])
        w_out = I["lru_w_out"][j] if kind == "lru" else I["gdn_w_out"][j]
        gnext = I["final_norm_g"] if final else I["norm_mix_g"][i + 1]
        vecs = _vecs(I, i, gnext)
        ins = []
        for c in cores:
            b, q = c // 4, c % 4
            ins.append({"hin": hs[c], "vecs": vecs, "y": np.ascontiguousarray(y[b][:, q * Sc:(q + 1) * Sc]), "mem": I["mem"][b],
                        "w_xq": w_xq, "w_kv": I["mem_kv_w"][i], "w_out": w_out, "w_fi": I["ffn_w_in"][i], "w_fo": I["ffn_w_out"][i]})
        res = run_bass_kernel_spmd(ncC, ins, core_ids=cores).results
        if final:
            out = np.zeros((NB, S, D), np.float32)
            for c in cores:
                out[c // 4, (c % 4) * Sc:(c % 4 + 1) * Sc, :] = res[c]["out"].T
        else:
            hs = [res[c]["hout"] for c in cores]
            hn = [np.concatenate([res[b * 4 + c]["hnn"] for c in range(4)], axis=1) for b in range(NB)]
    return out


def kernel_fused(I, NB=2, S=4096, depth=4, debug=False):
    I = {k: np.asarray(v) for k, v in I.items()}
    Sc = S // 4
    ncores = 4 * NB
    cores = list(range(ncores))
    nc = _prog(("fused", S, depth, NB, debug), lambda: build_fused_program(S, depth, NB, debug))
    vecs = np.concatenate([_vecs(I, i, I["final_norm_g"] if i == depth - 1 else I["norm_mix_g"][i + 1]) for i in range(depth)], axis=1)
    shared = {}
    for i in range(depth):
        j = i // 2
        kind = "lru" if i % 2 == 0 else "gdn"
        w_in = I["lru_w_in"][j] if kind == "lru" else I["gdn_w_in"][j]
        shared["w_xq%d" % i] = np.ascontiguousarray(w_in[:, -D:])
        shared["w_kv%d" % i] = I["mem_kv_w"][i]
        shared["w_out%d" % i] = I["lru_w_out"][j] if kind == "lru" else I["gdn_w_out"][j]
        shared["w_fi%d" % i] = I["ffn_w_in"][i]
        shared["w_fo%d" % i] = I["ffn_w_out"][i]
    ins = []
    for cidx in cores:
        b, g = cidx // 4, cidx % 4
        m = dict(shared)
        m["x"] = np.ascontiguousarray(I["x"][b, g * Sc:(g + 1) * Sc, :].T)
        m["mem"] = I["mem"][b]
        m["vecs"] = vecs
        for i in range(depth):
            j = i // 2
            if i % 2 == 0:
                li = _lru_inputs(I, j, g, None)
                for k in ("wx", "wg", "ga", "gx", "lvec"):
                    m["%s%d" % (k, i)] = li[k]
            else:
                gi = _gdn_inputs(I, j, g, None)
                m["gw%d" % i] = gi["w"]
                for k in ("cvec", "gvec", "pvec"):
                    m["%s%d" % (k, i)] = gi[k]
        ins.append(m)
    res = run_bass_kernel_spmd(nc, ins, core_ids=cores).results
    out = np.zeros((NB, S, D), np.float32)
    for cidx in cores:
        out[cidx // 4, (cidx % 4) * Sc:(cidx % 4 + 1) * Sc, :] = res[cidx]["out"].T
    if debug:
        return out, res
    return out


def kernel(**inputs):
    return kernel_fused(inputs)
```

```python
import numpy as np
from contextlib import ExitStack
import ml_dtypes
import concourse.bass as bass
import concourse.mybir as mybir
from concourse.bass_utils import run_bass_kernel_spmd

F32 = mybir.dt.float32
BF16 = mybir.dt.bfloat16
ALU = mybir.AluOpType
AF = mybir.ActivationFunctionType
AX = mybir.AxisListType
NPBF = ml_dtypes.bfloat16

D = 2048
KD = 16
EPS = 1e-6
MEM = 256
D_RNN = 2816
D_FF = 5632
XH = 4
GQK = 2048
GV = 4096
GDN_IN_MIX = 2 * GQK + 2 * GV + 64
ENGS = ("pe", "dve", "act", "pool", "sp")


class Res:
    __slots__ = ("name", "w", "r")

    def __init__(self, name=""):
        self.name = name
        self.w = None
        self.r = {}


class Prog:
    def __init__(self, nc, es):
        self.nc = nc
        self.es = es
        self.sems = {}
        self.cnt = {}
        self.known = {e: {} for e in ENGS}
        self.streams = {e: [] for e in ENGS}
        for e in ENGS:
            self._newsem(e)
        self.ndma = 0
        self.named = {}

    def _newsem(self, key):
        self.sems[key] = self.es.enter_context(self.nc.semaphore("s%d" % len(self.sems)))
        self.cnt[key] = 0

    def chan(self, name=None):
        if name is not None and name in self.named:
            return self.named[name]
        key = ("dma", self.ndma)
        self.ndma += 1
        self._newsem(key)
        if name is not None:
            self.named[name] = key
        return key

    def cc_chan(self, name):
        if name in self.named:
            return self.named[name]
        key = ("cc", self.ndma)
        self.ndma += 1
        self._newsem(key)
        self.named[name] = key
        return key

    def _deps(self, eng, reads, writes):
        deps = {}

        def add(d):
            if d is not None and deps.get(d[0], 0) < d[1]:
                deps[d[0]] = d[1]
        for r in reads:
            add(r.w)
        for w in writes:
            add(w.w)
            for k, n in w.r.items():
                add((k, n))
        waits = []
        for k, n in deps.items():
            if k == eng and eng == "pe":
                continue
            if isinstance(k, tuple) and k[0] == "dma":
                n = self.cnt[k]
            if self.known[eng].get(k, 0) < n:
                self.known[eng][k] = n
                waits.append((k, n))
        return waits

    def op(self, eng, fn, reads=(), writes=()):
        waits = self._deps(eng, reads, writes)
        self.cnt[eng] += 1
        n = self.cnt[eng]
        for r in reads:
            r.r[eng] = n
        for w in writes:
            w.w = (eng, n)
            w.r = {}
        self.streams[eng].append((waits, fn, (eng, 1)))

    def dma(self, queue, ch, fn, reads=(), writes=(), ndma=1, gate=()):
        waits = self._deps(queue, list(reads) + list(gate), writes)
        self.cnt[ch] += 16 * ndma
        n = self.cnt[ch]
        for r in reads:
            r.r[ch] = n
        for w in writes:
            w.w = (ch, n)
            w.r = {}
        self.streams[queue].append((waits, fn, (ch, 16)))

    def coll(self, key, fn, reads=(), writes=()):
        waits = self._deps("pool", reads, writes)
        self.cnt[key] += 1
        n = self.cnt[key]
        for r in reads:
            r.r[key] = n
        for w in writes:
            w.w = (key, n)
            w.r = {}
        self.streams["pool"].append((waits, fn, (key, None)))

    def barrier(self):
        for e in ENGS:
            waits = []
            for k, n in self.cnt.items():
                if n > 0 and self.known[e].get(k, 0) < n and k != e:
                    self.known[e][k] = n
                    waits.append((k, n))
            if waits:
                self.streams[e].append((waits, None, None))

    def emit(self):
        sems = self.sems

        def replay(stream):
            def run(e):
                for waits, fn, inc in stream:
                    for k, n in waits:
                        e.wait_ge(sems[k], n)
                    if fn is None:
                        continue
                    ins = fn(e)
                    if inc[1] is None:
                        ins.then_inc(sems[inc[0]])
                    elif isinstance(ins, (list, tuple)):
                        for i in ins:
                            i.then_inc(sems[inc[0]], inc[1])
                    else:
                        ins.then_inc(sems[inc[0]], inc[1])
            return run
        with self.nc.Block() as block:
            block.tensor(replay(self.streams["pe"]))
            block.vector(replay(self.streams["dve"]))
            block.scalar(replay(self.streams["act"]))
            block.gpsimd(replay(self.streams["pool"]))
            block.sync(replay(self.streams["sp"]))


class Ctx:
    pass


_UID = [0]


def _uid():
    _UID[0] += 1
    return _UID[0]


def make_ctx(nc, es, P, nslots=3):
    c = Ctx()
    c.nc, c.es, c.P = nc, es, P
    sb = lambda name, shape, dt=F32: es.enter_context(nc.sbuf_tensor("sb_" + name, shape, dt))
    c.sb = sb
    c.ones_f = sb("ones_f", [128, 128], F32)
    c.ones_b = sb("ones_b", [128, 128], BF16)
    c.ident = sb("ident", [128, 128], F32)
    c.r_const = Res("const")
    P.op("pool", lambda e: e.memset(c.ones_f[:], 1.0), writes=[c.r_const])
    P.op("pool", lambda e: e.memset(c.ones_b[:], 1.0), writes=[c.r_const])
    P.op("pool", lambda e: e.memset(c.ident[:], 0.0), writes=[c.r_const])
    P.op("pool", lambda e: e.affine_select(out=c.ident[:], in_=c.ident[:], pattern=[[-1, 128]],
                                           compare_op=ALU.not_equal, fill=1.0, base=0,
                                           channel_multiplier=1),
         reads=[c.r_const], writes=[c.r_const])
    c.nslots = nslots
    c.wslots = [sb("wslot%d" % i, [128, 32, 128], BF16) for i in range(nslots)]
    c.wres = [Res("w%d" % i) for i in range(nslots)]
    c.wch = [P.chan() for _ in range(nslots)]
    c.wnext = 0
    c.ps = [es.enter_context(nc.psum_tensor("ps%d" % i, [128, 512], F32)) for i in range(8)]
    c.psr = [Res("ps%d" % i) for i in range(8)]
    c.psn = 0
    c.progress = None
    return c


def next_ps(c, lo=0, hi=4):
    i = lo + (c.psn % (hi - lo))
    c.psn += 1
    return c.ps[i], c.psr[i]


class Prep:
    pass


def wprep(c, blocks, kctot, tasks=None):
    P, nc = c.P, c.nc
    pr = Prep()
    pr.n = len(blocks)
    pr.kctot = kctot
    pr.blocks = blocks
    pr.ap = nc.dram_tensor("wb%d" % _uid(), [len(blocks), 128, kctot * 128], BF16).ap()
    pr.res = [Res("prep") for _ in blocks]
    for i in range(len(blocks)):
        if tasks is None:
            prep_issue(c, pr, i)
        else:
            tasks.append((pr, i))
    return pr


def prep_issue(c, pr, i, gate=()):
    segs = pr.blocks[i]

    def fn(e, segs=segs, i=i):
        out = []
        dst = pr.ap[i].rearrange("p (kc m) -> p kc m", m=128)
        for (W, row0, KC, rpc, col0, MB, off) in segs:
            src = W[row0:row0 + KC * rpc, col0:col0 + MB].rearrange("(kc p) m -> p kc m", p=rpc)
            out.append(e.dma_start(out=dst[0:rpc, off:off + KC, 0:MB], in_=src))
        return out
    c.P.dma("pool", c.P.chan("prep"), fn, gate=list(gate), writes=[pr.res[i]], ndma=len(segs))


def wfetch(c, pr, i):
    P = c.P
    k = c.wnext
    c.wnext = (c.wnext + 1) % c.nslots
    slot, res, ch = c.wslots[k], c.wres[k], c.wch[k]
    P.dma("sp", ch, lambda e, slot=slot, i=i: e.dma_start(out=slot[:, 0:pr.kctot, :].rearrange("p kc m -> p (kc m)"), in_=pr.ap[i]),
          reads=[pr.res[i]], writes=[res])
    return slot, res


def prep_layer_weights(c, w_xq, w_kv, w_out, w_fi, w_fo, nyc, rpc_y, tasks=None):
    d = {}
    d["p_kv"] = wprep(c, [[(w_kv, 0, KD, 128, mo * 128, 128, 0)] for mo in range(2 * KD)], KD, tasks)
    d["p_xq"] = wprep(c, [[(w_xq, 0, KD, 128, b * 128, 128, 0)] for b in range(KD)], KD, tasks)
    ob = []
    for mo in range(KD):
        ob.append([(w_out, 0, 24, rpc_y, mo * 128, 128, 0)])
        ob.append([(w_out, 24 * rpc_y, nyc - 24, rpc_y, mo * 128, 128, 0), (w_out, nyc * rpc_y, KD, 128, mo * 128, 128, nyc - 24)])
    d["p_out"] = wprep(c, ob, 24, tasks)
    d["p_fi"] = wprep(c, [[(w_fi, 0, KD, 128, j * 128, 128, 0), (w_fi, 0, KD, 128, D_FF + j * 128, 128, KD)] for j in range(D_FF // 128)], 2 * KD, tasks)
    d["p_fo"] = wprep(c, [[(w_fo, half * 22 * 128, 22, 128, mo * 128, 128, 0)] for mo in range(KD) for half in range(2)], 22, tasks)
    return d


def prep_gdn_weights(c, w_ap, tasks=None):
    blocks = [[(w_ap, 0, KD, 128, f * 128, 128, 0)] for f in range(24)] + [[(w_ap, 0, KD, 128, 3072, 16, 0)]]
    return wprep(c, blocks, KD, tasks)


def emit_norm(c, h, rh, tsl, T, gcol, hn, rhn, tmp, inplace=False):
    P = c.P
    ps, rps = c.ps[7], c.psr[7]
    for kc in range(KD):
        sq, rsq = tmp.sq[kc % 2], tmp.rsq[kc % 2]
        P.op("act", lambda e, kc=kc, sq=sq: e.activation(out=sq[:, 0:T], in_=h[:, kc, tsl], func=AF.Square),
             reads=[rh], writes=[rsq])
        P.op("pe", lambda e, kc=kc, sq=sq: e.matmul(ps[:, 0:T], lhsT=c.ones_f[:], rhs=sq[:, 0:T],
                                                     start=(kc == 0), stop=(kc == KD - 1)),
             reads=[rsq, c.r_const], writes=[rps])
    P.op("act", lambda e: e.activation(out=tmp.rstd[:, 0:T], in_=ps[:, 0:T], func=AF.Sqrt, bias=EPS, scale=1.0 / D),
         reads=[rps], writes=[tmp.rrstd])
    P.op("dve", lambda e: e.reciprocal(out=tmp.rstd[:, 0:T], in_=tmp.rstd[:, 0:T]),
         reads=[tmp.rrstd], writes=[tmp.rrstd])
    for kc in range(KD):
        P.op("dve", lambda e, kc=kc: e.scalar_tensor_tensor(
            out=(h[:, kc, tsl] if inplace else hn[:, kc, 0:T]), in0=h[:, kc, tsl], scalar=c.vecs[:, gcol + kc:gcol + kc + 1],
            in1=tmp.rstd[:, 0:T], op0=ALU.mult, op1=ALU.mult),
            reads=[rh, tmp.rrstd, c.r_vecs], writes=[rh if inplace else rhn])


class Tmp:
    pass


def make_tmp(c, T):
    t = Tmp()
    t.sq = [c.sb("sq%d" % i, [128, T], F32) for i in range(2)]
    t.rsq = [Res("sq%d" % i) for i in range(2)]
    t.rstd = c.sb("rstd", [128, T], F32)
    t.rrstd = Res("rstd")
    return t


def emit_kv(c, L, mem_ap, p_kv, gcol_mem):
    P, nc = c.P, c.nc
    KT_, V_ = c.KT, c.V
    with ExitStack() as es2:
        u = _uid()
        sb2 = lambda name, shape, dt=F32: es2.enter_context(nc.sbuf_tensor("sk%d_%s" % (u, name), shape, dt))
        memt = sb2("memt", [128, 2, D], F32)
        rmem = Res("memt")
        sqj = sb2("sqj", [128, D], F32)
        rsqj = Res("sqj")
        ssq = sb2("ssq", [128, 2], F32)
        rssq = Res("ssq")
        memT = sb2("memT", [128, KD, MEM], BF16)
        rmemT = Res("memT")
        ch = P.chan("kvmem")
        P.dma("sp", ch, lambda e: e.dma_start(out=memt[:], in_=mem_ap.rearrange("(mc p) d -> p mc d", p=128)),
              writes=[rmem])
        for mc in range(2):
            P.op("act", lambda e, mc=mc: e.activation(out=sqj[:], in_=memt[:, mc, :], func=AF.Square),
                 reads=[rmem], writes=[rsqj])
            P.op("dve", lambda e, mc=mc: e.reduce_sum(out=ssq[:, mc:mc + 1], in_=sqj[:], axis=AX.X),
                 reads=[rsqj], writes=[rssq])
        P.op("act", lambda e: e.activation(out=ssq[:], in_=ssq[:], func=AF.Sqrt, bias=EPS, scale=1.0 / D),
             reads=[rssq], writes=[rssq])
        P.op("dve", lambda e: e.reciprocal(out=ssq[:], in_=ssq[:]), reads=[rssq], writes=[rssq])
        for mc in range(2):
            P.op("dve", lambda e, mc=mc: e.tensor_scalar(out=memt[:, mc, :], in0=memt[:, mc, :],
                                                           scalar1=ssq[:, mc:mc + 1], scalar2=None, op0=ALU.mult),
                 reads=[rmem, rssq], writes=[rmem])
        for kc in range(KD):
            for mc in range(2):
                ps, rps = next_ps(c)
                P.op("pe", lambda e, kc=kc, mc=mc, ps=ps: e.transpose(ps[:, 0:128], memt[:, mc, kc * 128:(kc + 1) * 128], c.ident[:]),
                     reads=[rmem, c.r_const], writes=[rps])
                P.op("act", lambda e, kc=kc, mc=mc, ps=ps: e.activation(
                    out=memT[:, kc, mc * 128:(mc + 1) * 128], in_=ps[:, 0:128], func=AF.Copy,
                    scale=c.vecs[:, gcol_mem + kc:gcol_mem + kc + 1]),
                    reads=[rps, c.r_vecs], writes=[rmemT])
        for mo in range(KD):
            slot, rw = wfetch(c, p_kv, mo)
            ps, rps = next_ps(c)

            def mm(e, slot=slot, ps=ps):
                r = None
                for kc in range(KD):
                    r = e.matmul(ps[:, 0:MEM], lhsT=slot[:, kc, 0:128], rhs=memT[:, kc, :], start=(kc == 0), stop=(kc == KD - 1))
                return r
            P.op("pe", mm, reads=[rw, rmemT], writes=[rps])
            P.op("act", lambda e, mo=mo, ps=ps: e.copy(out=KT_[:, mo, :], in_=ps[:, 0:MEM]), reads=[rps], writes=[c.rKT])
        for cb in range(KD):
            slot, rw = wfetch(c, p_kv, KD + cb)
            for mc in range(2):
                ps, rps = next_ps(c)

                def mm(e, slot=slot, ps=ps, mc=mc):
                    r = None
                    for kc in range(KD):
                        r = e.matmul(ps[:, 0:128], lhsT=memT[:, kc, mc * 128:(mc + 1) * 128], rhs=slot[:, kc, 0:128],
                                     start=(kc == 0), stop=(kc == KD - 1))
                    return r
                P.op("pe", mm, reads=[rw, rmemT], writes=[rps])
                P.op("dve", lambda e, cb=cb, mc=mc, ps=ps: e.tensor_copy(out=V_[:, mc, cb * 128:(cb + 1) * 128], in_=ps[:, 0:128]),
                     reads=[rps], writes=[c.rV])
        P.barrier()


def emit_tok_pass(c, lay, tsl, T, hn_dst, out_dst, do_main=True):
    P, nc = c.P, c.nc
    h, rh = c.h, c.rh
    hn, rhn = c.hn, c.rhn
    big, rbig = c.big, c.rbig
    tmp = c.tmp
    nyc, rpc_y = lay["nyc"], lay["rpc_y"]
    scale = float(512 ** -0.5)

    if do_main:
        emit_tok_main(c, lay, tsl, T)
    if hn_dst is not None:
        emit_norm(c, h, rh, tsl, T, lay["g_next"], hn, rhn, tmp)
        hn_dst(tsl)
    if out_dst is not None:
        emit_norm(c, h, rh, tsl, T, lay["g_next"], None, None, tmp, inplace=True)
        P.dma("sp", c.och, lambda e: e.dma_start(out=out_dst(tsl), in_=h[:, :, tsl]), reads=[rh])


def emit_tok_main(c, lay, tsl, T):
    P, nc = c.P, c.nc
    h, rh = c.h, c.rh
    hn, rhn = c.hn, c.rhn
    big, rbig = c.big, c.rbig
    tmp = c.tmp
    nyc, rpc_y = lay["nyc"], lay["rpc_y"]
    scale = float(512 ** -0.5)
    KT_, V_, qh_, ex_, rden_ = c.KT, c.V, c.qh, c.ex, c.rden
    emit_norm(c, h, rh, tsl, T, lay["g_mix"], hn, rhn, tmp)
    ych = c.ych
    P.dma("sp", ych, lambda e: e.dma_start(out=big[0:rpc_y, 0:nyc, 0:T], in_=lay["y_ap"](tsl, e)), reads=list(lay.get("y_res", ())), writes=[rbig])
    for hh in range(XH):
        for dc in range(4):
            slot, rw = wfetch(c, lay["p_xq"], hh * 4 + dc)
            ps, rps = next_ps(c)

            def mm(e, slot=slot, ps=ps):
                r = None
                for kc in range(KD):
                    r = e.matmul(ps[:, 0:T], lhsT=slot[:, kc, 0:128], rhs=hn[:, kc, 0:T], start=(kc == 0), stop=(kc == KD - 1))
                return r
            P.op("pe", mm, reads=[rw, rhn], writes=[rps])
            P.op("act", lambda e, dc=dc, ps=ps: e.copy(out=qh_[:, dc, 0:T], in_=ps[:, 0:T]), reads=[rps], writes=[c.rqh])
        for mc in range(2):
            ps, rps = next_ps(c, 4, 6)

            def mm(e, ps=ps, mc=mc, hh=hh):
                r = None
                for dc in range(4):
                    r = e.matmul(ps[:, 0:T], lhsT=KT_[:, hh * 4 + dc, mc * 128:(mc + 1) * 128], rhs=qh_[:, dc, 0:T],
                                 start=(dc == 0), stop=(dc == 3))
                return r
            P.op("pe", mm, reads=[c.rKT, c.rqh], writes=[rps])
            P.op("act", lambda e, ps=ps, mc=mc: e.activation(out=ex_[:, mc, 0:T], in_=ps[:, 0:T], func=AF.Exp, scale=scale),
                 reads=[rps], writes=[c.rex])
        ps, rps = c.ps[6], c.psr[6]

        def mmd(e, ps=ps):
            r = None
            for mc in range(2):
                r = e.matmul(ps[:, 0:T], lhsT=c.ones_b[:], rhs=ex_[:, mc, 0:T], start=(mc == 0), stop=(mc == 1))
            return r
        P.op("pe", mmd, reads=[c.rex, c.r_const], writes=[rps])
        P.op("dve", lambda e, ps=ps: e.reciprocal(out=rden_[:, 0:T], in_=ps[:, 0:T]), reads=[rps], writes=[c.rrden])
        for dc in range(4):
            ps, rps = next_ps(c, 4, 6)

            def mmo(e, ps=ps, dc=dc, hh=hh):
                r = None
                for mc in range(2):
                    r = e.matmul(ps[:, 0:T], lhsT=V_[:, mc, hh * 512 + dc * 128: hh * 512 + (dc + 1) * 128],
                                 rhs=ex_[:, mc, 0:T], start=(mc == 0), stop=(mc == 1))
                return r
            P.op("pe", mmo, reads=[c.rV, c.rex], writes=[rps])
            P.op("dve", lambda e, ps=ps, dc=dc, hh=hh: e.tensor_tensor(out=big[:, nyc + hh * 4 + dc, 0:T], in0=ps[:, 0:T],
                                                                       in1=rden_[:, 0:T], op=ALU.mult),
                 reads=[rps, c.rrden], writes=[rbig])
    for mo in range(KD):
        ps, rps = next_ps(c)
        slot, rw = wfetch(c, lay["p_out"], 2 * mo)

        def mm0(e, slot=slot, ps=ps):
            r = None
            for kc in range(24):
                r = e.matmul(ps[:, 0:T], lhsT=slot[0:rpc_y, kc, 0:128], rhs=big[0:rpc_y, kc, 0:T], start=(kc == 0), stop=False)
            return r
        P.op("pe", mm0, reads=[rw, rbig], writes=[rps])
        slot, rw = wfetch(c, lay["p_out"], 2 * mo + 1)

        def mm1(e, slot=slot, ps=ps):
            r = None
            for kc in range(24, nyc):
                r = e.matmul(ps[:, 0:T], lhsT=slot[0:rpc_y, kc - 24, 0:128], rhs=big[0:rpc_y, kc, 0:T], start=False, stop=False)
            for kc in range(nyc, nyc + KD):
                r = e.matmul(ps[:, 0:T], lhsT=slot[:, kc - 24, 0:128], rhs=big[:, kc, 0:T], start=False, stop=(kc == nyc + KD - 1))
            return r
        P.op("pe", mm1, reads=[rw, rbig], writes=[rps])
        P.op("dve", lambda e, mo=mo, ps=ps: e.tensor_tensor(out=h[:, mo, tsl], in0=ps[:, 0:T], in1=h[:, mo, tsl], op=ALU.add),
             reads=[rps, rh], writes=[rh])

    emit_norm(c, h, rh, tsl, T, lay["g_ffn"], hn, rhn, tmp)
    NJ = D_FF // 128
    for j in range(NJ):
        slot, rw = wfetch(c, lay["p_fi"], j)
        psg, rpsg = next_ps(c)
        psu, rpsu = next_ps(c)

        def mm(e, slot=slot, psg=psg, psu=psu):
            r = None
            for kc in range(KD):
                r = e.matmul(psg[:, 0:T], lhsT=slot[:, kc, 0:128], rhs=hn[:, kc, 0:T], start=(kc == 0), stop=(kc == KD - 1))
            for kc in range(KD):
                r = e.matmul(psu[:, 0:T], lhsT=slot[:, KD + kc, 0:128], rhs=hn[:, kc, 0:T], start=(kc == 0), stop=(kc == KD - 1))
            return r
        P.op("pe", mm, reads=[rw, rhn], writes=[rpsg, rpsu])
        sg, rsg = tmp.sq[j % 2], tmp.rsq[j % 2]
        P.op("act", lambda e, psg=psg, sg=sg: e.activation(out=sg[:, 0:T], in_=psg[:, 0:T], func=AF.Silu), reads=[rpsg], writes=[rsg])
        P.op("dve", lambda e, j=j, psu=psu, sg=sg: e.tensor_tensor(out=big[:, j, 0:T], in0=psu[:, 0:T], in1=sg[:, 0:T], op=ALU.mult),
             reads=[rpsu, rsg], writes=[rbig])
    for mo in range(KD):
        ps, rps = next_ps(c)
        for half in range(2):
            slot, rw = wfetch(c, lay["p_fo"], mo * 2 + half)

            def mm(e, slot=slot, ps=ps, half=half):
                r = None
                for kc in range(22):
                    r = e.matmul(ps[:, 0:T], lhsT=slot[:, kc, 0:128], rhs=big[:, half * 22 + kc, 0:T],
                                 start=(half == 0 and kc == 0), stop=(half == 1 and kc == 21))
                return r
            P.op("pe", mm, reads=[rw, rbig], writes=[rps])
        P.op("dve", lambda e, mo=mo, ps=ps: e.tensor_tensor(out=h[:, mo, tsl], in0=ps[:, 0:T], in1=h[:, mo, tsl], op=ALU.add),
             reads=[rps, rh], writes=[rh])


def emit_lru(c, S, TT, hn_src, wx_ap, wg_ap, ga_ap, gx_ap, lvec_ap, y_dst, tile_done=None, rhn_d=(), ry_d=(), after_setup=None):
    P, nc = c.P, c.nc
    NT = S // TT
    with ExitStack() as es2:
        cnt = [0]
        u = _uid()

        def sb2(name, shape, dt=F32):
            cnt[0] += 1
            return es2.enter_context(nc.sbuf_tensor("sl%d_%s_%d" % (u, name, cnt[0]), shape, dt))
        Wx = sb2("Wx", [128, KD, 704], BF16)
        Wg = sb2("Wg", [128, KD, 704], BF16)
        Ga = sb2("Ga", [88, 4, 2, 176], BF16)
        Gx = sb2("Gx", [88, 4, 2, 176], BF16)
        lv = sb2("lv", [88, 8, 8], F32)
        nsp = sb2("nsp", [88, 8], F32)
        rWl = []
        ch = P.chan("lruw")

        def ld(queue, fn):
            r = Res("lw")
            rWl.append(r)
            P.dma(queue, ch, fn, writes=[r])
        for q in range(4):
            ld("pool", lambda e, q=q: e.dma_start(
                out=Wx[:, q * 4:(q + 1) * 4, :], in_=wx_ap[q * 512:(q + 1) * 512, :].rearrange("(kc p) m -> p kc m", p=128)))
            ld("pool", lambda e, q=q: e.dma_start(
                out=Wg[:, q * 4:(q + 1) * 4, :], in_=wg_ap[q * 512:(q + 1) * 512, :].rearrange("(kc p) m -> p kc m", p=128)))
        ld("pool", lambda e: e.dma_start(out=Ga[:], in_=ga_ap.rearrange("b (kh p) n -> p b kh n", p=88)))
        ld("pool", lambda e: e.dma_start(out=Gx[:], in_=gx_ap.rearrange("b (kh p) n -> p b kh n", p=88)))
        ld("sp", lambda e: e.dma_start(out=lv[:], in_=lvec_ap))
        if after_setup is not None:
            after_setup()
        rW = Res("lruW")
        P.op("act", lambda e: e.activation(out=nsp[:], in_=lv[:, :, 7], func=AF.Exp, scale=-1.0), reads=rWl, writes=[rW])
        P.op("act", lambda e: e.activation(out=nsp[:], in_=nsp[:], func=AF.Ln, bias=1.0), reads=[rW], writes=[rW])
        P.op("dve", lambda e: e.tensor_scalar(out=nsp[:], in0=nsp[:], scalar1=-8.0, scalar2=None, op0=ALU.mult), reads=[rW], writes=[rW])

        hnb = [sb2("hnb", [128, KD, TT], BF16) for _ in range(2)]
        rhnb = [Res("hnb0"), Res("hnb1")]
        hch = [P.chan("hch0"), P.chan("hch1")]
        xpad = sb2("xpad", [88, 8, TT + 3], F32)
        rxpad = [Res("xpad%d" % i) for i in range(8)]
        xc = sb2("xc", [88, 8, TT], F32)
        rxc = [Res("xc%d" % i) for i in range(8)]
        xcb = sb2("xcb", [88, 8, TT], BF16)
        rxcb = [Res("xcb%d" % i) for i in range(8)]
        hst = sb2("hst", [88, 8], F32)
        rhst = [Res("hst%d" % i) for i in range(8)]
        P.op("dve", lambda e: e.memset(xpad[:], 0.0), writes=rxpad)
        P.op("dve", lambda e: e.memset(hst[:], 0.0), writes=rhst)
        roles = ["bx", "hs", "gl"]
        tb = {r: [sb2(r, [88, TT], F32) for _ in range(2)] for r in roles}
        tr = {r: [Res(r + "0"), Res(r + "1")] for r in roles}
        tA = sb2("tA", [88, TT], F32); rtA = Res("tA")
        tI = sb2("tI", [88, 4, TT], F32); rtI = [Res("tI%d" % i) for i in range(4)]
        aB = sb2("aB", [88, 4, TT], F32); raB = [Res("aB%d" % i) for i in range(4)]
        sB = sb2("sB", [88, 4, TT], F32); rsB = Res("sB")
        hb2 = sb2("hb2", [88, 8, 2], F32)
        nsph = sb2("nsph", [88, 8], F32)
        P.op("dve", lambda e: e.tensor_scalar(out=hb2[:], in0=lv[:, :, 5:7], scalar1=0.5, scalar2=None, op0=ALU.mult), reads=rWl, writes=[rW])
        P.op("dve", lambda e: e.tensor_scalar(out=nsph[:], in0=nsp[:], scalar1=0.5, scalar2=None, op0=ALU.mult), reads=[rW], writes=[rW])
        ytl = sb2("yt", [88, 8, TT], BF16)
        ryt = [Res("yt%d" % i) for i in range(8)]
        ych = [P.chan("ych0"), P.chan("ych1")]

        def lvs(cc, k):
            return lv[:, cc, k:k + 1]

        for t in range(NT):
            hb, rhb = hnb[t % 2], rhnb[t % 2]
            tsl = slice(t * TT, (t + 1) * TT)
            P.dma("sp", hch[t % 2], lambda e, hb=hb, t=t: e.dma_start(out=hb[:], in_=hn_src(t)), reads=list(rhn_d), writes=[rhb])
            for cc in range(8):
                ps, rps = next_ps(c)

                def mm(e, ps=ps, cc=cc, hb=hb):
                    r = None
                    for kc in range(KD):
                        r = e.matmul(ps[0:88, 0:TT], lhsT=Wx[:, kc, cc * 88:(cc + 1) * 88], rhs=hb[:, kc, :], start=(kc == 0), stop=(kc == KD - 1))
                    return r
                P.op("pe", mm, reads=[rW, rhb], writes=[rps])
                P.op("act", lambda e, ps=ps, cc=cc: e.copy(out=xpad[:, cc, 3:3 + TT], in_=ps[0:88, 0:TT]), reads=[rps], writes=[rxpad[cc]])
                P.op("dve", lambda e, cc=cc: e.tensor_scalar(out=xc[:, cc, :], in0=xpad[:, cc, 0:TT], scalar1=lvs(cc, 0), scalar2=lvs(cc, 4),
                                                             op0=ALU.mult, op1=ALU.add), reads=[rxpad[cc], rW], writes=[rxc[cc]])
                for k in range(1, 4):
                    P.op("dve", lambda e, cc=cc, k=k: e.scalar_tensor_tensor(out=xc[:, cc, :], in0=xpad[:, cc, k:k + TT], scalar=lvs(cc, k),
                                                                             in1=xc[:, cc, :], op0=ALU.mult, op1=ALU.add),
                         reads=[rxpad[cc], rxc[cc], rW], writes=[rxc[cc]])
                P.op("dve", lambda e, cc=cc: e.tensor_copy(out=xpad[:, cc, 0:3], in_=xpad[:, cc, TT:TT + 3]), reads=[rxpad[cc]], writes=[rxpad[cc]])
                P.op("act", lambda e, cc=cc: e.copy(out=xcb[:, cc, :], in_=xc[:, cc, :]), reads=[rxc[cc]], writes=[rxcb[cc]])
            for g4 in range(2):
                for cc in range(g4 * 4, g4 * 4 + 4):
                    bl, oh = cc // 2, cc % 2
                    ci4 = cc % 4
                    for (G, dstt, rdst_, bcol) in ((Ga, tA, rtA, 0), (Gx, tI, rtI[ci4], 1)):
                        ps, rps = next_ps(c, 4, 7)

                        def mmg(e, ps=ps, G=G, bl=bl, oh=oh):
                            r = None
                            for kh in range(2):
                                r = e.matmul(ps[0:88, 0:TT], lhsT=G[:, bl, kh, oh * 88:(oh + 1) * 88], rhs=xcb[:, 2 * bl + kh, :], start=(kh == 0), stop=(kh == 1))
                            return r
                        P.op("pe", mmg, reads=[rW, rxcb[2 * bl], rxcb[2 * bl + 1]], writes=[rps])
                        dst_ap = tA[:] if bcol == 0 else tI[:, ci4, :]
                        P.op("act", lambda e, ps=ps, dst_ap=dst_ap, bcol=bcol, cc=cc: e.activation(out=dst_ap, in_=ps[0:88, 0:TT], func=AF.Tanh, bias=hb2[:, cc, bcol:bcol + 1], scale=0.5),
                             reads=[rps, rW], writes=[rdst_])
                    P.op("act", lambda e, cc=cc, ci4=ci4: e.activation(out=aB[:, ci4, :], in_=tA[:], func=AF.Exp, scale=nsph[:, cc:cc + 1], bias=nsph[:, cc:cc + 1]),
                         reads=[rtA, rW], writes=[raB[ci4]])
                    P.op("act", lambda e, ci4=ci4: e.activation(out=sB[:, ci4, :], in_=aB[:, ci4, :], func=AF.Square), reads=[raB[ci4]], writes=[rsB])
                P.op("act", lambda e: e.activation(out=sB[:].rearrange("p c t -> p (c t)"), in_=sB[:].rearrange("p c t -> p (c t)"), func=AF.Sqrt, bias=0.25, scale=-0.25),
                     reads=[rsB], writes=[rsB])
                for cc in range(g4 * 4, g4 * 4 + 4):
                    ci4 = cc % 4
                    pb = cc % 2
                    B = {r: tb[r][pb] for r in roles}
                    R = {r: tr[r][pb] for r in roles}
                    P.op("dve", lambda e, cc=cc, ci4=ci4, B=B: e.scalar_tensor_tensor(out=B["bx"][:], in0=tI[:, ci4, :], scalar=1.0, in1=xc[:, cc, :], op0=ALU.add, op1=ALU.mult),
                         reads=[rtI[ci4], rxc[cc]], writes=[R["bx"]])
                    P.op("dve", lambda e, ci4=ci4, B=B: e.tensor_tensor(out=B["bx"][:], in0=B["bx"][:], in1=sB[:, ci4, :], op=ALU.mult), reads=[R["bx"], rsB], writes=[R["bx"]])
                    P.op("dve", lambda e, cc=cc, ci4=ci4, B=B: e.tensor_tensor_scan(out=B["hs"][:], data0=aB[:, ci4, :], data1=B["bx"][:], initial=hst[:, cc:cc + 1],
                                                                                     op0=ALU.mult, op1=ALU.add), reads=[raB[ci4], R["bx"], rhst[cc]], writes=[R["hs"]])
                    P.op("dve", lambda e, cc=cc, B=B: e.tensor_copy(out=hst[:, cc:cc + 1], in_=B["hs"][:, TT - 1:TT]), reads=[R["hs"]], writes=[rhst[cc]])
                    ps, rps = next_ps(c)

                    def mmgb(e, ps=ps, cc=cc, hb=hb):
                        r = None
                        for kc in range(KD):
                            r = e.matmul(ps[0:88, 0:TT], lhsT=Wg[:, kc, cc * 88:(cc + 1) * 88], rhs=hb[:, kc, :], start=(kc == 0), stop=(kc == KD - 1))
                        return r
                    P.op("pe", mmgb, reads=[rW, rhb], writes=[rps])
                    P.op("act", lambda e, ps=ps, B=B: e.activation(out=B["gl"][:], in_=ps[0:88, 0:TT], func=AF.Square), reads=[rps], writes=[R["gl"]])
                    P.op("dve", lambda e, B=B: e.tensor_scalar(out=B["gl"][:], in0=B["gl"][:], scalar1=0.044715, scalar2=1.0, op0=ALU.mult, op1=ALU.add), reads=[R["gl"]], writes=[R["gl"]])
                    P.op("dve", lambda e, ps=ps, B=B: e.tensor_tensor(out=B["gl"][:], in0=ps[0:88, 0:TT], in1=B["gl"][:], op=ALU.mult), reads=[rps, R["gl"]], writes=[R["gl"]])
                    P.op("act", lambda e, B=B: e.activation(out=B["gl"][:], in_=B["gl"][:], func=AF.Tanh, scale=0.7978845608028654), reads=[R["gl"]], writes=[R["gl"]])
                    P.op("dve", lambda e, ps=ps, B=B: e.scalar_tensor_tensor(out=B["gl"][:], in0=B["gl"][:], scalar=1.0, in1=ps[0:88, 0:TT], op0=ALU.add, op1=ALU.mult),
                         reads=[rps, R["gl"]], writes=[R["gl"]])
                    P.op("dve", lambda e, B=B, cc=cc: e.scalar_tensor_tensor(out=ytl[:, cc, :], in0=B["hs"][:], scalar=0.5, in1=B["gl"][:], op0=ALU.mult, op1=ALU.mult),
                         reads=[R["hs"], R["gl"]], writes=[ryt[cc]])
                    P.dma("sp", ych[pb], lambda e, cc=cc, t=t: e.dma_start(out=y_dst(t, cc * 88, (cc + 1) * 88), in_=ytl[:, cc, :]), reads=[ryt[cc]],
                          writes=list(ry_d(t)) if callable(ry_d) else [])
                    if c.progress is not None:
                        c.progress([ryt[cc]])
            if tile_done is not None:
                tile_done(t, ryt[7])
        P.barrier()


def _pass_bufs(c, es2, T, tag):
    nc = c.nc
    sb2 = lambda name, shape, dt=F32: es2.enter_context(nc.sbuf_tensor("sp_%s_%s" % (name, tag), shape, dt))
    c.hn = sb2("hn", [128, KD, T], BF16); c.rhn = Res("hn")
    c.big = sb2("big", [128, 48, T], BF16); c.rbig = Res("big")
    c.qh = sb2("qh", [128, 4, T], BF16); c.rqh = Res("qh")
    c.ex = sb2("ex", [128, 2, T], BF16); c.rex = Res("ex")
    c.rden = sb2("rden", [128, T], F32); c.rrden = Res("rden")
    t = Tmp()
    t.sq = [sb2("sq%d" % i, [128, T], F32) for i in range(2)]
    t.rsq = [Res("sq%d" % i) for i in range(2)]
    t.rstd = sb2("rstd", [128, T], F32)
    t.rrstd = Res("rstd")
    c.tmp = t


def build_tok_program(Sc, T, kind, first, final, do_main=True):
    nc = bass.Bass("TRN2", target_bir_lowering=False)
    dt = lambda name, shape, dtype, kind_: nc.dram_tensor(name, shape, dtype, kind=kind_).ap()
    hin = dt("hin", [D, Sc], F32, "ExternalInput")
    vecs = dt("vecs", [128, 64], F32, "ExternalInput")
    nyc, rpc_y = (32, 88) if kind == "lru" else (32, 128)
    if do_main:
        y = dt("y", [nyc * rpc_y, Sc], BF16, "ExternalInput")
        mem = dt("mem", [MEM, D], F32, "ExternalInput")
        w_xq = dt("w_xq", [D, D], F32, "ExternalInput")
        w_kv = dt("w_kv", [D, 2 * D], F32, "ExternalInput")
        w_out = dt("w_out", [nyc * rpc_y + D, D], F32, "ExternalInput")
        w_fi = dt("w_fi", [D, 2 * D_FF], F32, "ExternalInput")
        w_fo = dt("w_fo", [D_FF, D], F32, "ExternalInput")
    if final:
        outd = dt("out", [D, Sc], F32, "ExternalOutput")
    else:
        hout = dt("hout", [D, Sc], F32, "ExternalOutput")
        hnn = dt("hnn", [D, Sc], BF16, "ExternalOutput")
    with ExitStack() as es:
        P = Prog(nc, es)
        c = make_ctx(nc, es, P)
        c.vecs = c.sb("vecs", [128, 64], F32)
        c.r_vecs = Res("vecs")
        c.h = c.sb("h", [128, KD, Sc], F32)
        c.rh = Res("h")
        c.KT = c.sb("KT", [128, KD, MEM], BF16); c.rKT = Res("KT")
        c.V = c.sb("V", [128, 2, D], BF16); c.rV = Res("V")
        c.ych = P.chan()
        c.och = P.chan()
        P.dma("sp", P.chan(), lambda e: e.dma_start(out=c.vecs[:], in_=vecs), writes=[c.r_vecs])
        P.dma("sp", P.chan(), lambda e: e.dma_start(out=c.h[:], in_=hin.rearrange("(kc p) t -> p kc t", p=128)), writes=[c.rh])
        lay = dict(g_mix=0, g_mem=16, g_ffn=32, g_next=48, nyc=nyc, rpc_y=rpc_y)
        if do_main:
            lay.update(y_ap=lambda tsl, e: y[:, tsl].rearrange("(kc p) t -> p kc t", p=rpc_y))
            lay.update(prep_layer_weights(c, w_xq, w_kv, w_out, w_fi, w_fo, nyc, rpc_y))
            emit_kv(c, lay, mem, lay["p_kv"], lay["g_mem"])
        with ExitStack() as es2:
            _pass_bufs(c, es2, T, "p")
            for p0 in range(0, Sc, T):
                tsl = slice(p0, p0 + T)
                emit_tok_pass(c, lay, tsl, T,
                              hn_dst=None if final else (lambda tsl: P.dma("sp", c.och, lambda e: e.dma_start(
                                  out=hnn[:, tsl].rearrange("(kc p) t -> p kc t", p=128), in_=c.hn[:, :, 0:T]), reads=[c.rhn])),
                              out_dst=(lambda tsl: outd[:, tsl].rearrange("(kc p) t -> p kc t", p=128)) if final else None,
                              do_main=do_main)
            if not final:
                P.dma("sp", c.och, lambda e: e.dma_start(out=hout.rearrange("(kc p) t -> p kc t", p=128), in_=c.h[:]), reads=[c.rh])
            P.barrier()
        P.emit()
    return nc


def build_lru_program(S, TT):
    nc = bass.Bass("TRN2", target_bir_lowering=False)
    dt = lambda name, shape, dtype, kind_: nc.dram_tensor(name, shape, dtype, kind=kind_).ap()
    hn = dt("hn", [D, S], BF16, "ExternalInput")
    wx = dt("wx", [D, 704], F32, "ExternalInput")
    wg = dt("wg", [D, 704], F32, "ExternalInput")
    ga = dt("ga", [4, 176, 176], F32, "ExternalInput")
    gx = dt("gx", [4, 176, 176], F32, "ExternalInput")
    lvec = dt("lvec", [88, 8, 8], F32, "ExternalInput")
    y = dt("y", [704, S], BF16, "ExternalOutput")
    with ExitStack() as es:
        P = Prog(nc, es)
        c = make_ctx(nc, es, P, nslots=1)
        emit_lru(c, S, TT, lambda t: hn[:, t * TT:(t + 1) * TT].rearrange("(kc p) t -> p kc t", p=128), wx, wg, ga, gx, lvec,
                 lambda t, r0, r1: y[r0:r1, t * TT:(t + 1) * TT])
        P.emit()
    return nc


def emit_gdn(c, S, TT, hn_src, p_w, cvec_ap, gvec_ap, pvec_ap, y_dst, tile_done=None, rhn_d=(), ry_d=()):
    P, nc = c.P, c.nc
    NT = S // TT
    NCH = TT // 64
    C = 64
    H = 8
    with ExitStack() as es2:
        cnt = [0]
        u = _uid()

        def sb2(name, shape, dt=F32):
            cnt[0] += 1
            return es2.enter_context(nc.sbuf_tensor("sg%d_%s_%d" % (u, name, cnt[0]), shape, dt))
        rK = Res("gconst")
        sel = sb2("sel", [8, H, 128])
        nsel = sb2("nsel", [8, H, 128])
        mB = sb2("mB", [C, H, C])
        mBT = sb2("mBT", [C, H, C])
        mAtt = sb2("mAtt", [C, H, C])
        eye8 = sb2("eye8", [C, H, C])
        lastsel = sb2("lastsel", [C, C])
        cmask = sb2("cmask", [8, TT])

        gk = c.gdn_consts
        for nm, tile_ in (("sel", sel), ("nsel", nsel), ("mB", mB), ("mBT", mBT), ("mAtt", mAtt), ("eye8", eye8)):
            P.dma("sp", P.chan("gk_" + nm), lambda e, nm=nm, tile_=tile_: e.dma_start(out=tile_[:].rearrange("p h c -> p (h c)"), in_=gk[nm]), reads=[c.r_gk], writes=[rK])
        P.dma("sp", P.chan("gk_lastsel"), lambda e: e.dma_start(out=lastsel[:], in_=gk["lastsel"]), reads=[c.r_gk], writes=[rK])
        P.dma("sp", P.chan("gk_cmask"), lambda e: e.dma_start(out=cmask[:], in_=gk["cmask"]), reads=[c.r_gk], writes=[rK])
        cv = sb2("cv", [128, 16, 4])
        gn = sb2("gn", [128, 1])
        pv = sb2("pv", [8, 2])
        nalog = sb2("nalog", [8, 1])
        rcv = Res("cv")
        P.dma("sp", P.chan("gcv"), lambda e: e.dma_start(out=cv[:], in_=cvec_ap), writes=[rcv])
        rgn = Res("gn")
        P.dma("sp", P.chan("ggn"), lambda e: e.dma_start(out=gn[:], in_=gvec_ap), writes=[rgn])
        rpv = Res("pv")
        P.dma("sp", P.chan("gpv"), lambda e: e.dma_start(out=pv[:], in_=pvec_ap), writes=[rpv])
        P.op("act", lambda e: e.activation(out=nalog[:], in_=pv[:, 0:1], func=AF.Exp), reads=[rpv], writes=[rpv])
        P.op("dve", lambda e: e.tensor_scalar(out=nalog[:], in0=nalog[:], scalar1=-1.0, scalar2=None, op0=ALU.mult), reads=[rpv], writes=[rpv])

        hnb = sb2("hnb", [128, KD, TT], BF16); rhnb = Res("hnb"); hch = P.chan("hch0")
        carry = sb2("carry", [128, 16, 3]); rcar = [Res("car%d" % i) for i in range(16)]
        P.op("dve", lambda e: e.memset(carry[:], 0.0), writes=rcar)
        pad = [sb2("pad", [128, TT + 3]) for _ in range(2)]; rpad = [Res("pad0"), Res("pad1")]
        xcv = [sb2("xcv", [128, TT]) for _ in range(2)]; rxcv = [Res("xcv0"), Res("xcv1")]
        sqb = [sb2("sqb", [128, TT]) for _ in range(2)]; rsqb = [Res("sqb0"), Res("sqb1")]
        rsb = [sb2("rsb", [128, TT]) for _ in range(2)]; rrsb = [Res("rsb0"), Res("rsb1")]
        qT = sb2("qT", [128, 4, TT], BF16); rqT = [Res("qT%d" % i) for i in range(4)]
        kT = sb2("kT", [128, 4, TT], BF16); rkT = [Res("kT%d" % i) for i in range(4)]
        vT = sb2("vT", [128, H, TT], BF16); rvT = [Res("vT%d" % i) for i in range(H)]
        o_t = sb2("o_t", [128, H, TT]); ro_t = Res("o_t")
        betaT = sb2("betaT", [8, TT]); rbetaT = Res("betaT")
        lbT = sb2("lbT", [8, TT]); rlbT = Res("lbT")
        spT = sb2("spT", [8, TT]); rspT = Res("spT")
        gcT = sb2("gcT", [8, TT]); rgcT = Res("gcT")
        g2T = sb2("g2T", [8, TT]); rg2T = Res("g2T")
        Sst = sb2("Sst", [128, H, 128]); rS = Res("S")
        P.op("dve", lambda e: e.memset(Sst[:], 0.0), writes=[rS])
        colt = sb2("colt", [C, 16]); rcolt = Res("colt")
        cbg = sb2("cbg", [C, H]); rcbg = Res("cbg")
        ksc = sb2("ksc", [C, H]); rksc = Res("ksc")
        E = sb2("E", [128, H, C]); rE = Res("E")
        qd = sb2("qd", [128, H, C]); rqd = Res("qd")
        tt_ = [sb2("tt", [C, H, C]) for _ in range(3)]; rtt = [Res("tt%d" % i) for i in range(3)]
        Pm = [sb2("Pm", [C, H, C], BF16) for _ in range(2)]; rPm = [Res("Pm0"), Res("Pm1")]
        PTm = [sb2("PTm", [C, H, C], BF16) for _ in range(2)]; rPTm = [Res("PTm0"), Res("PTm1")]
        Um = [sb2("Um", [C, H, C], BF16) for _ in range(2)]; rUm = [Res("Um0"), Res("Um1")]
        attT = sb2("attT", [C, H, C], BF16); rattT = Res("attT")
        kbg = sb2("kbg", [C, H, 128], BF16); rkbg = Res("kbg")
        kst = sb2("kst", [C, H, 128], BF16); rkst = Res("kst")
        vb = sb2("vb", [C, H, 128], BF16); rvb = Res("vb")
        wv = sb2("wv", [C, H, 128]); rwv = Res("wv")
        kcT = sb2("kcT", [128, H, C]); rkcT = Res("kcT")
        vnew = sb2("vnew", [C, H, 128], BF16); rvnew = Res("vnew")
        identb = sb2("identb", [128, 128], BF16)
        P.op("act", lambda e: e.copy(out=identb[:], in_=c.ident[:]), reads=[c.r_const], writes=[rK])
        ytg = sb2("yt", [128, H, TT], BF16); ryt = [Res("yt%d" % i) for i in range(H)]
        ych = [P.chan("ych0"), P.chan("ych1")]
        ps, psr = c.ps, c.psr
        pn = [0]

        def bank():
            i = pn[0] % 8
            pn[0] += 1
            return ps[i], psr[i]

        def inproj(col0, M, TTn):
            slot, rw = wfetch(c, p_w, col0 // 128)
            pb, rpb = bank()

            def mm(e, slot=slot, pb=pb):
                r = None
                for kc in range(KD):
                    r = e.matmul(pb[0:M, 0:TTn], lhsT=slot[:, kc, 0:M], rhs=hnb[:, kc, :], start=(kc == 0), stop=(kc == KD - 1))
                return r
            P.op("pe", mm, reads=[rw, rhnb], writes=[rpb])
            return pb, rpb

        for t in range(NT):
            tsl = slice(t * TT, (t + 1) * TT)
            P.dma("sp", hch, lambda e, t=t: e.dma_start(out=hnb[:], in_=hn_src(t)), reads=list(rhn_d), writes=[rhnb])
            slot, rw = wfetch(c, p_w, 24)
            pbb, rpbb = bank()
            pba, rpba = bank()

            def mmba(e, slot=slot, pbb=pbb, pba=pba):
                r = None
                for kc in range(KD):
                    r = e.matmul(pbb[0:8, 0:TT], lhsT=slot[:, kc, 0:8], rhs=hnb[:, kc, :], start=(kc == 0), stop=(kc == KD - 1))
                for kc in range(KD):
                    r = e.matmul(pba[0:8, 0:TT], lhsT=slot[:, kc, 8:16], rhs=hnb[:, kc, :], start=(kc == 0), stop=(kc == KD - 1))
                return r
            P.op("pe", mmba, reads=[rw, rhnb], writes=[rpbb, rpba])
            P.op("act", lambda e, pbb=pbb: e.activation(out=betaT[:], in_=pbb[0:8, 0:TT], func=AF.Sigmoid), reads=[rpbb], writes=[rbetaT])
            P.op("act", lambda e, pbb=pbb: e.activation(out=lbT[:], in_=pbb[0:8, 0:TT], func=AF.Exp, scale=-1.0), reads=[rpbb], writes=[rlbT])
            P.op("act", lambda e: e.activation(out=lbT[:], in_=lbT[:], func=AF.Ln, bias=1.0), reads=[rlbT], writes=[rlbT])
            P.op("act", lambda e, pba=pba: e.activation(out=spT[:], in_=pba[0:8, 0:TT], func=AF.Exp, bias=pv[:, 1:2]), reads=[rpba, rpv], writes=[rspT])
            P.op("act", lambda e: e.activation(out=spT[:], in_=spT[:], func=AF.Ln, bias=1.0), reads=[rspT], writes=[rspT])
            P.op("dve", lambda e: e.tensor_scalar(out=spT[:], in0=spT[:], scalar1=nalog[:, 0:1], scalar2=None, op0=ALU.mult), reads=[rspT, rpv], writes=[rspT])
            P.op("dve", lambda e: e.tensor_tensor_scan(out=gcT[:], data0=cmask[:], data1=spT[:], initial=0.0, op0=ALU.mult, op1=ALU.add),
                 reads=[rspT, rK], writes=[rgcT])
            P.op("dve", lambda e: e.tensor_tensor(out=g2T[:], in0=gcT[:], in1=lbT[:], op=ALU.subtract), reads=[rgcT, rlbT], writes=[rg2T])
            for f in range(16):
                pb, rpb = inproj(f * 128, 128, TT)
                pd, rpd = pad[f % 2], rpad[f % 2]
                xc_, rxc_ = xcv[f % 2], rxcv[f % 2]
                P.op("act", lambda e, pb=pb, pd=pd: e.copy(out=pd[:, 3:3 + TT], in_=pb[:, 0:TT]), reads=[rpb], writes=[rpd])
                P.op("dve", lambda e, pd=pd, f=f: e.tensor_copy(out=pd[:, 0:3], in_=carry[:, f, :]), reads=[rcar[f], rpd], writes=[rpd])
                P.op("dve", lambda e, pd=pd, xc_=xc_, f=f: e.tensor_scalar(out=xc_[:], in0=pd[:, 0:TT], scalar1=cv[:, f, 0:1], scalar2=None, op0=ALU.mult),
                     reads=[rpd, rcv], writes=[rxc_])
                for k in range(1, 4):
                    P.op("dve", lambda e, pd=pd, xc_=xc_, f=f, k=k: e.scalar_tensor_tensor(out=xc_[:], in0=pd[:, k:k + TT], scalar=cv[:, f, k:k + 1], in1=xc_[:],
                                                                                           op0=ALU.mult, op1=ALU.add), reads=[rpd, rxc_, rcv], writes=[rxc_])
                P.op("dve", lambda e, pd=pd, f=f: e.tensor_copy(out=carry[:, f, :], in_=pd[:, TT:TT + 3]), reads=[rpd], writes=[rcar[f]])
                if f >= 8:
                    hv = f - 8
                    P.op("act", lambda e, xc_=xc_, hv=hv: e.activation(out=vT[:, hv, :], in_=xc_[:], func=AF.Silu), reads=[rxc_], writes=[rvT[hv]])
                else:
                    dstT, rdst, hq = (qT, rqT, f) if f < 4 else (kT, rkT, f - 4)
                    qscale = float(128 ** -0.5) if f < 4 else 1.0
                    P.op("act", lambda e, xc_=xc_: e.activation(out=xc_[:], in_=xc_[:], func=AF.Silu), reads=[rxc_], writes=[rxc_])
                    sq_, rsq_ = sqb[f % 2], rsqb[f % 2]
                    rs_, rrs_ = rsb[f % 2], rrsb[f % 2]
                    P.op("act", lambda e, xc_=xc_, sq_=sq_: e.activation(out=sq_[:], in_=xc_[:], func=AF.Square), reads=[rxc_], writes=[rsq_])
                    pn_, rpn_ = bank()
                    P.op("pe", lambda e, sq_=sq_, pn_=pn_: e.matmul(pn_[:, 0:TT], lhsT=c.ones_f[:], rhs=sq_[:], start=True, stop=True), reads=[rsq_, c.r_const], writes=[rpn_])
                    P.op("act", lambda e, rs_=rs_, pn_=pn_: e.activation(out=rs_[:], in_=pn_[:, 0:TT], func=AF.Sqrt, bias=EPS), reads=[rpn_], writes=[rrs_])
                    P.op("dve", lambda e, rs_=rs_: e.reciprocal(out=rs_[:], in_=rs_[:]), reads=[rrs_], writes=[rrs_])
                    P.op("dve", lambda e, xc_=xc_, rs_=rs_, dstT=dstT, hq=hq, qscale=qscale: e.scalar_tensor_tensor(
                        out=dstT[:, hq, :], in0=xc_[:], scalar=qscale, in1=rs_[:], op0=ALU.mult, op1=ALU.mult), reads=[rxc_, rrs_], writes=[rdst[hq]])
            for ci in range(NCH):
                cs = slice(ci * C, (ci + 1) * C)
                pc, rpc = bank()

                def trc(e, pc=pc, cs=cs):
                    e.transpose(pc[0:C, 0:8], gcT[0:8, cs], c.ident[0:8, 0:8])
                    return e.transpose(pc[0:C, 8:16], betaT[0:8, cs], c.ident[0:8, 0:8])
                P.op("pe", trc, reads=[rgcT, rbetaT, c.r_const], writes=[rpc])
                P.op("act", lambda e, pc=pc: e.copy(out=colt[:], in_=pc[0:C, 0:16]), reads=[rpc], writes=[rcolt])
                pl, rpl = bank()
                P.op("pe", lambda e, pl=pl: e.matmul(pl[0:C, 0:8], lhsT=lastsel[:], rhs=colt[:, 0:8], start=True, stop=True), reads=[rcolt, rK], writes=[rpl])
                P.op("act", lambda e: e.activation(out=cbg[:], in_=colt[:, 0:8], func=AF.Exp), reads=[rcolt], writes=[rcbg])
                P.op("dve", lambda e: e.tensor_tensor(out=cbg[:], in0=cbg[:], in1=colt[:, 8:16], op=ALU.mult), reads=[rcbg, rcolt], writes=[rcbg])
                P.op("dve", lambda e, pl=pl: e.tensor_tensor(out=ksc[:], in0=pl[0:C, 0:8], in1=colt[:, 0:8], op=ALU.subtract), reads=[rpl, rcolt], writes=[rksc])
                P.op("act", lambda e: e.activation(out=ksc[:], in_=ksc[:], func=AF.Exp), reads=[rksc], writes=[rksc])
                pe_, rpe_ = bank()

                def mmE(e, pe_=pe_, cs=cs):
                    r = None
                    for h in range(H):
                        r = e.matmul(pe_[:, h * C:(h + 1) * C], lhsT=sel[:, h, :], rhs=gcT[0:8, cs], start=True, stop=True)
                    return r
                P.op("pe", mmE, reads=[rgcT, rK], writes=[rpe_])
                P.op("act", lambda e, pe_=pe_: e.activation(out=E[:].rearrange("p h c -> p (h c)"), in_=pe_[:, 0:H * C], func=AF.Exp), reads=[rpe_], writes=[rE])
                for rep in range(2):
                    P.op("dve", lambda e, rep=rep, cs=cs: e.tensor_tensor(
                        out=qd[:].rearrange("p (q r) c -> p q r c", r=2)[:, :, rep, :], in0=qT[:, :, cs],
                        in1=E[:].rearrange("p (q r) c -> p q r c", r=2)[:, :, rep, :], op=ALU.mult), reads=rqT + [rE], writes=[rqd])
                pkk, rpkk = bank()
                pqk, rpqk = bank()

                def mmkk(e, pkk=pkk, pqk=pqk, cs=cs):
                    r = None
                    for h in range(H):
                        r = e.matmul(pkk[0:C, h * C:(h + 1) * C], lhsT=kT[:, h // 2, cs], rhs=kT[:, h // 2, cs], start=True, stop=True)
                    for h in range(H):
                        r = e.matmul(pqk[0:C, h * C:(h + 1) * C], lhsT=kT[:, h // 2, cs], rhs=qT[:, h // 2, cs], start=True, stop=True)
                    return r
                P.op("pe", mmkk, reads=rkT + rqT, writes=[rpkk, rpqk])
                pd1, rpd1 = bank()
                pd2, rpd2 = bank()
                pd3, rpd3 = bank()

                def mmd(e, pd1=pd1, pd2=pd2, pd3=pd3, cs=cs):
                    r = None
                    for h in range(H):
                        hs = slice(h * C, (h + 1) * C)
                        e.matmul(pd1[0:C, hs], lhsT=g2T[0:8, cs], rhs=sel[:, h, 0:C], start=True, stop=False)
                        e.matmul(pd1[0:C, hs], lhsT=nsel[:, h, 0:C], rhs=gcT[0:8, cs], start=False, stop=True)
                        e.matmul(pd2[0:C, hs], lhsT=sel[:, h, 0:C], rhs=g2T[0:8, cs], start=True, stop=False)
                        e.matmul(pd2[0:C, hs], lhsT=gcT[0:8, cs], rhs=nsel[:, h, 0:C], start=False, stop=True)
                        e.matmul(pd3[0:C, hs], lhsT=sel[:, h, 0:C], rhs=gcT[0:8, cs], start=True, stop=False)
                        r = e.matmul(pd3[0:C, hs], lhsT=gcT[0:8, cs], rhs=nsel[:, h, 0:C], start=False, stop=True)
                    return r
                P.op("pe", mmd, reads=[rg2T, rgcT, rK], writes=[rpd1, rpd2, rpd3])
                p0, rp0 = Pm[0], rPm[0]
                pt0, rpt0 = PTm[0], rPTm[0]
                fl = lambda a: a[:].rearrange("p h c -> p (h c)")
                for (pdx, rpdx, tmp_, rtmp_, msk, pmat, rpmat, dst, rdst_) in (
                        (pd1, rpd1, tt_[0], rtt[0], mB, pkk, rpkk, p0, rp0),
                        (pd2, rpd2, tt_[1], rtt[1], mBT, pkk, rpkk, pt0, rpt0),
                        (pd3, rpd3, tt_[2], rtt[2], mAtt, pqk, rpqk, attT, rattT)):
                    P.op("dve", lambda e, pdx=pdx, tmp_=tmp_: e.tensor_scalar(out=fl(tmp_), in0=pdx[0:C, 0:H * C], scalar1=0.0, scalar2=None, op0=ALU.min),
                         reads=[rpdx], writes=[rtmp_])
                    P.op("act", lambda e, tmp_=tmp_: e.activation(out=fl(tmp_), in_=fl(tmp_), func=AF.Exp), reads=[rtmp_], writes=[rtmp_])
                    P.op("dve", lambda e, tmp_=tmp_, msk=msk: e.tensor_tensor(out=fl(tmp_), in0=fl(tmp_), in1=fl(msk), op=ALU.mult), reads=[rtmp_, rK], writes=[rtmp_])
                    P.op("dve", lambda e, tmp_=tmp_, pmat=pmat, dst=dst: e.tensor_tensor(out=fl(dst), in0=pmat[0:C, 0:H * C], in1=fl(tmp_), op=ALU.mult),
                         reads=[rpmat, rtmp_], writes=[rdst_])
                P.op("dve", lambda e: e.tensor_tensor(out=fl(Um[0]), in0=fl(pt0), in1=fl(eye8), op=ALU.add), reads=[rpt0, rK], writes=[rUm[0]])
                cur = 0
                for r in range(0, 6):
                    nxt = 1 - cur
                    Pc, rPc, PTc, rPTc, Uc, rUc = Pm[cur], rPm[cur], PTm[cur], rPTm[cur], Um[cur], rUm[cur]
                    Pn, rPn, PTn, rPTn, Un, rUn = Pm[nxt], rPm[nxt], PTm[nxt], rPTm[nxt], Um[nxt], rUm[nxt]
                    if r >= 1:
                        pu, rpu = bank()

                        def mmu(e, pu=pu, Pc=Pc, Uc=Uc):
                            rr = None
                            for h in range(H):
                                rr = e.matmul(pu[0:C, h * C:(h + 1) * C], lhsT=Pc[:, h, :], rhs=Uc[:, h, :], start=True, stop=True)
                            return rr
                        P.op("pe", mmu, reads=[rPc, rUc], writes=[rpu])
                    if r < 5:
                        pp, rpp = bank()

                        def mmp(e, pp=pp, Pc=Pc, PTc=PTc):
                            rr = None
                            for h in range(H):
                                rr = e.matmul(pp[0:C, h * C:(h + 1) * C], lhsT=PTc[:, h, :], rhs=Pc[:, h, :], start=True, stop=True)
                            return rr
                        P.op("pe", mmp, reads=[rPc, rPTc], writes=[rpp])
                        if r < 4:
                            ppt, rppt = bank()

                            def mmpt(e, ppt=ppt, Pc=Pc, PTc=PTc):
                                rr = None
                                for h in range(H):
                                    rr = e.matmul(ppt[0:C, h * C:(h + 1) * C], lhsT=Pc[:, h, :], rhs=PTc[:, h, :], start=True, stop=True)
                                return rr
                            P.op("pe", mmpt, reads=[rPc, rPTc], writes=[rppt])
                    if r >= 1:
                        P.op("dve", lambda e, pu=pu, Uc=Uc, Un=Un: e.tensor_tensor(out=fl(Un), in0=pu[0:C, 0:H * C], in1=fl(Uc), op=ALU.add),
                             reads=[rpu, rUc], writes=[rUn])
                    else:
                        P.op("act", lambda e, Uc=Uc, Un=Un: e.copy(out=fl(Un), in_=fl(Uc)), reads=[rUc], writes=[rUn])
                    if r < 5:
                        P.op("act", lambda e, pp=pp, Pn=Pn: e.copy(out=fl(Pn), in_=pp[0:C, 0:H * C]), reads=[rpp], writes=[rPn])
                        if r < 4:
                            P.op("act", lambda e, ppt=ppt, PTn=PTn: e.copy(out=fl(PTn), in_=ppt[0:C, 0:H * C]), reads=[rppt], writes=[rPTn])
                    cur = nxt
                U, rU = Um[cur], rUm[cur]
                pkt, rpkt = bank()

                def trk(e, pkt=pkt, cs=cs):
                    rr = None
                    for hq in range(4):
                        rr = e.matmul(pkt[0:C, hq * 128:(hq + 1) * 128], lhsT=kT[:, hq, cs], rhs=identb[:], start=True, stop=True)
                    return rr
                P.op("pe", trk, reads=rkT + [rK], writes=[rpkt])
                for rep in range(2):
                    kv_ = lambda a: a[:].rearrange("p (q r) d -> p q r d", r=2)[:, :, rep, :]
                    P.op("dve", lambda e, pkt=pkt, rep=rep: e.tensor_tensor(
                        out=kbg[:].rearrange("p (q r) d -> p q r d", r=2)[:, :, rep, :], in0=pkt[0:C, 0:512].rearrange("p (q d) -> p q d", d=128),
                        in1=cbg[:].rearrange("p (q r) -> p q r", r=2)[:, :, rep].unsqueeze(2).to_broadcast([C, 4, 128]), op=ALU.mult),
                        reads=[rpkt, rcbg], writes=[rkbg])
                    P.op("dve", lambda e, pkt=pkt, rep=rep: e.tensor_tensor(
                        out=kst[:].rearrange("p (q r) d -> p q r d", r=2)[:, :, rep, :], in0=pkt[0:C, 0:512].rearrange("p (q d) -> p q d", d=128),
                        in1=ksc[:].rearrange("p (q r) -> p q r", r=2)[:, :, rep].unsqueeze(2).to_broadcast([C, 4, 128]), op=ALU.mult),
                        reads=[rpkt, rksc], writes=[rkst])
                for half in range(2):
                    pvt, rpvt = bank()

                    def trv(e, pvt=pvt, cs=cs, half=half):
                        rr = None
                        for hh in range(4):
                            rr = e.matmul(pvt[0:C, hh * 128:(hh + 1) * 128], lhsT=vT[:, half * 4 + hh, cs], rhs=identb[:], start=True, stop=True)
                        return rr
                    P.op("pe", trv, reads=rvT + [rK], writes=[rpvt])
                    P.op("dve", lambda e, pvt=pvt, half=half: e.tensor_tensor(
                        out=vb[:, half * 4:(half + 1) * 4, :], in0=pvt[0:C, 0:512].rearrange("p (q d) -> p q d", d=128),
                        in1=colt[:, 8 + half * 4:8 + (half + 1) * 4].unsqueeze(2).to_broadcast([C, 4, 128]), op=ALU.mult),
                        reads=[rpvt, rcolt], writes=[rvb])
                for half in range(2):
                    pw, rpw = bank()

                    def mmw(e, pw=pw, half=half, U=U):
                        rr = None
                        for hh in range(4):
                            h = half * 4 + hh
                            rr = e.matmul(pw[0:C, hh * 128:(hh + 1) * 128], lhsT=U[:, h, :], rhs=vb[:, h, :], start=True, stop=True)
                        return rr
                    P.op("pe", mmw, reads=[rU, rvb], writes=[rpw])
                    P.op("act", lambda e, pw=pw, half=half: e.copy(out=wv[:, half * 4:(half + 1) * 4, :].rearrange("p q d -> p (q d)"), in_=pw[0:C, 0:512]),
                         reads=[rpw], writes=[rwv])
                pkc, rpkc = bank()

                def mmkc(e, pkc=pkc, U=U):
                    rr = None
                    for h in range(H):
                        rr = e.matmul(pkc[:, h * C:(h + 1) * C], lhsT=kbg[:, h, :], rhs=U[:, h, :], start=True, stop=True)
                    return rr
                P.op("pe", mmkc, reads=[rU, rkbg], writes=[rpkc])
                P.op("act", lambda e, pkc=pkc: e.copy(out=fl(kcT), in_=pkc[:, 0:H * C]), reads=[rpkc], writes=[rkcT])
                for half in range(2):
                    pv_, rpv_ = bank()

                    def mmv(e, pv_=pv_, half=half):
                        rr = None
                        for hh in range(4):
                            h = half * 4 + hh
                            rr = e.matmul(pv_[0:C, hh * 128:(hh + 1) * 128], lhsT=kcT[:, h, :], rhs=Sst[:, h, :], start=True, stop=True)
                        return rr
                    P.op("pe", mmv, reads=[rkcT, rS], writes=[rpv_])
                    P.op("dve", lambda e, pv_=pv_, half=half: e.tensor_tensor(
                        out=vnew[:, half * 4:(half + 1) * 4, :].rearrange("p q d -> p (q d)"),
                        in0=wv[:, half * 4:(half + 1) * 4, :].rearrange("p q d -> p (q d)"), in1=pv_[0:C, 0:512], op=ALU.subtract),
                        reads=[rpv_, rwv], writes=[rvnew])
                po, rpo = bank()

                def mmo(e, po=po):
                    rr = None
                    for h in range(H):
                        e.matmul(po[:, h * C:(h + 1) * C], lhsT=Sst[:, h, :], rhs=qd[:, h, :], start=True, stop=False)
                        rr = e.matmul(po[:, h * C:(h + 1) * C], lhsT=vnew[:, h, :], rhs=attT[:, h, :], start=False, stop=True)
                    return rr
                P.op("pe", mmo, reads=[rS, rqd, rvnew, rattT], writes=[rpo])
                P.op("act", lambda e, po=po, cs=cs: e.copy(out=o_t[:, :, cs], in_=po[:, 0:H * C].rearrange("p (h c) -> p h c", c=C)), reads=[rpo], writes=[ro_t])
                for half in range(2):
                    pss, rpss = bank()

                    def mms(e, pss=pss, half=half):
                        rr = None
                        for hh in range(4):
                            h = half * 4 + hh
                            rr = e.matmul(pss[:, hh * 128:(hh + 1) * 128], lhsT=kst[:, h, :], rhs=vnew[:, h, :], start=True, stop=True)
                        return rr
                    P.op("pe", mms, reads=[rkst, rvnew], writes=[rpss])
                    for hh in range(4):
                        h = half * 4 + hh
                        P.op("dve", lambda e, pss=pss, h=h, hh=hh: e.scalar_tensor_tensor(
                            out=Sst[:, h, :], in0=Sst[:, h, :], scalar=E[:, h, C - 1:C], in1=pss[:, hh * 128:(hh + 1) * 128], op0=ALU.mult, op1=ALU.add),
                            reads=[rS, rE, rpss], writes=[rS])
                if c.progress is not None and ci < NCH - 1:
                    c.progress([rS])
            for h in range(H):
                pz, rpz = inproj(2048 + h * 128, 128, TT)
                zs, rzs = xcv[h % 2], rxcv[h % 2]
                P.op("act", lambda e, pz=pz, zs=zs: e.activation(out=zs[:], in_=pz[:, 0:TT], func=AF.Silu), reads=[rpz], writes=[rzs])
                sq_, rsq_ = sqb[h % 2], rsqb[h % 2]
                rs_, rrs_ = rsb[h % 2], rrsb[h % 2]
                P.op("act", lambda e, sq_=sq_, h=h: e.activation(out=sq_[:], in_=o_t[:, h, :], func=AF.Square), reads=[ro_t], writes=[rsq_])
                pn_, rpn_ = bank()
                P.op("pe", lambda e, sq_=sq_, pn_=pn_: e.matmul(pn_[:, 0:TT], lhsT=c.ones_f[:], rhs=sq_[:], start=True, stop=True), reads=[rsq_, c.r_const], writes=[rpn_])
                P.op("act", lambda e, rs_=rs_, pn_=pn_: e.activation(out=rs_[:], in_=pn_[:, 0:TT], func=AF.Sqrt, bias=EPS, scale=1.0 / 128), reads=[rpn_], writes=[rrs_])
                P.op("dve", lambda e, rs_=rs_: e.reciprocal(out=rs_[:], in_=rs_[:]), reads=[rrs_], writes=[rrs_])
                P.op("dve", lambda e, rs_=rs_, h=h: e.scalar_tensor_tensor(out=rs_[:], in0=o_t[:, h, :], scalar=gn[:, 0:1], in1=rs_[:], op0=ALU.mult, op1=ALU.mult),
                     reads=[ro_t, rgn, rrs_], writes=[rrs_])
                P.op("dve", lambda e, rs_=rs_, zs=zs, h=h: e.tensor_tensor(out=ytg[:, h, :], in0=rs_[:], in1=zs[:], op=ALU.mult), reads=[rrs_, rzs], writes=[ryt[h]])
                P.dma("sp", ych[h % 2], lambda e, h=h, t=t: e.dma_start(out=y_dst(t, h * 128, (h + 1) * 128), in_=ytg[:, h, :]), reads=[ryt[h]],
                      writes=list(ry_d(t)) if callable(ry_d) else [])
            if tile_done is not None:
                tile_done(t, ryt[H - 1])
        P.barrier()


def build_gdn_consts(c, TT):
    P, nc = c.P, c.nc
    C, H = 64, 8
    gk = {}
    shapes = dict(sel=[8, H * 128], nsel=[8, H * 128], mB=[C, H * C], mBT=[C, H * C], mAtt=[C, H * C], eye8=[C, H * C], lastsel=[C, C], cmask=[8, TT])
    for nm, sh in shapes.items():
        gk[nm] = nc.dram_tensor("gk_%s" % nm, sh, F32).ap()
    c.gdn_consts = gk
    c.r_gk = Res("gk")
    with ExitStack() as es2:
        u = _uid()
        sb2 = lambda name, shape: es2.enter_context(nc.sbuf_tensor("sgk%d_%s" % (u, name), shape, F32))
        rK = Res("gkbuild")
        sel = sb2("sel", [8, H, 128]); nsel = sb2("nsel", [8, H, 128])
        mB = sb2("mB", [C, H, C]); mBT = sb2("mBT", [C, H, C]); mAtt = sb2("mAtt", [C, H, C]); eye8 = sb2("eye8", [C, H, C])
        lastsel = sb2("lastsel", [C, C]); cmask = sb2("cmask", [8, TT])

        def cst(fn):
            P.op("pool", fn, reads=[rK], writes=[rK])
        cst(lambda e: e.memset(sel[:], 0.0))
        cst(lambda e: e.affine_select(out=sel[:], in_=sel[:], pattern=[[-1, H], [0, 128]], compare_op=ALU.not_equal, fill=1.0, base=0, channel_multiplier=1))
        cst(lambda e: e.memset(nsel[:], 0.0))
        cst(lambda e: e.affine_select(out=nsel[:], in_=nsel[:], pattern=[[-1, H], [0, 128]], compare_op=ALU.not_equal, fill=-1.0, base=0, channel_multiplier=1))
        cst(lambda e: e.memset(mB[:], -1.0))
        cst(lambda e: e.affine_select(out=mB[:], in_=mB[:], pattern=[[0, H], [-1, C]], compare_op=ALU.is_gt, fill=0.0, base=0, channel_multiplier=1))
        cst(lambda e: e.memset(mBT[:], -1.0))
        cst(lambda e: e.affine_select(out=mBT[:], in_=mBT[:], pattern=[[0, H], [1, C]], compare_op=ALU.is_gt, fill=0.0, base=0, channel_multiplier=-1))
        cst(lambda e: e.memset(mAtt[:], 1.0))
        cst(lambda e: e.affine_select(out=mAtt[:], in_=mAtt[:], pattern=[[0, H], [1, C]], compare_op=ALU.is_ge, fill=0.0, base=0, channel_multiplier=-1))
        cst(lambda e: e.memset(eye8[:], 0.0))
        cst(lambda e: e.affine_select(out=eye8[:], in_=eye8[:], pattern=[[0, H], [-1, C]], compare_op=ALU.not_equal, fill=1.0, base=0, channel_multiplier=1))
        cst(lambda e: e.memset(lastsel[:], 0.0))
        cst(lambda e: e.affine_select(out=lastsel[:], in_=lastsel[:], pattern=[[0, C]], compare_op=ALU.not_equal, fill=1.0, base=-(C - 1), channel_multiplier=1))
        cst(lambda e: e.memset(cmask[:], 1.0))
        cst(lambda e: e.memset(cmask[:].rearrange("p (n c) -> p n c", c=C)[:, :, 0:1], 0.0))
        ch = P.chan("gkst")
        for nm, t in (("sel", sel), ("nsel", nsel), ("mB", mB), ("mBT", mBT), ("mAtt", mAtt), ("eye8", eye8)):
            P.dma("sp", ch, lambda e, nm=nm, t=t: e.dma_start(out=gk[nm], in_=t[:].rearrange("p h c -> p (h c)")), reads=[rK], writes=[c.r_gk])
        P.dma("sp", ch, lambda e: e.dma_start(out=gk["lastsel"], in_=lastsel[:]), reads=[rK], writes=[c.r_gk])
        P.dma("sp", ch, lambda e: e.dma_start(out=gk["cmask"], in_=cmask[:]), reads=[rK], writes=[c.r_gk])
        P.barrier()


def build_gdn_program(S, TT):
    nc = bass.Bass("TRN2", target_bir_lowering=False)
    dt = lambda name, shape, dtype, kind_: nc.dram_tensor(name, shape, dtype, kind=kind_).ap()
    hn = dt("hn", [D, S], BF16, "ExternalInput")
    w = dt("w", [D, 3088], F32, "ExternalInput")
    cvec = dt("cvec", [128, 16, 4], F32, "ExternalInput")
    gvec = dt("gvec", [128, 1], F32, "ExternalInput")
    pvec = dt("pvec", [8, 2], F32, "ExternalInput")
    y = dt("y", [1024, S], BF16, "ExternalOutput")
    with ExitStack() as es:
        P = Prog(nc, es)
        c = make_ctx(nc, es, P, nslots=3)
        build_gdn_consts(c, TT)
        emit_gdn(c, S, TT, lambda t: hn[:, t * TT:(t + 1) * TT].rearrange("(kc p) t -> p kc t", p=128), prep_gdn_weights(c, w), cvec, gvec, pvec,
                 lambda t, r0, r1: y[r0:r1, t * TT:(t + 1) * TT])
        P.emit()
    return nc


def build_fused_program(S=4096, depth=4, NB=2, debug=False):
    GROUPS = [[4 * b + q for q in range(4)] for b in range(NB)]
    Sc = S // 4
    T = 512
    TT = 256
    NQ = Sc // 256
    NYK = S // 512
    nc = bass.Bass("TRN2", target_bir_lowering=False)
    dt = lambda name, shape, dtype, kind_: nc.dram_tensor(name, shape, dtype, kind=kind_).ap()
    xin = dt("x", [D, Sc], F32, "ExternalInput")
    mem = dt("mem", [MEM, D], F32, "ExternalInput")
    vecs = dt("vecs", [128, 64 * depth], F32, "ExternalInput")
    outd = dt("out", [D, Sc], F32, "ExternalOutput")
    W = []
    for i in range(depth):
        kind = "lru" if i % 2 == 0 else "gdn"
        ychan = 704 if kind == "lru" else 1024
        d = dict(kind=kind, ychan=ychan)
        d["w_xq"] = dt("w_xq%d" % i, [D, D], F32, "ExternalInput")
        d["w_kv"] = dt("w_kv%d" % i, [D, 2 * D], F32, "ExternalInput")
        d["w_out"] = dt("w_out%d" % i, [4 * ychan + D, D], F32, "ExternalInput")
        d["w_fi"] = dt("w_fi%d" % i, [D, 2 * D_FF], F32, "ExternalInput")
        d["w_fo"] = dt("w_fo%d" % i, [D_FF, D], F32, "ExternalInput")
        if kind == "lru":
            d["wx"] = dt("wx%d" % i, [D, 704], F32, "ExternalInput")
            d["wg"] = dt("wg%d" % i, [D, 704], F32, "ExternalInput")
            d["ga"] = dt("ga%d" % i, [4, 176, 176], F32, "ExternalInput")
            d["gx"] = dt("gx%d" % i, [4, 176, 176], F32, "ExternalInput")
            d["lvec"] = dt("lvec%d" % i, [88, 8, 8], F32, "ExternalInput")
        else:
            d["gw"] = dt("gw%d" % i, [D, 3088], F32, "ExternalInput")
            d["cvec"] = dt("cvec%d" % i, [128, 16, 4], F32, "ExternalInput")
            d["gvec"] = dt("gvec%d" % i, [128, 1], F32, "ExternalInput")
            d["pvec"] = dt("pvec%d" % i, [8, 2], F32, "ExternalInput")
        W.append(d)
    hn_loc = nc.dram_tensor("hn_loc", [NQ, D, 256], BF16).ap()
    hn_all = nc.dram_tensor("hn_all", [NQ, 4 * D, 256], BF16).ap()
    r_hn_loc = [Res("hn_loc%d" % q) for q in range(NQ)]
    r_hn_all = [Res("hn_all%d" % q) for q in range(NQ)]
    ybuf = {}
    for kind, ychan in (("lru", 704), ("gdn", 1024)):
        ybuf[kind] = (nc.dram_tensor("y_loc_" + kind, [NYK, ychan, 512], BF16).ap(),
                      nc.dram_tensor("y_all_" + kind, [NYK, 4 * ychan, 512], BF16).ap(),
                      [Res("yl%d" % k) for k in range(NYK)], [Res("ya%d" % k) for k in range(NYK)])
    if debug:
        dbg_hn = dt("dbg_hn", [NQ, 4 * D, 256], BF16, "ExternalOutput")
        dbg_y = dt("dbg_y", [NYK, 4 * 704, 512], BF16, "ExternalOutput")
        dbg_yl = dt("dbg_yl", [NYK, 704, 512], BF16, "ExternalOutput")
    with ExitStack() as es:
        P = Prog(nc, es)
        c = make_ctx(nc, es, P, nslots=4)
        c.vecs = c.sb("vecs", [128, 64 * depth], F32)
        c.r_vecs = Res("vecs")
        c.h = c.sb("h", [128, KD, Sc], F32)
        c.rh = Res("h")
        c.ych = P.chan()
        c.och = P.chan()
        P.dma("sp", P.chan(), lambda e: e.dma_start(out=c.vecs[:], in_=vecs), writes=[c.r_vecs])
        P.dma("sp", P.chan(), lambda e: e.dma_start(out=c.h[:], in_=xin.rearrange("(kc p) t -> p kc t", p=128)), writes=[c.rh])

        hn_ch = [P.chan() for _ in range(NQ)]
        pid_cache = {}
        build_gdn_consts(c, TT)

        def hn_exchange(tsl):
            hn_ = c.hn
            for s0 in range(0, T, 256):
                q = (tsl.start + s0) // 256
                P.dma("sp", hn_ch[q], lambda e, q=q, s0=s0, hn_=hn_: e.dma_start(out=hn_loc[q].rearrange("(kc p) t -> p kc t", p=128),
                                                                              in_=hn_[:, :, s0:s0 + 256]),
                      reads=[c.rhn], writes=[r_hn_loc[q]])
                P.coll(P.cc_chan("hn%d" % q), lambda e, q=q: e.collective_compute("AllGather", ALU.bypass, replica_groups=GROUPS,
                                                                               ins=[hn_loc[q].opt()], outs=[hn_all[q].opt()]),
                       reads=[r_hn_loc[q]], writes=[r_hn_all[q]])

        with ExitStack() as es2:
            _pass_bufs(c, es2, T, "a")
            for p0 in range(0, Sc, T):
                emit_tok_pass(c, dict(g_next=0, nyc=0, rpc_y=128), slice(p0, p0 + T), T, hn_dst=hn_exchange, out_dst=None, do_main=False)
            P.barrier()
        preps = {}
        gpreps = {}
        for i in range(depth):
            d = W[i]
            kind, ychan = d["kind"], d["ychan"]
            final = (i == depth - 1)
            y_loc, y_all, r_yl, r_ya = ybuf[kind]
            tasks = []
            if kind == "gdn" and i not in gpreps:
                gpreps[i] = prep_gdn_weights(c, d["gw"])
            preps[i] = prep_layer_weights(c, d["w_xq"], d["w_kv"], d["w_out"], d["w_fi"], d["w_fo"], 32, 88 if kind == "lru" else 128, tasks)
            if i + 1 < depth and W[i + 1]["kind"] == "gdn":
                gpreps[i + 1] = prep_gdn_weights(c, W[i + 1]["gw"], tasks)
            NTB = S // TT
            ncalls = [NTB * 8 if kind == "lru" else NTB * (TT // 64 - 1)]

            def progress(gate, tasks=tasks, ncalls=ncalls):
                k = -(-len(tasks) // max(1, ncalls[0]))
                ncalls[0] -= 1
                for _ in range(k):
                    if tasks:
                        pr_, bi_ = tasks.pop(0)
                        prep_issue(c, pr_, bi_, gate)
            c.progress = progress

            def hn_src(t):
                r, q = t // NQ, t % NQ
                return hn_all[q][r * D:(r + 1) * D, :].rearrange("(kc p) t -> p kc t", p=128)

            def y_dst(t, r0, r1, y_loc=y_loc):
                return y_loc[t // 2][r0:r1, (t % 2) * 256:(t % 2 + 1) * 256]

            ystores = [[] for _ in range(NYK)]

            def tile_done(t, gate, y_loc=y_loc, y_all=y_all, ystores=ystores, r_ya=r_ya):
                if t % 2 == 1:
                    k = t // 2
                    r_yl = ystores
                    P.coll(P.cc_chan("y%d" % k), lambda e, k=k: e.collective_compute("AllGather", ALU.bypass, replica_groups=GROUPS,
                                                                                  ins=[y_loc[k].opt()], outs=[y_all[k].opt()]),
                           reads=r_yl[k], writes=[r_ya[k]])

            def ry_d(t, ystores=ystores):
                r = Res("ys")
                ystores[t // 2].append(r)
                return [r]
            if kind == "lru":
                emit_lru(c, S, TT, hn_src, d["wx"], d["wg"], d["ga"], d["gx"], d["lvec"], y_dst, tile_done, rhn_d=r_hn_all, ry_d=ry_d)
            else:
                emit_gdn(c, S, TT, hn_src, gpreps[i], d["cvec"], d["gvec"], d["pvec"], y_dst, tile_done, rhn_d=r_hn_all, ry_d=ry_d)
            if debug and i == 0:
                dch = P.chan()
                P.dma("sp", dch, lambda e: e.dma_start(out=dbg_hn, in_=hn_all), reads=r_hn_all)
                P.dma("sp", dch, lambda e: e.dma_start(out=dbg_y, in_=y_all), reads=r_ya)
                P.dma("sp", dch, lambda e: e.dma_start(out=dbg_yl, in_=y_loc), reads=r_yl)
            c.progress = None
            while tasks:
                pr_, bi_ = tasks.pop(0)
                prep_issue(c, pr_, bi_)
            with ExitStack() as es3:
                c.KT = es3.enter_context(nc.sbuf_tensor("sb_KT%d" % i, [128, KD, MEM], BF16)); c.rKT = Res("KT")
                c.V = es3.enter_context(nc.sbuf_tensor("sb_V%d" % i, [128, 2, D], BF16)); c.rV = Res("V")
                rpc_y = 88 if kind == "lru" else 128
                lay = dict(g_mix=64 * i, g_mem=64 * i + 16, g_ffn=64 * i + 32, g_next=64 * i + 48, nyc=32, rpc_y=rpc_y, y_res=r_ya)
                lay.update(preps[i])

                def y_ap(tsl, e, y_all=y_all, rpc_y=rpc_y):
                    if "cidx" not in pid_cache:
                        pid_cache["cidx"] = e.partition_id() % 4
                    cidx = pid_cache["cidx"]
                    ya4 = y_all.rearrange("(c q) r t -> c q r t", c=4)
                    return ya4[bass.ds(cidx, 1), tsl.start // 512].rearrange("o (kc p) t -> p (o kc) t", p=rpc_y)
                lay["y_ap"] = y_ap
                emit_kv(c, lay, mem, lay["p_kv"], lay["g_mem"])
                with ExitStack() as es2:
                    _pass_bufs(c, es2, T, "c%d" % i)
                    for p0 in range(0, Sc, T):
                        tsl = slice(p0, p0 + T)
                        emit_tok_pass(c, lay, tsl, T, hn_dst=None if final else hn_exchange,
                                      out_dst=(lambda tsl: outd[:, tsl].rearrange("(kc p) t -> p kc t", p=128)) if final else None)
                    P.barrier()
        P.emit()
    return nc


_PROGS = {}


def _prog(key, fn):
    if key not in _PROGS:
        _PROGS[key] = fn()
    return _PROGS[key]


def _vecs(I, i, gnext):
    v = np.zeros((128, 64), np.float32)
    for k, g in enumerate([I["norm_mix_g"][i], I["norm_mem_g"][i], I["norm_ffn_g"][i], gnext]):
        v[:, 16 * k:16 * (k + 1)] = np.asarray(g, np.float32).reshape(16, 128).T
    return v


def _lru_inputs(I, j, g, hn):
    ch = np.arange(g * 704, (g + 1) * 704)
    cw = I["lru_conv_w"][j]
    lvec = np.stack([cw[0, ch], cw[1, ch], cw[2, ch], cw[3, ch], I["lru_conv_b"][j][ch], I["lru_gate_a_b"][j][ch],
                     I["lru_gate_x_b"][j][ch], I["lru_lambda"][j][ch]], -1)
    lvec = np.ascontiguousarray(lvec.reshape(8, 88, 8).transpose(1, 0, 2))
    return {"hn": hn, "wx": np.ascontiguousarray(I["lru_w_in"][j][:, ch]), "wg": np.ascontiguousarray(I["lru_w_in"][j][:, D_RNN + ch]),
            "ga": np.ascontiguousarray(I["lru_gate_a_w"][j][4 * g:4 * g + 4]), "gx": np.ascontiguousarray(I["lru_gate_x_w"][j][4 * g:4 * g + 4]),
            "lvec": lvec}


def _gdn_inputs(I, j, g, hn):
    Wi = I["gdn_w_in"][j]
    hq = np.arange(4 * g * 128, (4 * g + 4) * 128)
    hv = np.arange(8 * g * 128, (8 * g + 8) * 128)
    cols = np.concatenate([hq, GQK + hq, 2 * GQK + hv, 2 * GQK + GV + hv, 2 * GQK + 2 * GV + np.arange(8 * g, 8 * g + 8),
                           2 * GQK + 2 * GV + 32 + np.arange(8 * g, 8 * g + 8)])
    W = np.ascontiguousarray(Wi[:, cols])
    cch = np.concatenate([hq, GQK + hq, 2 * GQK + hv])
    cvec = np.ascontiguousarray(I["gdn_conv_w"][j][:, cch].T.reshape(16, 128, 4).transpose(1, 0, 2))
    gvec = np.ascontiguousarray(np.asarray(I["gdn_norm_g"][j], np.float32).reshape(128, 1))
    pvec = np.ascontiguousarray(np.stack([I["gdn_a_log"][j][8 * g:8 * g + 8], I["gdn_dt_bias"][j][8 * g:8 * g + 8]], -1).astype(np.float32))
    return {"hn": hn, "w": W, "cvec": cvec, "gvec": gvec, "pvec": pvec}


def kernel_unfused(I, NB=2, S=4096, depth=4):
    I = {k: np.asarray(v) for k, v in I.items()}
    Sc = S // 4
    T = min(512, Sc)
    TT = min(512, S)
    ncores = 4 * NB
    cores = list(range(ncores))
    x = I["x"]
    ncA = _prog(("tokA", Sc, T), lambda: build_tok_program(Sc, T, "lru", True, False, do_main=False))
    hs = [np.ascontiguousarray(x[i // 4, (i % 4) * Sc:(i % 4 + 1) * Sc, :].T) for i in cores]
    res = run_bass_kernel_spmd(ncA, [{"hin": hs[i], "vecs": _vecs(I, 0, I["norm_mix_g"][0])} for i in cores], core_ids=cores).results
    hn = [np.concatenate([res[b * 4 + c]["hnn"] for c in range(4)], axis=1) for b in range(NB)]
    out = None
    for i in range(depth):
        j = i // 2
        kind = "lru" if i % 2 == 0 else "gdn"
        final = (i == depth - 1)
        if kind == "lru":
            ncB = _prog(("lru", S, TT), lambda: build_lru_program(S, TT))
            ins = [_lru_inputs(I, j, c % 4, hn[c // 4]) for c in cores]
        else:
            ncB = _prog(("gdn", S, TT), lambda: build_gdn_program(S, TT))
            ins = [_gdn_inputs(I, j, c % 4, hn[c // 4]) for c in cores]
        res = run_bass_kernel_spmd(ncB, ins, core_ids=cores).results
        y = [np.concatenate([res[b * 4 + g]["y"] for g in range(4)], axis=0) for b in range(NB)]
        ncC = _prog(("tokC", Sc, T, kind, final), lambda: build_tok_program(Sc, T, kind, False, final))
        w_in = I["lru_w_in"][j] if kind == "lru" else I["gdn_w_in"][j]
        w_xq = np.ascontiguousarray(w_in[:, -D:])
        w_out = I["lru_w_out"][j] if kind == "lru" else I["gdn_w_out"][j]
        gnext = I["final_norm_g"] if final else I["norm_mix_g"][i + 1]
        vecs = _vecs(I, i, gnext)
        ins = []
        for c in cores:
            b, q = c // 4, c % 4
            ins.append({"hin": hs[c], "vecs": vecs, "y": np.ascontiguousarray(y[b][:, q * Sc:(q + 1) * Sc]), "mem": I["mem"][b],
                        "w_xq": w_xq, "w_kv": I["mem_kv_w"][i], "w_out": w_out, "w_fi": I["ffn_w_in"][i], "w_fo": I["ffn_w_out"][i]})
        res = run_bass_kernel_spmd(ncC, ins, core_ids=cores).results
        if final:
            out = np.zeros((NB, S, D), np.float32)
            for c in cores:
                out[c // 4, (c % 4) * Sc:(c % 4 + 1) * Sc, :] = res[c]["out"].T
        else:
            hs = [res[c]["hout"] for c in cores]
            hn = [np.concatenate([res[b * 4 + c]["hnn"] for c in range(4)], axis=1) for b in range(NB)]
    return out


def kernel_fused(I, NB=2, S=4096, depth=4, debug=False):
    I = {k: np.asarray(v) for k, v in I.items()}
    Sc = S // 4
    ncores = 4 * NB
    cores = list(range(ncores))
    nc = _prog(("fused", S, depth, NB, debug), lambda: build_fused_program(S, depth, NB, debug))
    vecs = np.concatenate([_vecs(I, i, I["final_norm_g"] if i == depth - 1 else I["norm_mix_g"][i + 1]) for i in range(depth)], axis=1)
    shared = {}
    for i in range(depth):
        j = i // 2
        kind = "lru" if i % 2 == 0 else "gdn"
        w_in = I["lru_w_in"][j] if kind == "lru" else I["gdn_w_in"][j]
        shared["w_xq%d" % i] = np.ascontiguousarray(w_in[:, -D:])
        shared["w_kv%d" % i] = I["mem_kv_w"][i]
        shared["w_out%d" % i] = I["lru_w_out"][j] if kind == "lru" else I["gdn_w_out"][j]
        shared["w_fi%d" % i] = I["ffn_w_in"][i]
        shared["w_fo%d" % i] = I["ffn_w_out"][i]
    ins = []
    for cidx in cores:
        b, g = cidx // 4, cidx % 4
        m = dict(shared)
        m["x"] = np.ascontiguousarray(I["x"][b, g * Sc:(g + 1) * Sc, :].T)
        m["mem"] = I["mem"][b]
        m["vecs"] = vecs
        for i in range(depth):
            j = i // 2
            if i % 2 == 0:
                li = _lru_inputs(I, j, g, None)
                for k in ("wx", "wg", "ga", "gx", "lvec"):
                    m["%s%d" % (k, i)] = li[k]
            else:
                gi = _gdn_inputs(I, j, g, None)
                m["gw%d" % i] = gi["w"]
                for k in ("cvec", "gvec", "pvec"):
                    m["%s%d" % (k, i)] = gi[k]
        ins.append(m)
    res = run_bass_kernel_spmd(nc, ins, core_ids=cores).results
    out = np.zeros((NB, S, D), np.float32)
    for cidx in cores:
        out[cidx // 4, (cidx % 4) * Sc:(cidx % 4 + 1) * Sc, :] = res[cidx]["out"].T
    if debug:
        return out, res
    return out


def kernel(**inputs):
    return kernel_fused(inputs)
```

```python
import numpy as np
from contextlib import ExitStack
import ml_dtypes
import concourse.bass as bass
import concourse.mybir as mybir
from concourse.bass_utils import run_bass_kernel_spmd

F32 = mybir.dt.float32
BF16 = mybir.dt.bfloat16
ALU = mybir.AluOpType
AF = mybir.ActivationFunctionType
AX = mybir.AxisListType
NPBF = ml_dtypes.bfloat16

D = 2048
KD = 16
EPS = 1e-6
MEM = 256
D_RNN = 2816
D_FF = 5632
XH = 4
GQK = 2048
GV = 4096
GDN_IN_MIX = 2 * GQK + 2 * GV + 64
ENGS = ("pe", "dve", "act", "pool", "sp")


class Res:
    __slots__ = ("name", "w", "r")

    def __init__(self, name=""):
        self.name = name
        self.w = None
        self.r = {}


class Prog:
    def __init__(self, nc, es):
        self.nc = nc
        self.es = es
        self.sems = {}
        self.cnt = {}
        self.known = {e: {} for e in ENGS}
        self.streams = {e: [] for e in ENGS}
        for e in ENGS:
            self._newsem(e)
        self.ndma = 0
        self.named = {}

    def _newsem(self, key):
        self.sems[key] = self.es.enter_context(self.nc.semaphore("s%d" % len(self.sems)))
        self.cnt[key] = 0

    def chan(self, name=None):
        if name is not None and name in self.named:
            return self.named[name]
        key = ("dma", self.ndma)
        self.ndma += 1
        self._newsem(key)
        if name is not None:
            self.named[name] = key
        return key

    def cc_chan(self, name):
        if name in self.named:
            return self.named[name]
        key = ("cc", self.ndma)
        self.ndma += 1
        self._newsem(key)
        self.named[name] = key
        return key

    def _deps(self, eng, reads, writes):
        deps = {}

        def add(d):
            if d is not None and deps.get(d[0], 0) < d[1]:
                deps[d[0]] = d[1]
        for r in reads:
            add(r.w)
        for w in writes:
            add(w.w)
            for k, n in w.r.items():
                add((k, n))
        waits = []
        for k, n in deps.items():
            if k == eng and eng == "pe":
                continue
            if isinstance(k, tuple) and k[0] == "dma":
                n = self.cnt[k]
            if self.known[eng].get(k, 0) < n:
                self.known[eng][k] = n
                waits.append((k, n))
        return waits

    def op(self, eng, fn, reads=(), writes=()):
        waits = self._deps(eng, reads, writes)
        self.cnt[eng] += 1
        n = self.cnt[eng]
        for r in reads:
            r.r[eng] = n
        for w in writes:
            w.w = (eng, n)
            w.r = {}
        self.streams[eng].append((waits, fn, (eng, 1)))

    def dma(self, queue, ch, fn, reads=(), writes=(), ndma=1, gate=()):
        waits = self._deps(queue, list(reads) + list(gate), writes)
        self.cnt[ch] += 16 * ndma
        n = self.cnt[ch]
        for r in reads:
            r.r[ch] = n
        for w in writes:
            w.w = (ch, n)
            w.r = {}
        self.streams[queue].append((waits, fn, (ch, 16)))

    def coll(self, key, fn, reads=(), writes=()):
        waits = self._deps("pool", reads, writes)
        self.cnt[key] += 1
        n = self.cnt[key]
        for r in reads:
            r.r[key] = n
        for w in writes:
            w.w = (key, n)
            w.r = {}
        self.streams["pool"].append((waits, fn, (key, None)))

    def barrier(self):
        for e in ENGS:
            waits = []
            for k, n in self.cnt.items():
                if n > 0 and self.known[e].get(k, 0) < n and k != e:
                    self.known[e][k] = n
                    waits.append((k, n))
            if waits:
                self.streams[e].append((waits, None, None))

    def emit(self):
        sems = self.sems

        def replay(stream):
            def run(e):
                for waits, fn, inc in stream:
                    for k, n in waits:
                        e.wait_ge(sems[k], n)
                    if fn is None:
                        continue
                    ins = fn(e)
                    if inc[1] is None:
                        ins.then_inc(sems[inc[0]])
                    elif isinstance(ins, (list, tuple)):
                        for i in ins:
                            i.then_inc(sems[inc[0]], inc[1])
                    else:
                        ins.then_inc(sems[inc[0]], inc[1])
            return run
        with self.nc.Block() as block:
            block.tensor(replay(self.streams["pe"]))
            block.vector(replay(self.streams["dve"]))
            block.scalar(replay(self.streams["act"]))
            block.gpsimd(replay(self.streams["pool"]))
            block.sync(replay(self.streams["sp"]))


class Ctx:
    pass


_UID = [0]


def _uid():
    _UID[0] += 1
    return _UID[0]


def make_ctx(nc, es, P, nslots=3):
    c = Ctx()
    c.nc, c.es, c.P = nc, es, P
    sb = lambda name, shape, dt=F32: es.enter_context(nc.sbuf_tensor("sb_" + name, shape, dt))
    c.sb = sb
    c.ones_f = sb("ones_f", [128, 128], F32)
    c.ones_b = sb("ones_b", [128, 128], BF16)
    c.ident = sb("ident", [128, 128], F32)
    c.r_const = Res("const")
    P.op("pool", lambda e: e.memset(c.ones_f[:], 1.0), writes=[c.r_const])
    P.op("pool", lambda e: e.memset(c.ones_b[:], 1.0), writes=[c.r_const])
    P.op("pool", lambda e: e.memset(c.ident[:], 0.0), writes=[c.r_const])
    P.op("pool", lambda e: e.affine_select(out=c.ident[:], in_=c.ident[:], pattern=[[-1, 128]],
                                           compare_op=ALU.not_equal, fill=1.0, base=0,
                                           channel_multiplier=1),
         reads=[c.r_const], writes=[c.r_const])
    c.nslots = nslots
    c.wslots = [sb("wslot%d" % i, [128, 32, 128], BF16) for i in range(nslots)]
    c.wres = [Res("w%d" % i) for i in range(nslots)]
    c.wch = [P.chan() for _ in range(nslots)]
    c.wnext = 0
    c.ps = [es.enter_context(nc.psum_tensor("ps%d" % i, [128, 512], F32)) for i in range(8)]
    c.psr = [Res("ps%d" % i) for i in range(8)]
    c.psn = 0
    c.progress = None
    return c


def next_ps(c, lo=0, hi=4):
    i = lo + (c.psn % (hi - lo))
    c.psn += 1
    return c.ps[i], c.psr[i]


class Prep:
    pass


def wprep(c, blocks, kctot, tasks=None):
    P, nc = c.P, c.nc
    pr = Prep()
    pr.n = len(blocks)
    pr.kctot = kctot
    pr.blocks = blocks
    pr.ap = nc.dram_tensor("wb%d" % _uid(), [len(blocks), 128, kctot * 128], BF16).ap()
    pr.res = [Res("prep") for _ in blocks]
    for i in range(len(blocks)):
        if tasks is None:
            prep_issue(c, pr, i)
        else:
            tasks.append((pr, i))
    return pr


def prep_issue(c, pr, i, gate=()):
    segs = pr.blocks[i]

    def fn(e, segs=segs, i=i):
        out = []
        dst = pr.ap[i].rearrange("p (kc m) -> p kc m", m=128)
        for (W, row0, KC, rpc, col0, MB, off) in segs:
            src = W[row0:row0 + KC * rpc, col0:col0 + MB].rearrange("(kc p) m -> p kc m", p=rpc)
            out.append(e.dma_start(out=dst[0:rpc, off:off + KC, 0:MB], in_=src))
        return out
    c.P.dma("pool", c.P.chan("prep"), fn, gate=list(gate), writes=[pr.res[i]], ndma=len(segs))


def wfetch(c, pr, i):
    P = c.P
    k = c.wnext
    c.wnext = (c.wnext + 1) % c.nslots
    slot, res, ch = c.wslots[k], c.wres[k], c.wch[k]
    P.dma("sp", ch, lambda e, slot=slot, i=i: e.dma_start(out=slot[:, 0:pr.kctot, :].rearrange("p kc m -> p (kc m)"), in_=pr.ap[i]),
          reads=[pr.res[i]], writes=[res])
    return slot, res


def prep_layer_weights(c, w_xq, w_kv, w_out, w_fi, w_fo, nyc, rpc_y, tasks=None):
    d = {}
    d["p_kv"] = wprep(c, [[(w_kv, 0, KD, 128, mo * 128, 128, 0)] for mo in range(2 * KD)], KD, tasks)
    d["p_xq"] = wprep(c, [[(w_xq, 0, KD, 128, b * 128, 128, 0)] for b in range(KD)], KD, tasks)
    ob = []
    for mo in range(KD):
        ob.append([(w_out, 0, 24, rpc_y, mo * 128, 128, 0)])
        ob.append([(w_out, 24 * rpc_y, nyc - 24, rpc_y, mo * 128, 128, 0), (w_out, nyc * rpc_y, KD, 128, mo * 128, 128, nyc - 24)])
    d["p_out"] = wprep(c, ob, 24, tasks)
    d["p_fi"] = wprep(c, [[(w_fi, 0, KD, 128, j * 128, 128, 0), (w_fi, 0, KD, 128, D_FF + j * 128, 128, KD)] for j in range(D_FF // 128)], 2 * KD, tasks)
    d["p_fo"] = wprep(c, [[(w_fo, half * 22 * 128, 22, 128, mo * 128, 128, 0)] for mo in range(KD) for half in range(2)], 22, tasks)
    return d


def prep_gdn_weights(c, w_ap, tasks=None):
    blocks = [[(w_ap, 0, KD, 128, f * 128, 128, 0)] for f in range(24)] + [[(w_ap, 0, KD, 128, 3072, 16, 0)]]
    return wprep(c, blocks, KD, tasks)


def emit_norm(c, h, rh, tsl, T, gcol, hn, rhn, tmp, inplace=False):
    P = c.P
    ps, rps = c.ps[7], c.psr[7]
    for kc in range(KD):
        sq, rsq = tmp.sq[kc % 2], tmp.rsq[kc % 2]
        P.op("act", lambda e, kc=kc, sq=sq: e.activation(out=sq[:, 0:T], in_=h[:, kc, tsl], func=AF.Square),
             reads=[rh], writes=[rsq])
        P.op("pe", lambda e, kc=kc, sq=sq: e.matmul(ps[:, 0:T], lhsT=c.ones_f[:], rhs=sq[:, 0:T],
                                                     start=(kc == 0), stop=(kc == KD - 1)),
             reads=[rsq, c.r_const], writes=[rps])
    P.op("act", lambda e: e.activation(out=tmp.rstd[:, 0:T], in_=ps[:, 0:T], func=AF.Sqrt, bias=EPS, scale=1.0 / D),
         reads=[rps], writes=[tmp.rrstd])
    P.op("dve", lambda e: e.reciprocal(out=tmp.rstd[:, 0:T], in_=tmp.rstd[:, 0:T]),
         reads=[tmp.rrstd], writes=[tmp.rrstd])
    for kc in range(KD):
        P.op("dve", lambda e, kc=kc: e.scalar_tensor_tensor(
            out=(h[:, kc, tsl] if inplace else hn[:, kc, 0:T]), in0=h[:, kc, tsl], scalar=c.vecs[:, gcol + kc:gcol + kc + 1],
            in1=tmp.rstd[:, 0:T], op0=ALU.mult, op1=ALU.mult),
            reads=[rh, tmp.rrstd, c.r_vecs], writes=[rh if inplace else rhn])


class Tmp:
    pass


def make_tmp(c, T):
    t = Tmp()
    t.sq = [c.sb("sq%d" % i, [128, T], F32) for i in range(2)]
    t.rsq = [Res("sq%d" % i) for i in range(2)]
    t.rstd = c.sb("rstd", [128, T], F32)
    t.rrstd = Res("rstd")
    return t


def emit_kv(c, L, mem_ap, p_kv, gcol_mem):
    P, nc = c.P, c.nc
    KT_, V_ = c.KT, c.V
    with ExitStack() as es2:
        u = _uid()
        sb2 = lambda name, shape, dt=F32: es2.enter_context(nc.sbuf_tensor("sk%d_%s" % (u, name), shape, dt))
        memt = sb2("memt", [128, 2, D], F32)
        rmem = Res("memt")
        sqj = sb2("sqj", [128, D], F32)
        rsqj = Res("sqj")
        ssq = sb2("ssq", [128, 2], F32)
        rssq = Res("ssq")
        memT = sb2("memT", [128, KD, MEM], BF16)
        rmemT = Res("memT")
        ch = P.chan("kvmem")
        P.dma("sp", ch, lambda e: e.dma_start(out=memt[:], in_=mem_ap.rearrange("(mc p) d -> p mc d", p=128)),
              writes=[rmem])
        for mc in range(2):
            P.op("act", lambda e, mc=mc: e.activation(out=sqj[:], in_=memt[:, mc, :], func=AF.Square),
                 reads=[rmem], writes=[rsqj])
            P.op("dve", lambda e, mc=mc: e.reduce_sum(out=ssq[:, mc:mc + 1], in_=sqj[:], axis=AX.X),
                 reads=[rsqj], writes=[rssq])
        P.op("act", lambda e: e.activation(out=ssq[:], in_=ssq[:], func=AF.Sqrt, bias=EPS, scale=1.0 / D),
             reads=[rssq], writes=[rssq])
        P.op("dve", lambda e: e.reciprocal(out=ssq[:], in_=ssq[:]), reads=[rssq], writes=[rssq])
        for mc in range(2):
            P.op("dve", lambda e, mc=mc: e.tensor_scalar(out=memt[:, mc, :], in0=memt[:, mc, :],
                                                           scalar1=ssq[:, mc:mc + 1], scalar2=None, op0=ALU.mult),
                 reads=[rmem, rssq], writes=[rmem])
        for kc in range(KD):
            for mc in range(2):
                ps, rps = next_ps(c)
                P.op("pe", lambda e, kc=kc, mc=mc, ps=ps: e.transpose(ps[:, 0:128], memt[:, mc, kc * 128:(kc + 1) * 128], c.ident[:]),
                     reads=[rmem, c.r_const], writes=[rps])
                P.op("act", lambda e, kc=kc, mc=mc, ps=ps: e.activation(
                    out=memT[:, kc, mc * 128:(mc + 1) * 128], in_=ps[:, 0:128], func=AF.Copy,
                    scale=c.vecs[:, gcol_mem + kc:gcol_mem + kc + 1]),
                    reads=[rps, c.r_vecs], writes=[rmemT])
        for mo in range(KD):
            slot, rw = wfetch(c, p_kv, mo)
            ps, rps = next_ps(c)

            def mm(e, slot=slot, ps=ps):
                r = None
                for kc in range(KD):
                    r = e.matmul(ps[:, 0:MEM], lhsT=slot[:, kc, 0:128], rhs=memT[:, kc, :], start=(kc == 0), stop=(kc == KD - 1))
                return r
            P.op("pe", mm, reads=[rw, rmemT], writes=[rps])
            P.op("act", lambda e, mo=mo, ps=ps: e.copy(out=KT_[:, mo, :], in_=ps[:, 0:MEM]), reads=[rps], writes=[c.rKT])
        for cb in range(KD):
            slot, rw = wfetch(c, p_kv, KD + cb)
            for mc in range(2):
                ps, rps = next_ps(c)

                def mm(e, slot=slot, ps=ps, mc=mc):
                    r = None
                    for kc in range(KD):
                        r = e.matmul(ps[:, 0:128], lhsT=memT[:, kc, mc * 128:(mc + 1) * 128], rhs=slot[:, kc, 0:128],
                                     start=(kc == 0), stop=(kc == KD - 1))
                    return r
                P.op("pe", mm, reads=[rw, rmemT], writes=[rps])
                P.op("dve", lambda e, cb=cb, mc=mc, ps=ps: e.tensor_copy(out=V_[:, mc, cb * 128:(cb + 1) * 128], in_=ps[:, 0:128]),
                     reads=[rps], writes=[c.rV])
        P.barrier()


def emit_tok_pass(c, lay, tsl, T, hn_dst, out_dst, do_main=True):
    P, nc = c.P, c.nc
    h, rh = c.h, c.rh
    hn, rhn = c.hn, c.rhn
    big, rbig = c.big, c.rbig
    tmp = c.tmp
    nyc, rpc_y = lay["nyc"], lay["rpc_y"]
    scale = float(512 ** -0.5)

    if do_main:
        emit_tok_main(c, lay, tsl, T)
    if hn_dst is not None:
        emit_norm(c, h, rh, tsl, T, lay["g_next"], hn, rhn, tmp)
        hn_dst(tsl)
    if out_dst is not None:
        emit_norm(c, h, rh, tsl, T, lay["g_next"], None, None, tmp, inplace=True)
        P.dma("sp", c.och, lambda e: e.dma_start(out=out_dst(tsl), in_=h[:, :, tsl]), reads=[rh])


def emit_tok_main(c, lay, tsl, T):
    P, nc = c.P, c.nc
    h, rh = c.h, c.rh
    hn, rhn = c.hn, c.rhn
    big, rbig = c.big, c.rbig
    tmp = c.tmp
    nyc, rpc_y = lay["nyc"], lay["rpc_y"]
    scale = float(512 ** -0.5)
    KT_, V_, qh_, ex_, rden_ = c.KT, c.V, c.qh, c.ex, c.rden
    emit_norm(c, h, rh, tsl, T, lay["g_mix"], hn, rhn, tmp)
    ych = c.ych
    P.dma("sp", ych, lambda e: e.dma_start(out=big[0:rpc_y, 0:nyc, 0:T], in_=lay["y_ap"](tsl, e)), reads=list(lay.get("y_res", ())), writes=[rbig])
    for hh in range(XH):
        for dc in range(4):
            slot, rw = wfetch(c, lay["p_xq"], hh * 4 + dc)
            ps, rps = next_ps(c)

            def mm(e, slot=slot, ps=ps):
                r = None
                for kc in range(KD):
                    r = e.matmul(ps[:, 0:T], lhsT=slot[:, kc, 0:128], rhs=hn[:, kc, 0:T], start=(kc == 0), stop=(kc == KD - 1))
                return r
            P.op("pe", mm, reads=[rw, rhn], writes=[rps])
            P.op("act", lambda e, dc=dc, ps=ps: e.copy(out=qh_[:, dc, 0:T], in_=ps[:, 0:T]), reads=[rps], writes=[c.rqh])
        for mc in range(2):
            ps, rps = next_ps(c, 4, 6)

            def mm(e, ps=ps, mc=mc, hh=hh):
                r = None
                for dc in range(4):
                    r = e.matmul(ps[:, 0:T], lhsT=KT_[:, hh * 4 + dc, mc * 128:(mc + 1) * 128], rhs=qh_[:, dc, 0:T],
                                 start=(dc == 0), stop=(dc == 3))
                return r
            P.op("pe", mm, reads=[c.rKT, c.rqh], writes=[rps])
            P.op("act", lambda e, ps=ps, mc=mc: e.activation(out=ex_[:, mc, 0:T], in_=ps[:, 0:T], func=AF.Exp, scale=scale),
                 reads=[rps], writes=[c.rex])
        ps, rps = c.ps[6], c.psr[6]

        def mmd(e, ps=ps):
            r = None
            for mc in range(2):
                r = e.matmul(ps[:, 0:T], lhsT=c.ones_b[:], rhs=ex_[:, mc, 0:T], start=(mc == 0), stop=(mc == 1))
            return r
        P.op("pe", mmd, reads=[c.rex, c.r_const], writes=[rps])
        P.op("dve", lambda e, ps=ps: e.reciprocal(out=rden_[:, 0:T], in_=ps[:, 0:T]), reads=[rps], writes=[c.rrden])
        for dc in range(4):
            ps, rps = next_ps(c, 4, 6)

            def mmo(e, ps=ps, dc=dc, hh=hh):
                r = None
                for mc in range(2):
                    r = e.matmul(ps[:, 0:T], lhsT=V_[:, mc, hh * 512 + dc * 128: hh * 512 + (dc + 1) * 128],
                                 rhs=ex_[:, mc, 0:T], start=(mc == 0), stop=(mc == 1))
                return r
            P.op("pe", mmo, reads=[c.rV, c.rex], writes=[rps])
            P.op("dve", lambda e, ps=ps, dc=dc, hh=hh: e.tensor_tensor(out=big[:, nyc + hh * 4 + dc, 0:T], in0=ps[:, 0:T],
                                                                       in1=rden_[:, 0:T], op=ALU.mult),
                 reads=[rps, c.rrden], writes=[rbig])
    for mo in range(KD):
        ps, rps = next_ps(c)
        slot, rw = wfetch(c, lay["p_out"], 2 * mo)

        def mm0(e, slot=slot, ps=ps):
            r = None
            for kc in range(24):
                r = e.matmul(ps[:, 0:T], lhsT=slot[0:rpc_y, kc, 0:128], rhs=big[0:rpc_y, kc, 0:T], start=(kc == 0), stop=False)
            return r
        P.op("pe", mm0, reads=[rw, rbig], writes=[rps])
        slot, rw = wfetch(c, lay["p_out"], 2 * mo + 1)

        def mm1(e, slot=slot, ps=ps):
            r = None
            for kc in range(24, nyc):
                r = e.matmul(ps[:, 0:T], lhsT=slot[0:rpc_y, kc - 24, 0:128], rhs=big[0:rpc_y, kc, 0:T], start=False, stop=False)
            for kc in range(nyc, nyc + KD):
                r = e.matmul(ps[:, 0:T], lhsT=slot[:, kc - 24, 0:128], rhs=big[:, kc, 0:T], start=False, stop=(kc == nyc + KD - 1))
            return r
        P.op("pe", mm1, reads=[rw, rbig], writes=[rps])
        P.op("dve", lambda e, mo=mo, ps=ps: e.tensor_tensor(out=h[:, mo, tsl], in0=ps[:, 0:T], in1=h[:, mo, tsl], op=ALU.add),
             reads=[rps, rh], writes=[rh])

    emit_norm(c, h, rh, tsl, T, lay["g_ffn"], hn, rhn, tmp)
    NJ = D_FF // 128
    for j in range(NJ):
        slot, rw = wfetch(c, lay["p_fi"], j)
        psg, rpsg = next_ps(c)
        psu, rpsu = next_ps(c)

        def mm(e, slot=slot, psg=psg, psu=psu):
            r = None
            for kc in range(KD):
                r = e.matmul(psg[:, 0:T], lhsT=slot[:, kc, 0:128], rhs=hn[:, kc, 0:T], start=(kc == 0), stop=(kc == KD - 1))
            for kc in range(KD):
                r = e.matmul(psu[:, 0:T], lhsT=slot[:, KD + kc, 0:128], rhs=hn[:, kc, 0:T], start=(kc == 0), stop=(kc == KD - 1))
            return r
        P.op("pe", mm, reads=[rw, rhn], writes=[rpsg, rpsu])
        sg, rsg = tmp.sq[j % 2], tmp.rsq[j % 2]
        P.op("act", lambda e, psg=psg, sg=sg: e.activation(out=sg[:, 0:T], in_=psg[:, 0:T], func=AF.Silu), reads=[rpsg], writes=[rsg])
        P.op("dve", lambda e, j=j, psu=psu, sg=sg: e.tensor_tensor(out=big[:, j, 0:T], in0=psu[:, 0:T], in1=sg[:, 0:T], op=ALU.mult),
             reads=[rpsu, rsg], writes=[rbig])
    for mo in range(KD):
        ps, rps = next_ps(c)
        for half in range(2):
            slot, rw = wfetch(c, lay["p_fo"], mo * 2 + half)

            def mm(e, slot=slot, ps=ps, half=half):
                r = None
                for kc in range(22):
                    r = e.matmul(ps[:, 0:T], lhsT=slot[:, kc, 0:128], rhs=big[:, half * 22 + kc, 0:T],
                                 start=(half == 0 and kc == 0), stop=(half == 1 and kc == 21))
                return r
            P.op("pe", mm, reads=[rw, rbig], writes=[rps])
        P.op("dve", lambda e, mo=mo, ps=ps: e.tensor_tensor(out=h[:, mo, tsl], in0=ps[:, 0:T], in1=h[:, mo, tsl], op=ALU.add),
             reads=[rps, rh], writes=[rh])


def emit_lru(c, S, TT, hn_src, wx_ap, wg_ap, ga_ap, gx_ap, lvec_ap, y_dst, tile_done=None, rhn_d=(), ry_d=(), after_setup=None):
    P, nc = c.P, c.nc
    NT = S // TT
    with ExitStack() as es2:
        cnt = [0]
        u = _uid()

        def sb2(name, shape, dt=F32):
            cnt[0] += 1
            return es2.enter_context(nc.sbuf_tensor("sl%d_%s_%d" % (u, name, cnt[0]), shape, dt))
        Wx = sb2("Wx", [128, KD, 704], BF16)
        Wg = sb2("Wg", [128, KD, 704], BF16)
        Ga = sb2("Ga", [88, 4, 2, 176], BF16)
        Gx = sb2("Gx", [88, 4, 2, 176], BF16)
        lv = sb2("lv", [88, 8, 8], F32)
        nsp = sb2("nsp", [88, 8], F32)
        rWl = []
        ch = P.chan("lruw")

        def ld(queue, fn):
            r = Res("lw")
            rWl.append(r)
            P.dma(queue, ch, fn, writes=[r])
        for q in range(4):
            ld("pool", lambda e, q=q: e.dma_start(
                out=Wx[:, q * 4:(q + 1) * 4, :], in_=wx_ap[q * 512:(q + 1) * 512, :].rearrange("(kc p) m -> p kc m", p=128)))
            ld("pool", lambda e, q=q: e.dma_start(
                out=Wg[:, q * 4:(q + 1) * 4, :], in_=wg_ap[q * 512:(q + 1) * 512, :].rearrange("(kc p) m -> p kc m", p=128)))
        ld("pool", lambda e: e.dma_start(out=Ga[:], in_=ga_ap.rearrange("b (kh p) n -> p b kh n", p=88)))
        ld("pool", lambda e: e.dma_start(out=Gx[:], in_=gx_ap.rearrange("b (kh p) n -> p b kh n", p=88)))
        ld("sp", lambda e: e.dma_start(out=lv[:], in_=lvec_ap))
        if after_setup is not None:
            after_setup()
        rW = Res("lruW")
        P.op("act", lambda e: e.activation(out=nsp[:], in_=lv[:, :, 7], func=AF.Exp, scale=-1.0), reads=rWl, writes=[rW])
        P.op("act", lambda e: e.activation(out=nsp[:], in_=nsp[:], func=AF.Ln, bias=1.0), reads=[rW], writes=[rW])
        P.op("dve", lambda e: e.tensor_scalar(out=nsp[:], in0=nsp[:], scalar1=-8.0, scalar2=None, op0=ALU.mult), reads=[rW], writes=[rW])

        hnb = [sb2("hnb", [128, KD, TT], BF16) for _ in range(2)]
        rhnb = [Res("hnb0"), Res("hnb1")]
        hch = [P.chan("hch0"), P.chan("hch1")]
        xpad = sb2("xpad", [88, 8, TT + 3], F32)
        rxpad = [Res("xpad%d" % i) for i in range(8)]
        xc = sb2("xc", [88, 8, TT], F32)
        rxc = [Res("xc%d" % i) for i in range(8)]
        xcb = sb2("xcb", [88, 8, TT], BF16)
        rxcb = [Res("xcb%d" % i) for i in range(8)]
        hst = sb2("hst", [88, 8], F32)
        rhst = [Res("hst%d" % i) for i in range(8)]
        P.op("dve", lambda e: e.memset(xpad[:], 0.0), writes=rxpad)
        P.op("dve", lambda e: e.memset(hst[:], 0.0), writes=rhst)
        roles = ["bx", "hs", "gl"]
        tb = {r: [sb2(r, [88, TT], F32) for _ in range(2)] for r in roles}
        tr = {r: [Res(r + "0"), Res(r + "1")] for r in roles}
        tA = sb2("tA", [88, TT], F32); rtA = Res("tA")
        tI = sb2("tI", [88, 4, TT], F32); rtI = [Res("tI%d" % i) for i in range(4)]
        aB = sb2("aB", [88, 4, TT], F32); raB = [Res("aB%d" % i) for i in range(4)]
        sB = sb2("sB", [88, 4, TT], F32); rsB = Res("sB")
        hb2 = sb2("hb2", [88, 8, 2], F32)
        nsph = sb2("nsph", [88, 8], F32)
        P.op("dve", lambda e: e.tensor_scalar(out=hb2[:], in0=lv[:, :, 5:7], scalar1=0.5, scalar2=None, op0=ALU.mult), reads=rWl, writes=[rW])
        P.op("dve", lambda e: e.tensor_scalar(out=nsph[:], in0=nsp[:], scalar1=0.5, scalar2=None, op0=ALU.mult), reads=[rW], writes=[rW])
        ytl = sb2("yt", [88, 8, TT], BF16)
        ryt = [Res("yt%d" % i) for i in range(8)]
        ych = [P.chan("ych0"), P.chan("ych1")]

        def lvs(cc, k):
            return lv[:, cc, k:k + 1]

        def load_hn(t):
            hb_, rhb_ = hnb[t % 2], rhnb[t % 2]
            P.dma("sp", hch[t % 2], lambda e, hb_=hb_, t=t: e.dma_start(out=hb_[:], in_=hn_src(t)), reads=list(rhn_d), writes=[rhb_])
        load_hn(0)
        for t in range(NT):
            hb, rhb = hnb[t % 2], rhnb[t % 2]
            tsl = slice(t * TT, (t + 1) * TT)
            if t + 1 < NT:
                load_hn(t + 1)
            for cc in range(8):
                ps, rps = next_ps(c)

                def mm(e, ps=ps, cc=cc, hb=hb):
                    r = None
                    for kc in range(KD):
                        r = e.matmul(ps[0:88, 0:TT], lhsT=Wx[:, kc, cc * 88:(cc + 1) * 88], rhs=hb[:, kc, :], start=(kc == 0), stop=(kc == KD - 1))
                    return r
                P.op("pe", mm, reads=[rW, rhb], writes=[rps])
                P.op("act", lambda e, ps=ps, cc=cc: e.copy(out=xpad[:, cc, 3:3 + TT], in_=ps[0:88, 0:TT]), reads=[rps], writes=[rxpad[cc]])
                P.op("dve", lambda e, cc=cc: e.tensor_scalar(out=xc[:, cc, :], in0=xpad[:, cc, 0:TT], scalar1=lvs(cc, 0), scalar2=lvs(cc, 4),
                                                             op0=ALU.mult, op1=ALU.add), reads=[rxpad[cc], rW], writes=[rxc[cc]])
                for k in range(1, 4):
                    P.op("dve", lambda e, cc=cc, k=k: e.scalar_tensor_tensor(out=xc[:, cc, :], in0=xpad[:, cc, k:k + TT], scalar=lvs(cc, k),
                                                                             in1=xc[:, cc, :], op0=ALU.mult, op1=ALU.add),
                         reads=[rxpad[cc], rxc[cc], rW], writes=[rxc[cc]])
                P.op("dve", lambda e, cc=cc: e.tensor_copy(out=xpad[:, cc, 0:3], in_=xpad[:, cc, TT:TT + 3]), reads=[rxpad[cc]], writes=[rxpad[cc]])
                P.op("act", lambda e, cc=cc: e.copy(out=xcb[:, cc, :], in_=xc[:, cc, :]), reads=[rxc[cc]], writes=[rxcb[cc]])
            for g4 in range(2):
                for cc in range(g4 * 4, g4 * 4 + 4):
                    bl, oh = cc // 2, cc % 2
                    ci4 = cc % 4
                    for (G, dstt, rdst_, bcol) in ((Ga, tA, rtA, 0), (Gx, tI, rtI[ci4], 1)):
                        ps, rps = next_ps(c, 4, 7)

                        def mmg(e, ps=ps, G=G, bl=bl, oh=oh):
                            r = None
                            for kh in range(2):
                                r = e.matmul(ps[0:88, 0:TT], lhsT=G[:, bl, kh, oh * 88:(oh + 1) * 88], rhs=xcb[:, 2 * bl + kh, :], start=(kh == 0), stop=(kh == 1))
                            return r
                        P.op("pe", mmg, reads=[rW, rxcb[2 * bl], rxcb[2 * bl + 1]], writes=[rps])
                        dst_ap = tA[:] if bcol == 0 else tI[:, ci4, :]
                        P.op("act", lambda e, ps=ps, dst_ap=dst_ap, bcol=bcol, cc=cc: e.activation(out=dst_ap, in_=ps[0:88, 0:TT], func=AF.Tanh, bias=hb2[:, cc, bcol:bcol + 1], scale=0.5),
                             reads=[rps, rW], writes=[rdst_])
                    P.op("act", lambda e, cc=cc, ci4=ci4: e.activation(out=aB[:, ci4, :], in_=tA[:], func=AF.Exp, scale=nsph[:, cc:cc + 1], bias=nsph[:, cc:cc + 1]),
                         reads=[rtA, rW], writes=[raB[ci4]])
                    P.op("act", lambda e, ci4=ci4: e.activation(out=sB[:, ci4, :], in_=aB[:, ci4, :], func=AF.Square), reads=[raB[ci4]], writes=[rsB])
                P.op("act", lambda e: e.activation(out=sB[:].rearrange("p c t -> p (c t)"), in_=sB[:].rearrange("p c t -> p (c t)"), func=AF.Sqrt, bias=0.25, scale=-0.25),
                     reads=[rsB], writes=[rsB])
                for cc in range(g4 * 4, g4 * 4 + 4):
                    ci4 = cc % 4
                    pb = cc % 2
                    B = {r: tb[r][pb] for r in roles}
                    R = {r: tr[r][pb] for r in roles}
                    P.op("dve", lambda e, cc=cc, ci4=ci4, B=B: e.scalar_tensor_tensor(out=B["bx"][:], in0=tI[:, ci4, :], scalar=1.0, in1=xc[:, cc, :], op0=ALU.add, op1=ALU.mult),
                         reads=[rtI[ci4], rxc[cc]], writes=[R["bx"]])
                    P.op("dve", lambda e, ci4=ci4, B=B: e.tensor_tensor(out=B["bx"][:], in0=B["bx"][:], in1=sB[:, ci4, :], op=ALU.mult), reads=[R["bx"], rsB], writes=[R["bx"]])
                    P.op("dve", lambda e, cc=cc, ci4=ci4, B=B: e.tensor_tensor_scan(out=B["hs"][:], data0=aB[:, ci4, :], data1=B["bx"][:], initial=hst[:, cc:cc + 1],
                                                                                     op0=ALU.mult, op1=ALU.add), reads=[raB[ci4], R["bx"], rhst[cc]], writes=[R["hs"]])
                    P.op("dve", lambda e, cc=cc, B=B: e.tensor_copy(out=hst[:, cc:cc + 1], in_=B["hs"][:, TT - 1:TT]), reads=[R["hs"]], writes=[rhst[cc]])
                    ps, rps = next_ps(c)

                    def mmgb(e, ps=ps, cc=cc, hb=hb):
                        r = None
                        for kc in range(KD):
                            r = e.matmul(ps[0:88, 0:TT], lhsT=Wg[:, kc, cc * 88:(cc + 1) * 88], rhs=hb[:, kc, :], start=(kc == 0), stop=(kc == KD - 1))
                        return r
                    P.op("pe", mmgb, reads=[rW, rhb], writes=[rps])
                    P.op("act", lambda e, ps=ps, B=B: e.activation(out=B["gl"][:], in_=ps[0:88, 0:TT], func=AF.Square), reads=[rps], writes=[R["gl"]])
                    P.op("dve", lambda e, B=B: e.tensor_scalar(out=B["gl"][:], in0=B["gl"][:], scalar1=0.044715, scalar2=1.0, op0=ALU.mult, op1=ALU.add), reads=[R["gl"]], writes=[R["gl"]])
                    P.op("dve", lambda e, ps=ps, B=B: e.tensor_tensor(out=B["gl"][:], in0=ps[0:88, 0:TT], in1=B["gl"][:], op=ALU.mult), reads=[rps, R["gl"]], writes=[R["gl"]])
                    P.op("act", lambda e, B=B: e.activation(out=B["gl"][:], in_=B["gl"][:], func=AF.Tanh, scale=0.7978845608028654), reads=[R["gl"]], writes=[R["gl"]])
                    P.op("dve", lambda e, ps=ps, B=B: e.scalar_tensor_tensor(out=B["gl"][:], in0=B["gl"][:], scalar=1.0, in1=ps[0:88, 0:TT], op0=ALU.add, op1=ALU.mult),
                         reads=[rps, R["gl"]], writes=[R["gl"]])
                    P.op("dve", lambda e, B=B, cc=cc: e.scalar_tensor_tensor(out=ytl[:, cc, :], in0=B["hs"][:], scalar=0.5, in1=B["gl"][:], op0=ALU.mult, op1=ALU.mult),
                         reads=[R["hs"], R["gl"]], writes=[ryt[cc]])
                    P.dma("sp", ych[pb], lambda e, cc=cc, t=t: e.dma_start(out=y_dst(t, cc * 88, (cc + 1) * 88), in_=ytl[:, cc, :]), reads=[ryt[cc]],
                          writes=list(ry_d(t)) if callable(ry_d) else [])
                    if c.progress is not None:
                        c.progress([ryt[cc]])
            if tile_done is not None:
                tile_done(t, ryt[7])
        P.barrier()


def _pass_bufs(c, es2, T, tag):
    nc = c.nc
    sb2 = lambda name, shape, dt=F32: es2.enter_context(nc.sbuf_tensor("sp_%s_%s" % (name, tag), shape, dt))
    c.hn = sb2("hn", [128, KD, T], BF16); c.rhn = Res("hn")
    c.big = sb2("big", [128, 48, T], BF16); c.rbig = Res("big")
    c.qh = sb2("qh", [128, 4, T], BF16); c.rqh = Res("qh")
    c.ex = sb2("ex", [128, 2, T], BF16); c.rex = Res("ex")
    c.rden = sb2("rden", [128, T], F32); c.rrden = Res("rden")
    t = Tmp()
    t.sq = [sb2("sq%d" % i, [128, T], F32) for i in range(2)]
    t.rsq = [Res("sq%d" % i) for i in range(2)]
    t.rstd = sb2("rstd", [128, T], F32)
    t.rrstd = Res("rstd")
    c.tmp = t


def build_tok_program(Sc, T, kind, first, final, do_main=True):
    nc = bass.Bass("TRN2", target_bir_lowering=False)
    dt = lambda name, shape, dtype, kind_: nc.dram_tensor(name, shape, dtype, kind=kind_).ap()
    hin = dt("hin", [D, Sc], F32, "ExternalInput")
    vecs = dt("vecs", [128, 64], F32, "ExternalInput")
    nyc, rpc_y = (32, 88) if kind == "lru" else (32, 128)
    if do_main:
        y = dt("y", [nyc * rpc_y, Sc], BF16, "ExternalInput")
        mem = dt("mem", [MEM, D], F32, "ExternalInput")
        w_xq = dt("w_xq", [D, D], F32, "ExternalInput")
        w_kv = dt("w_kv", [D, 2 * D], F32, "ExternalInput")
        w_out = dt("w_out", [nyc * rpc_y + D, D], F32, "ExternalInput")
        w_fi = dt("w_fi", [D, 2 * D_FF], F32, "ExternalInput")
        w_fo = dt("w_fo", [D_FF, D], F32, "ExternalInput")
    if final:
        outd = dt("out", [D, Sc], F32, "ExternalOutput")
    else:
        hout = dt("hout", [D, Sc], F32, "ExternalOutput")
        hnn = dt("hnn", [D, Sc], BF16, "ExternalOutput")
    with ExitStack() as es:
        P = Prog(nc, es)
        c = make_ctx(nc, es, P)
        c.vecs = c.sb("vecs", [128, 64], F32)
        c.r_vecs = Res("vecs")
        c.h = c.sb("h", [128, KD, Sc], F32)
        c.rh = Res("h")
        c.KT = c.sb("KT", [128, KD, MEM], BF16); c.rKT = Res("KT")
        c.V = c.sb("V", [128, 2, D], BF16); c.rV = Res("V")
        c.ych = P.chan()
        c.och = P.chan()
        P.dma("sp", P.chan(), lambda e: e.dma_start(out=c.vecs[:], in_=vecs), writes=[c.r_vecs])
        P.dma("sp", P.chan(), lambda e: e.dma_start(out=c.h[:], in_=hin.rearrange("(kc p) t -> p kc t", p=128)), writes=[c.rh])
        lay = dict(g_mix=0, g_mem=16, g_ffn=32, g_next=48, nyc=nyc, rpc_y=rpc_y)
        if do_main:
            lay.update(y_ap=lambda tsl, e: y[:, tsl].rearrange("(kc p) t -> p kc t", p=rpc_y))
            lay.update(prep_layer_weights(c, w_xq, w_kv, w_out, w_fi, w_fo, nyc, rpc_y))
            emit_kv(c, lay, mem, lay["p_kv"], lay["g_mem"])
        with ExitStack() as es2:
            _pass_bufs(c, es2, T, "p")
            for p0 in range(0, Sc, T):
                tsl = slice(p0, p0 + T)
                emit_tok_pass(c, lay, tsl, T,
                              hn_dst=None if final else (lambda tsl: P.dma("sp", c.och, lambda e: e.dma_start(
                                  out=hnn[:, tsl].rearrange("(kc p) t -> p kc t", p=128), in_=c.hn[:, :, 0:T]), reads=[c.rhn])),
                              out_dst=(lambda tsl: outd[:, tsl].rearrange("(kc p) t -> p kc t", p=128)) if final else None,
                              do_main=do_main)
            if not final:
                P.dma("sp", c.och, lambda e: e.dma_start(out=hout.rearrange("(kc p) t -> p kc t", p=128), in_=c.h[:]), reads=[c.rh])
            P.barrier()
        P.emit()
    return nc


def build_lru_program(S, TT):
    nc = bass.Bass("TRN2", target_bir_lowering=False)
    dt = lambda name, shape, dtype, kind_: nc.dram_tensor(name, shape, dtype, kind=kind_).ap()
    hn = dt("hn", [D, S], BF16, "ExternalInput")
    wx = dt("wx", [D, 704], F32, "ExternalInput")
    wg = dt("wg", [D, 704], F32, "ExternalInput")
    ga = dt("ga", [4, 176, 176], F32, "ExternalInput")
    gx = dt("gx", [4, 176, 176], F32, "ExternalInput")
    lvec = dt("lvec", [88, 8, 8], F32, "ExternalInput")
    y = dt("y", [704, S], BF16, "ExternalOutput")
    with ExitStack() as es:
        P = Prog(nc, es)
        c = make_ctx(nc, es, P, nslots=1)
        emit_lru(c, S, TT, lambda t: hn[:, t * TT:(t + 1) * TT].rearrange("(kc p) t -> p kc t", p=128), wx, wg, ga, gx, lvec,
                 lambda t, r0, r1: y[r0:r1, t * TT:(t + 1) * TT])
        P.emit()
    return nc


def emit_gdn(c, S, TT, hn_src, p_w, cvec_ap, gvec_ap, pvec_ap, y_dst, tile_done=None, rhn_d=(), ry_d=()):
    P, nc = c.P, c.nc
    NT = S // TT
    NCH = TT // 64
    C = 64
    H = 8
    with ExitStack() as es2:
        cnt = [0]
        u = _uid()

        def sb2(name, shape, dt=F32):
            cnt[0] += 1
            return es2.enter_context(nc.sbuf_tensor("sg%d_%s_%d" % (u, name, cnt[0]), shape, dt))
        rK = Res("gconst")
        sel = sb2("sel", [8, H, 128])
        nsel = sb2("nsel", [8, H, 128])
        mB = sb2("mB", [C, H, C])
        mBT = sb2("mBT", [C, H, C])
        mAtt = sb2("mAtt", [C, H, C])
        eye8 = sb2("eye8", [C, H, C])
        lastsel = sb2("lastsel", [C, C])
        cmask = sb2("cmask", [8, TT])

        gk = c.gdn_consts
        for nm, tile_ in (("sel", sel), ("nsel", nsel), ("mB", mB), ("mBT", mBT), ("mAtt", mAtt), ("eye8", eye8)):
            P.dma("sp", P.chan("gk_" + nm), lambda e, nm=nm, tile_=tile_: e.dma_start(out=tile_[:].rearrange("p h c -> p (h c)"), in_=gk[nm]), reads=[c.r_gk], writes=[rK])
        P.dma("sp", P.chan("gk_lastsel"), lambda e: e.dma_start(out=lastsel[:], in_=gk["lastsel"]), reads=[c.r_gk], writes=[rK])
        P.dma("sp", P.chan("gk_cmask"), lambda e: e.dma_start(out=cmask[:], in_=gk["cmask"]), reads=[c.r_gk], writes=[rK])
        cv = sb2("cv", [128, 16, 4])
        gn = sb2("gn", [128, 1])
        pv = sb2("pv", [8, 2])
        nalog = sb2("nalog", [8, 1])
        rcv = Res("cv")
        P.dma("sp", P.chan("gcv"), lambda e: e.dma_start(out=cv[:], in_=cvec_ap), writes=[rcv])
        rgn = Res("gn")
        P.dma("sp", P.chan("ggn"), lambda e: e.dma_start(out=gn[:], in_=gvec_ap), writes=[rgn])
        rpv = Res("pv")
        P.dma("sp", P.chan("gpv"), lambda e: e.dma_start(out=pv[:], in_=pvec_ap), writes=[rpv])
        P.op("act", lambda e: e.activation(out=nalog[:], in_=pv[:, 0:1], func=AF.Exp), reads=[rpv], writes=[rpv])
        P.op("dve", lambda e: e.tensor_scalar(out=nalog[:], in0=nalog[:], scalar1=-1.0, scalar2=None, op0=ALU.mult), reads=[rpv], writes=[rpv])

        hnbs = [sb2("hnb", [128, KD, TT], BF16) for _ in range(2)]; rhnbs = [Res("hnb0"), Res("hnb1")]; hchs = [P.chan("hch0"), P.chan("hch1")]
        hnb, rhnb = hnbs[0], rhnbs[0]
        carry = sb2("carry", [128, 16, 3]); rcar = [Res("car%d" % i) for i in range(16)]
        P.op("dve", lambda e: e.memset(carry[:], 0.0), writes=rcar)
        pad = [sb2("pad", [128, TT + 3]) for _ in range(2)]; rpad = [Res("pad0"), Res("pad1")]
        xcv = [sb2("xcv", [128, TT]) for _ in range(2)]; rxcv = [Res("xcv0"), Res("xcv1")]
        sqb = [sb2("sqb", [128, TT]) for _ in range(2)]; rsqb = [Res("sqb0"), Res("sqb1")]
        rsb = [sb2("rsb", [128, TT]) for _ in range(2)]; rrsb = [Res("rsb0"), Res("rsb1")]
        qT = sb2("qT", [128, 4, TT], BF16); rqT = [Res("qT%d" % i) for i in range(4)]
        kT = sb2("kT", [128, 4, TT], BF16); rkT = [Res("kT%d" % i) for i in range(4)]
        vT = sb2("vT", [128, H, TT], BF16); rvT = [Res("vT%d" % i) for i in range(H)]
        o_t = sb2("o_t", [128, H, TT]); ro_t = Res("o_t")
        betaT = sb2("betaT", [8, TT]); rbetaT = Res("betaT")
        lbT = sb2("lbT", [8, TT]); rlbT = Res("lbT")
        spT = sb2("spT", [8, TT]); rspT = Res("spT")
        gcT = sb2("gcT", [8, TT]); rgcT = Res("gcT")
        g2T = sb2("g2T", [8, TT]); rg2T = Res("g2T")
        Sst = sb2("Sst", [128, H, 128]); rS = Res("S")
        P.op("dve", lambda e: e.memset(Sst[:], 0.0), writes=[rS])
        colt = sb2("colt", [C, 16]); rcolt = Res("colt")
        cbg = sb2("cbg", [C, H]); rcbg = Res("cbg")
        ksc = sb2("ksc", [C, H]); rksc = Res("ksc")
        E = sb2("E", [128, H, C]); rE = Res("E")
        qd = sb2("qd", [128, H, C]); rqd = Res("qd")
        tt_ = [sb2("tt", [C, H, C]) for _ in range(3)]; rtt = [Res("tt%d" % i) for i in range(3)]
        Pm = [sb2("Pm", [C, H, C], BF16) for _ in range(2)]; rPm = [Res("Pm0"), Res("Pm1")]
        PTm = [sb2("PTm", [C, H, C], BF16) for _ in range(2)]; rPTm = [Res("PTm0"), Res("PTm1")]
        Um = [sb2("Um", [C, H, C], BF16) for _ in range(2)]; rUm = [Res("Um0"), Res("Um1")]
        attT = sb2("attT", [C, H, C], BF16); rattT = Res("attT")
        kbg = sb2("kbg", [C, H, 128], BF16); rkbg = Res("kbg")
        kst = sb2("kst", [C, H, 128], BF16); rkst = Res("kst")
        vb = sb2("vb", [C, H, 128], BF16); rvb = Res("vb")
        wv = sb2("wv", [C, H, 128]); rwv = Res("wv")
        kcT = sb2("kcT", [128, H, C]); rkcT = Res("kcT")
        vnew = sb2("vnew", [C, H, 128], BF16); rvnew = Res("vnew")
        identb = sb2("identb", [128, 128], BF16)
        P.op("act", lambda e: e.copy(out=identb[:], in_=c.ident[:]), reads=[c.r_const], writes=[rK])
        ytg = sb2("yt", [128, H, TT], BF16); ryt = [Res("yt%d" % i) for i in range(H)]
        ych = [P.chan("ych0"), P.chan("ych1")]
        ps, psr = c.ps, c.psr
        pn = [0]

        def bank():
            i = pn[0] % 8
            pn[0] += 1
            return ps[i], psr[i]

        hcur = {"hnb": hnbs[0], "rhnb": rhnbs[0]}

        def inproj(col0, M, TTn):
            slot, rw = wfetch(c, p_w, col0 // 128)
            pb, rpb = bank()
            hnb, rhnb = hcur["hnb"], hcur["rhnb"]

            def mm(e, slot=slot, pb=pb, hnb=hnb):
                r = None
                for kc in range(KD):
                    r = e.matmul(pb[0:M, 0:TTn], lhsT=slot[:, kc, 0:M], rhs=hnb[:, kc, :], start=(kc == 0), stop=(kc == KD - 1))
                return r
            P.op("pe", mm, reads=[rw, rhnb], writes=[rpb])
            return pb, rpb

        def load_hn(t):
            hb_, rhb_ = hnbs[t % 2], rhnbs[t % 2]
            P.dma("sp", hchs[t % 2], lambda e, hb_=hb_, t=t: e.dma_start(out=hb_[:], in_=hn_src(t)), reads=list(rhn_d), writes=[rhb_])
        load_hn(0)
        for t in range(NT):
            tsl = slice(t * TT, (t + 1) * TT)
            hnb, rhnb = hnbs[t % 2], rhnbs[t % 2]
            hcur["hnb"], hcur["rhnb"] = hnb, rhnb
            if t + 1 < NT:
                load_hn(t + 1)
            slot, rw = wfetch(c, p_w, 24)
            pbb, rpbb = bank()
            pba, rpba = bank()

            def mmba(e, slot=slot, pbb=pbb, pba=pba, hnb=hnb):
                r = None
                for kc in range(KD):
                    r = e.matmul(pbb[0:8, 0:TT], lhsT=slot[:, kc, 0:8], rhs=hnb[:, kc, :], start=(kc == 0), stop=(kc == KD - 1))
                for kc in range(KD):
                    r = e.matmul(pba[0:8, 0:TT], lhsT=slot[:, kc, 8:16], rhs=hnb[:, kc, :], start=(kc == 0), stop=(kc == KD - 1))
                return r
            P.op("pe", mmba, reads=[rw, rhnb], writes=[rpbb, rpba])
            P.op("act", lambda e, pbb=pbb: e.activation(out=betaT[:], in_=pbb[0:8, 0:TT], func=AF.Sigmoid), reads=[rpbb], writes=[rbetaT])
            P.op("act", lambda e, pbb=pbb: e.activation(out=lbT[:], in_=pbb[0:8, 0:TT], func=AF.Exp, scale=-1.0), reads=[rpbb], writes=[rlbT])
            P.op("act", lambda e: e.activation(out=lbT[:], in_=lbT[:], func=AF.Ln, bias=1.0), reads=[rlbT], writes=[rlbT])
            P.op("act", lambda e, pba=pba: e.activation(out=spT[:], in_=pba[0:8, 0:TT], func=AF.Exp, bias=pv[:, 1:2]), reads=[rpba, rpv], writes=[rspT])
            P.op("act", lambda e: e.activation(out=spT[:], in_=spT[:], func=AF.Ln, bias=1.0), reads=[rspT], writes=[rspT])
            P.op("dve", lambda e: e.tensor_scalar(out=spT[:], in0=spT[:], scalar1=nalog[:, 0:1], scalar2=None, op0=ALU.mult), reads=[rspT, rpv], writes=[rspT])
            P.op("dve", lambda e: e.tensor_tensor_scan(out=gcT[:], data0=cmask[:], data1=spT[:], initial=0.0, op0=ALU.mult, op1=ALU.add),
                 reads=[rspT, rK], writes=[rgcT])
            P.op("dve", lambda e: e.tensor_tensor(out=g2T[:], in0=gcT[:], in1=lbT[:], op=ALU.subtract), reads=[rgcT, rlbT], writes=[rg2T])
            for f in range(16):
                pb, rpb = inproj(f * 128, 128, TT)
                pd, rpd = pad[f % 2], rpad[f % 2]
                xc_, rxc_ = xcv[f % 2], rxcv[f % 2]
                P.op("act", lambda e, pb=pb, pd=pd: e.copy(out=pd[:, 3:3 + TT], in_=pb[:, 0:TT]), reads=[rpb], writes=[rpd])
                P.op("dve", lambda e, pd=pd, f=f: e.tensor_copy(out=pd[:, 0:3], in_=carry[:, f, :]), reads=[rcar[f], rpd], writes=[rpd])
                P.op("dve", lambda e, pd=pd, xc_=xc_, f=f: e.tensor_scalar(out=xc_[:], in0=pd[:, 0:TT], scalar1=cv[:, f, 0:1], scalar2=None, op0=ALU.mult),
                     reads=[rpd, rcv], writes=[rxc_])
                for k in range(1, 4):
                    P.op("dve", lambda e, pd=pd, xc_=xc_, f=f, k=k: e.scalar_tensor_tensor(out=xc_[:], in0=pd[:, k:k + TT], scalar=cv[:, f, k:k + 1], in1=xc_[:],
                                                                                           op0=ALU.mult, op1=ALU.add), reads=[rpd, rxc_, rcv], writes=[rxc_])
                P.op("dve", lambda e, pd=pd, f=f: e.tensor_copy(out=carry[:, f, :], in_=pd[:, TT:TT + 3]), reads=[rpd], writes=[rcar[f]])
                if f >= 8:
                    hv = f - 8
                    P.op("act", lambda e, xc_=xc_, hv=hv: e.activation(out=vT[:, hv, :], in_=xc_[:], func=AF.Silu), reads=[rxc_], writes=[rvT[hv]])
                else:
                    dstT, rdst, hq = (qT, rqT, f) if f < 4 else (kT, rkT, f - 4)
                    qscale = float(128 ** -0.5) if f < 4 else 1.0
                    P.op("act", lambda e, xc_=xc_: e.activation(out=xc_[:], in_=xc_[:], func=AF.Silu), reads=[rxc_], writes=[rxc_])
                    sq_, rsq_ = sqb[f % 2], rsqb[f % 2]
                    rs_, rrs_ = rsb[f % 2], rrsb[f % 2]
                    P.op("act", lambda e, xc_=xc_, sq_=sq_: e.activation(out=sq_[:], in_=xc_[:], func=AF.Square), reads=[rxc_], writes=[rsq_])
                    pn_, rpn_ = bank()
                    P.op("pe", lambda e, sq_=sq_, pn_=pn_: e.matmul(pn_[:, 0:TT], lhsT=c.ones_f[:], rhs=sq_[:], start=True, stop=True), reads=[rsq_, c.r_const], writes=[rpn_])
                    P.op("act", lambda e, rs_=rs_, pn_=pn_: e.activation(out=rs_[:], in_=pn_[:, 0:TT], func=AF.Sqrt, bias=EPS), reads=[rpn_], writes=[rrs_])
                    P.op("dve", lambda e, rs_=rs_: e.reciprocal(out=rs_[:], in_=rs_[:]), reads=[rrs_], writes=[rrs_])
                    P.op("dve", lambda e, xc_=xc_, rs_=rs_, dstT=dstT, hq=hq, qscale=qscale: e.scalar_tensor_tensor(
                        out=dstT[:, hq, :], in0=xc_[:], scalar=qscale, in1=rs_[:], op0=ALU.mult, op1=ALU.mult), reads=[rxc_, rrs_], writes=[rdst[hq]])
            for ci in range(NCH):
                cs = slice(ci * C, (ci + 1) * C)
                pc, rpc = bank()

                def trc(e, pc=pc, cs=cs):
                    e.transpose(pc[0:C, 0:8], gcT[0:8, cs], c.ident[0:8, 0:8])
                    return e.transpose(pc[0:C, 8:16], betaT[0:8, cs], c.ident[0:8, 0:8])
                P.op("pe", trc, reads=[rgcT, rbetaT, c.r_const], writes=[rpc])
                P.op("act", lambda e, pc=pc: e.copy(out=colt[:], in_=pc[0:C, 0:16]), reads=[rpc], writes=[rcolt])
                pl, rpl = bank()
                P.op("pe", lambda e, pl=pl: e.matmul(pl[0:C, 0:8], lhsT=lastsel[:], rhs=colt[:, 0:8], start=True, stop=True), reads=[rcolt, rK], writes=[rpl])
                P.op("act", lambda e: e.activation(out=cbg[:], in_=colt[:, 0:8], func=AF.Exp), reads=[rcolt], writes=[rcbg])
                P.op("dve", lambda e: e.tensor_tensor(out=cbg[:], in0=cbg[:], in1=colt[:, 8:16], op=ALU.mult), reads=[rcbg, rcolt], writes=[rcbg])
                P.op("dve", lambda e, pl=pl: e.tensor_tensor(out=ksc[:], in0=pl[0:C, 0:8], in1=colt[:, 0:8], op=ALU.subtract), reads=[rpl, rcolt], writes=[rksc])
                P.op("act", lambda e: e.activation(out=ksc[:], in_=ksc[:], func=AF.Exp), reads=[rksc], writes=[rksc])
                pe_, rpe_ = bank()

                def mmE(e, pe_=pe_, cs=cs):
                    r = None
                    for h in range(H):
                        r = e.matmul(pe_[:, h * C:(h + 1) * C], lhsT=sel[:, h, :], rhs=gcT[0:8, cs], start=True, stop=True)
                    return r
                P.op("pe", mmE, reads=[rgcT, rK], writes=[rpe_])
                P.op("act", lambda e, pe_=pe_: e.activation(out=E[:].rearrange("p h c -> p (h c)"), in_=pe_[:, 0:H * C], func=AF.Exp), reads=[rpe_], writes=[rE])
                for rep in range(2):
                    P.op("dve", lambda e, rep=rep, cs=cs: e.tensor_tensor(
                        out=qd[:].rearrange("p (q r) c -> p q r c", r=2)[:, :, rep, :], in0=qT[:, :, cs],
                        in1=E[:].rearrange("p (q r) c -> p q r c", r=2)[:, :, rep, :], op=ALU.mult), reads=rqT + [rE], writes=[rqd])
                pkk, rpkk = bank()
                pqk, rpqk = bank()

                def mmkk(e, pkk=pkk, pqk=pqk, cs=cs):
                    r = None
                    for h in range(H):
                        r = e.matmul(pkk[0:C, h * C:(h + 1) * C], lhsT=kT[:, h // 2, cs], rhs=kT[:, h // 2, cs], start=True, stop=True)
                    for h in range(H):
                        r = e.matmul(pqk[0:C, h * C:(h + 1) * C], lhsT=kT[:, h // 2, cs], rhs=qT[:, h // 2, cs], start=True, stop=True)
                    return r
                P.op("pe", mmkk, reads=rkT + rqT, writes=[rpkk, rpqk])
                pd1, rpd1 = bank()
                pd2, rpd2 = bank()
                pd3, rpd3 = bank()

                def mmd(e, pd1=pd1, pd2=pd2, pd3=pd3, cs=cs):
                    r = None
                    for h in range(H):
                        hs = slice(h * C, (h + 1) * C)
                        e.matmul(pd1[0:C, hs], lhsT=g2T[0:8, cs], rhs=sel[:, h, 0:C], start=True, stop=False)
                        e.matmul(pd1[0:C, hs], lhsT=nsel[:, h, 0:C], rhs=gcT[0:8, cs], start=False, stop=True)
                        e.matmul(pd2[0:C, hs], lhsT=sel[:, h, 0:C], rhs=g2T[0:8, cs], start=True, stop=False)
                        e.matmul(pd2[0:C, hs], lhsT=gcT[0:8, cs], rhs=nsel[:, h, 0:C], start=False, stop=True)
                        e.matmul(pd3[0:C, hs], lhsT=sel[:, h, 0:C], rhs=gcT[0:8, cs], start=True, stop=False)
                        r = e.matmul(pd3[0:C, hs], lhsT=gcT[0:8, cs], rhs=nsel[:, h, 0:C], start=False, stop=True)
                    return r
                P.op("pe", mmd, reads=[rg2T, rgcT, rK], writes=[rpd1, rpd2, rpd3])
                p0, rp0 = Pm[0], rPm[0]
                pt0, rpt0 = PTm[0], rPTm[0]
                fl = lambda a: a[:].rearrange("p h c -> p (h c)")
                for (pdx, rpdx, tmp_, rtmp_, msk, pmat, rpmat, dst, rdst_) in (
                        (pd1, rpd1, tt_[0], rtt[0], mB, pkk, rpkk, p0, rp0),
                        (pd2, rpd2, tt_[1], rtt[1], mBT, pkk, rpkk, pt0, rpt0),
                        (pd3, rpd3, tt_[2], rtt[2], mAtt, pqk, rpqk, attT, rattT)):
                    P.op("dve", lambda e, pdx=pdx, tmp_=tmp_: e.tensor_scalar(out=fl(tmp_), in0=pdx[0:C, 0:H * C], scalar1=0.0, scalar2=None, op0=ALU.min),
                         reads=[rpdx], writes=[rtmp_])
                    P.op("act", lambda e, tmp_=tmp_: e.activation(out=fl(tmp_), in_=fl(tmp_), func=AF.Exp), reads=[rtmp_], writes=[rtmp_])
                    P.op("dve", lambda e, tmp_=tmp_, msk=msk: e.tensor_tensor(out=fl(tmp_), in0=fl(tmp_), in1=fl(msk), op=ALU.mult), reads=[rtmp_, rK], writes=[rtmp_])
                    P.op("dve", lambda e, tmp_=tmp_, pmat=pmat, dst=dst: e.tensor_tensor(out=fl(dst), in0=pmat[0:C, 0:H * C], in1=fl(tmp_), op=ALU.mult),
                         reads=[rpmat, rtmp_], writes=[rdst_])
                P.op("dve", lambda e: e.tensor_tensor(out=fl(Um[0]), in0=fl(pt0), in1=fl(eye8), op=ALU.add), reads=[rpt0, rK], writes=[rUm[0]])
                pkt, rpkt = bank()

                def trk(e, pkt=pkt, cs=cs):
                    rr = None
                    for hq in range(4):
                        rr = e.matmul(pkt[0:C, hq * 128:(hq + 1) * 128], lhsT=kT[:, hq, cs], rhs=identb[:], start=True, stop=True)
                    return rr
                P.op("pe", trk, reads=rkT + [rK], writes=[rpkt])
                for rep in range(2):
                    kv_ = lambda a: a[:].rearrange("p (q r) d -> p q r d", r=2)[:, :, rep, :]
                    P.op("dve", lambda e, pkt=pkt, rep=rep: e.tensor_tensor(
                        out=kbg[:].rearrange("p (q r) d -> p q r d", r=2)[:, :, rep, :], in0=pkt[0:C, 0:512].rearrange("p (q d) -> p q d", d=128),
                        in1=cbg[:].rearrange("p (q r) -> p q r", r=2)[:, :, rep].unsqueeze(2).to_broadcast([C, 4, 128]), op=ALU.mult),
                        reads=[rpkt, rcbg], writes=[rkbg])
                    P.op("dve", lambda e, pkt=pkt, rep=rep: e.tensor_tensor(
                        out=kst[:].rearrange("p (q r) d -> p q r d", r=2)[:, :, rep, :], in0=pkt[0:C, 0:512].rearrange("p (q d) -> p q d", d=128),
                        in1=ksc[:].rearrange("p (q r) -> p q r", r=2)[:, :, rep].unsqueeze(2).to_broadcast([C, 4, 128]), op=ALU.mult),
                        reads=[rpkt, rksc], writes=[rkst])
                for half in range(2):
                    pvt, rpvt = bank()

                    def trv(e, pvt=pvt, cs=cs, half=half):
                        rr = None
                        for hh in range(4):
                            rr = e.matmul(pvt[0:C, hh * 128:(hh + 1) * 128], lhsT=vT[:, half * 4 + hh, cs], rhs=identb[:], start=True, stop=True)
                        return rr
                    P.op("pe", trv, reads=rvT + [rK], writes=[rpvt])
                    P.op("dve", lambda e, pvt=pvt, half=half: e.tensor_tensor(
                        out=vb[:, half * 4:(half + 1) * 4, :], in0=pvt[0:C, 0:512].rearrange("p (q d) -> p q d", d=128),
                        in1=colt[:, 8 + half * 4:8 + (half + 1) * 4].unsqueeze(2).to_broadcast([C, 4, 128]), op=ALU.mult),
                        reads=[rpvt, rcolt], writes=[rvb])
                cur = 0
                for r in range(0, 6):
                    nxt = 1 - cur
                    Pc, rPc, PTc, rPTc, Uc, rUc = Pm[cur], rPm[cur], PTm[cur], rPTm[cur], Um[cur], rUm[cur]
                    Pn, rPn, PTn, rPTn, Un, rUn = Pm[nxt], rPm[nxt], PTm[nxt], rPTm[nxt], Um[nxt], rUm[nxt]
                    if r >= 1:
                        pu, rpu = bank()

                        def mmu(e, pu=pu, Pc=Pc, Uc=Uc):
                            rr = None
                            for h in range(H):
                                rr = e.matmul(pu[0:C, h * C:(h + 1) * C], lhsT=Pc[:, h, :], rhs=Uc[:, h, :], start=True, stop=True)
                            return rr
                        P.op("pe", mmu, reads=[rPc, rUc], writes=[rpu])
                    if r < 5:
                        pp, rpp = bank()

                        def mmp(e, pp=pp, Pc=Pc, PTc=PTc):
                            rr = None
                            for h in range(H):
                                rr = e.matmul(pp[0:C, h * C:(h + 1) * C], lhsT=PTc[:, h, :], rhs=Pc[:, h, :], start=True, stop=True)
                            return rr
                        P.op("pe", mmp, reads=[rPc, rPTc], writes=[rpp])
                        if r < 4:
                            ppt, rppt = bank()

                            def mmpt(e, ppt=ppt, Pc=Pc, PTc=PTc):
                                rr = None
                                for h in range(H):
                                    rr = e.matmul(ppt[0:C, h * C:(h + 1) * C], lhsT=Pc[:, h, :], rhs=PTc[:, h, :], start=True, stop=True)
                                return rr
                            P.op("pe", mmpt, reads=[rPc, rPTc], writes=[rppt])
                    if r >= 1:
                        P.op("dve", lambda e, pu=pu, Uc=Uc, Un=Un: e.tensor_tensor(out=fl(Un), in0=pu[0:C, 0:H * C], in1=fl(Uc), op=ALU.add),
                             reads=[rpu, rUc], writes=[rUn])
                    else:
                        P.op("act", lambda e, Uc=Uc, Un=Un: e.copy(out=fl(Un), in_=fl(Uc)), reads=[rUc], writes=[rUn])
                    if r < 5:
                        P.op("act", lambda e, pp=pp, Pn=Pn: e.copy(out=fl(Pn), in_=pp[0:C, 0:H * C]), reads=[rpp], writes=[rPn])
                        if r < 4:
                            P.op("act", lambda e, ppt=ppt, PTn=PTn: e.copy(out=fl(PTn), in_=ppt[0:C, 0:H * C]), reads=[rppt], writes=[rPTn])
                    cur = nxt
                U, rU = Um[cur], rUm[cur]
                for half in range(2):
                    pw, rpw = bank()

                    def mmw(e, pw=pw, half=half, U=U):
                        rr = None
                        for hh in range(4):
                            h = half * 4 + hh
                            rr = e.matmul(pw[0:C, hh * 128:(hh + 1) * 128], lhsT=U[:, h, :], rhs=vb[:, h, :], start=True, stop=True)
                        return rr
                    P.op("pe", mmw, reads=[rU, rvb], writes=[rpw])
                    P.op("act", lambda e, pw=pw, half=half: e.copy(out=wv[:, half * 4:(half + 1) * 4, :].rearrange("p q d -> p (q d)"), in_=pw[0:C, 0:512]),
                         reads=[rpw], writes=[rwv])
                pkc, rpkc = bank()

                def mmkc(e, pkc=pkc, U=U):
                    rr = None
                    for h in range(H):
                        rr = e.matmul(pkc[:, h * C:(h + 1) * C], lhsT=kbg[:, h, :], rhs=U[:, h, :], start=True, stop=True)
                    return rr
                P.op("pe", mmkc, reads=[rU, rkbg], writes=[rpkc])
                P.op("act", lambda e, pkc=pkc: e.copy(out=fl(kcT), in_=pkc[:, 0:H * C]), reads=[rpkc], writes=[rkcT])
                for half in range(2):
                    pv_, rpv_ = bank()

                    def mmv(e, pv_=pv_, half=half):
                        rr = None
                        for hh in range(4):
                            h = half * 4 + hh
                            rr = e.matmul(pv_[0:C, hh * 128:(hh + 1) * 128], lhsT=kcT[:, h, :], rhs=Sst[:, h, :], start=True, stop=True)
                        return rr
                    P.op("pe", mmv, reads=[rkcT, rS], writes=[rpv_])
                    P.op("dve", lambda e, pv_=pv_, half=half: e.tensor_tensor(
                        out=vnew[:, half * 4:(half + 1) * 4, :].rearrange("p q d -> p (q d)"),
                        in0=wv[:, half * 4:(half + 1) * 4, :].rearrange("p q d -> p (q d)"), in1=pv_[0:C, 0:512], op=ALU.subtract),
                        reads=[rpv_, rwv], writes=[rvnew])
                po, rpo = bank()

                def mmo(e, po=po):
                    rr = None
                    for h in range(H):
                        e.matmul(po[:, h * C:(h + 1) * C], lhsT=Sst[:, h, :], rhs=qd[:, h, :], start=True, stop=False)
                        rr = e.matmul(po[:, h * C:(h + 1) * C], lhsT=vnew[:, h, :], rhs=attT[:, h, :], start=False, stop=True)
                    return rr
                P.op("pe", mmo, reads=[rS, rqd, rvnew, rattT], writes=[rpo])
                P.op("act", lambda e, po=po, cs=cs: e.copy(out=o_t[:, :, cs], in_=po[:, 0:H * C].rearrange("p (h c) -> p h c", c=C)), reads=[rpo], writes=[ro_t])
                for half in range(2):
                    pss, rpss = bank()

                    def mms(e, pss=pss, half=half):
                        rr = None
                        for hh in range(4):
                            h = half * 4 + hh
                            rr = e.matmul(pss[:, hh * 128:(hh + 1) * 128], lhsT=kst[:, h, :], rhs=vnew[:, h, :], start=True, stop=True)
                        return rr
                    P.op("pe", mms, reads=[rkst, rvnew], writes=[rpss])
                    for hh in range(4):
                        h = half * 4 + hh
                        P.op("dve", lambda e, pss=pss, h=h, hh=hh: e.scalar_tensor_tensor(
                            out=Sst[:, h, :], in0=Sst[:, h, :], scalar=E[:, h, C - 1:C], in1=pss[:, hh * 128:(hh + 1) * 128], op0=ALU.mult, op1=ALU.add),
                            reads=[rS, rE, rpss], writes=[rS])
                if c.progress is not None and ci < NCH - 1:
                    c.progress([rS])
            for h in range(H):
                pz, rpz = inproj(2048 + h * 128, 128, TT)
                zs, rzs = xcv[h % 2], rxcv[h % 2]
                P.op("act", lambda e, pz=pz, zs=zs: e.activation(out=zs[:], in_=pz[:, 0:TT], func=AF.Silu), reads=[rpz], writes=[rzs])
                sq_, rsq_ = sqb[h % 2], rsqb[h % 2]
                rs_, rrs_ = rsb[h % 2], rrsb[h % 2]
                P.op("act", lambda e, sq_=sq_, h=h: e.activation(out=sq_[:], in_=o_t[:, h, :], func=AF.Square), reads=[ro_t], writes=[rsq_])
                pn_, rpn_ = bank()
                P.op("pe", lambda e, sq_=sq_, pn_=pn_: e.matmul(pn_[:, 0:TT], lhsT=c.ones_f[:], rhs=sq_[:], start=True, stop=True), reads=[rsq_, c.r_const], writes=[rpn_])
                P.op("act", lambda e, rs_=rs_, pn_=pn_: e.activation(out=rs_[:], in_=pn_[:, 0:TT], func=AF.Sqrt, bias=EPS, scale=1.0 / 128), reads=[rpn_], writes=[rrs_])
                P.op("dve", lambda e, rs_=rs_: e.reciprocal(out=rs_[:], in_=rs_[:]), reads=[rrs_], writes=[rrs_])
                P.op("dve", lambda e, rs_=rs_, h=h: e.scalar_tensor_tensor(out=rs_[:], in0=o_t[:, h, :], scalar=gn[:, 0:1], in1=rs_[:], op0=ALU.mult, op1=ALU.mult),
                     reads=[ro_t, rgn, rrs_], writes=[rrs_])
                P.op("dve", lambda e, rs_=rs_, zs=zs, h=h: e.tensor_tensor(out=ytg[:, h, :], in0=rs_[:], in1=zs[:], op=ALU.mult), reads=[rrs_, rzs], writes=[ryt[h]])
                P.dma("sp", ych[h % 2], lambda e, h=h, t=t: e.dma_start(out=y_dst(t, h * 128, (h + 1) * 128), in_=ytg[:, h, :]), reads=[ryt[h]],
                      writes=list(ry_d(t)) if callable(ry_d) else [])
            if tile_done is not None:
                tile_done(t, ryt[H - 1])
        P.barrier()


def build_gdn_consts(c, TT):
    P, nc = c.P, c.nc
    C, H = 64, 8
    gk = {}
    shapes = dict(sel=[8, H * 128], nsel=[8, H * 128], mB=[C, H * C], mBT=[C, H * C], mAtt=[C, H * C], eye8=[C, H * C], lastsel=[C, C], cmask=[8, TT])
    for nm, sh in shapes.items():
        gk[nm] = nc.dram_tensor("gk_%s" % nm, sh, F32).ap()
    c.gdn_consts = gk
    c.r_gk = Res("gk")
    with ExitStack() as es2:
        u = _uid()
        sb2 = lambda name, shape: es2.enter_context(nc.sbuf_tensor("sgk%d_%s" % (u, name), shape, F32))
        rK = Res("gkbuild")
        sel = sb2("sel", [8, H, 128]); nsel = sb2("nsel", [8, H, 128])
        mB = sb2("mB", [C, H, C]); mBT = sb2("mBT", [C, H, C]); mAtt = sb2("mAtt", [C, H, C]); eye8 = sb2("eye8", [C, H, C])
        lastsel = sb2("lastsel", [C, C]); cmask = sb2("cmask", [8, TT])

        def cst(fn):
            P.op("pool", fn, reads=[rK], writes=[rK])
        cst(lambda e: e.memset(sel[:], 0.0))
        cst(lambda e: e.affine_select(out=sel[:], in_=sel[:], pattern=[[-1, H], [0, 128]], compare_op=ALU.not_equal, fill=1.0, base=0, channel_multiplier=1))
        cst(lambda e: e.memset(nsel[:], 0.0))
        cst(lambda e: e.affine_select(out=nsel[:], in_=nsel[:], pattern=[[-1, H], [0, 128]], compare_op=ALU.not_equal, fill=-1.0, base=0, channel_multiplier=1))
        cst(lambda e: e.memset(mB[:], -1.0))
        cst(lambda e: e.affine_select(out=mB[:], in_=mB[:], pattern=[[0, H], [-1, C]], compare_op=ALU.is_gt, fill=0.0, base=0, channel_multiplier=1))
        cst(lambda e: e.memset(mBT[:], -1.0))
        cst(lambda e: e.affine_select(out=mBT[:], in_=mBT[:], pattern=[[0, H], [1, C]], compare_op=ALU.is_gt, fill=0.0, base=0, channel_multiplier=-1))
        cst(lambda e: e.memset(mAtt[:], 1.0))
        cst(lambda e: e.affine_select(out=mAtt[:], in_=mAtt[:], pattern=[[0, H], [1, C]], compare_op=ALU.is_ge, fill=0.0, base=0, channel_multiplier=-1))
        cst(lambda e: e.memset(eye8[:], 0.0))
        cst(lambda e: e.affine_select(out=eye8[:], in_=eye8[:], pattern=[[0, H], [-1, C]], compare_op=ALU.not_equal, fill=1.0, base=0, channel_multiplier=1))
        cst(lambda e: e.memset(lastsel[:], 0.0))
        cst(lambda e: e.affine_select(out=lastsel[:], in_=lastsel[:], pattern=[[0, C]], compare_op=ALU.not_equal, fill=1.0, base=-(C - 1), channel_multiplier=1))
        cst(lambda e: e.memset(cmask[:], 1.0))
        cst(lambda e: e.memset(cmask[:].rearrange("p (n c) -> p n c", c=C)[:, :, 0:1], 0.0))
        ch = P.chan("gkst")
        for nm, t in (("sel", sel), ("nsel", nsel), ("mB", mB), ("mBT", mBT), ("mAtt", mAtt), ("eye8", eye8)):
            P.dma("sp", ch, lambda e, nm=nm, t=t: e.dma_start(out=gk[nm], in_=t[:].rearrange("p h c -> p (h c)")), reads=[rK], writes=[c.r_gk])
        P.dma("sp", ch, lambda e: e.dma_start(out=gk["lastsel"], in_=lastsel[:]), reads=[rK], writes=[c.r_gk])
        P.dma("sp", ch, lambda e: e.dma_start(out=gk["cmask"], in_=cmask[:]), reads=[rK], writes=[c.r_gk])
        P.barrier()


def build_gdn_program(S, TT):
    nc = bass.Bass("TRN2", target_bir_lowering=False)
    dt = lambda name, shape, dtype, kind_: nc.dram_tensor(name, shape, dtype, kind=kind_).ap()
    hn = dt("hn", [D, S], BF16, "ExternalInput")
    w = dt("w", [D, 3088], F32, "ExternalInput")
    cvec = dt("cvec", [128, 16, 4], F32, "ExternalInput")
    gvec = dt("gvec", [128, 1], F32, "ExternalInput")
    pvec = dt("pvec", [8, 2], F32, "ExternalInput")
    y = dt("y", [1024, S], BF16, "ExternalOutput")
    with ExitStack() as es:
        P = Prog(nc, es)
        c = make_ctx(nc, es, P, nslots=3)
        build_gdn_consts(c, TT)
        emit_gdn(c, S, TT, lambda t: hn[:, t * TT:(t + 1) * TT].rearrange("(kc p) t -> p kc t", p=128), prep_gdn_weights(c, w), cvec, gvec, pvec,
                 lambda t, r0, r1: y[r0:r1, t * TT:(t + 1) * TT])
        P.emit()
    return nc


def build_fused_program(S=4096, depth=4, NB=2, debug=False):
    GROUPS = [[4 * b + q for q in range(4)] for b in range(NB)]
    Sc = S // 4
    T = 512
    TT = 256
    NQ = Sc // 256
    NYK = S // 512
    nc = bass.Bass("TRN2", target_bir_lowering=False)
    dt = lambda name, shape, dtype, kind_: nc.dram_tensor(name, shape, dtype, kind=kind_).ap()
    xin = dt("x", [D, Sc], F32, "ExternalInput")
    mem = dt("mem", [MEM, D], F32, "ExternalInput")
    vecs = dt("vecs", [128, 64 * depth], F32, "ExternalInput")
    outd = dt("out", [D, Sc], F32, "ExternalOutput")
    W = []
    for i in range(depth):
        kind = "lru" if i % 2 == 0 else "gdn"
        ychan = 704 if kind == "lru" else 1024
        d = dict(kind=kind, ychan=ychan)
        d["w_xq"] = dt("w_xq%d" % i, [D, D], F32, "ExternalInput")
        d["w_kv"] = dt("w_kv%d" % i, [D, 2 * D], F32, "ExternalInput")
        d["w_out"] = dt("w_out%d" % i, [4 * ychan + D, D], F32, "ExternalInput")
        d["w_fi"] = dt("w_fi%d" % i, [D, 2 * D_FF], F32, "ExternalInput")
        d["w_fo"] = dt("w_fo%d" % i, [D_FF, D], F32, "ExternalInput")
        if kind == "lru":
            d["wx"] = dt("wx%d" % i, [D, 704], F32, "ExternalInput")
            d["wg"] = dt("wg%d" % i, [D, 704], F32, "ExternalInput")
            d["ga"] = dt("ga%d" % i, [4, 176, 176], F32, "ExternalInput")
            d["gx"] = dt("gx%d" % i, [4, 176, 176], F32, "ExternalInput")
            d["lvec"] = dt("lvec%d" % i, [88, 8, 8], F32, "ExternalInput")
        else:
            d["gw"] = dt("gw%d" % i, [D, 3088], F32, "ExternalInput")
            d["cvec"] = dt("cvec%d" % i, [128, 16, 4], F32, "ExternalInput")
            d["gvec"] = dt("gvec%d" % i, [128, 1], F32, "ExternalInput")
            d["pvec"] = dt("pvec%d" % i, [8, 2], F32, "ExternalInput")
        W.append(d)
    hn_loc = nc.dram_tensor("hn_loc", [NQ, D, 256], BF16).ap()
    hn_all = nc.dram_tensor("hn_all", [NQ, 4 * D, 256], BF16).ap()
    r_hn_loc = [Res("hn_loc%d" % q) for q in range(NQ)]
    r_hn_all = [Res("hn_all%d" % q) for q in range(NQ)]
    ybuf = {}
    for kind, ychan in (("lru", 704), ("gdn", 1024)):
        ybuf[kind] = (nc.dram_tensor("y_loc_" + kind, [NYK, ychan, 512], BF16).ap(),
                      nc.dram_tensor("y_all_" + kind, [NYK, 4 * ychan, 512], BF16).ap(),
                      [Res("yl%d" % k) for k in range(NYK)], [Res("ya%d" % k) for k in range(NYK)])
    if debug:
        dbg_hn = dt("dbg_hn", [NQ, 4 * D, 256], BF16, "ExternalOutput")
        dbg_y = dt("dbg_y", [NYK, 4 * 704, 512], BF16, "ExternalOutput")
        dbg_yl = dt("dbg_yl", [NYK, 704, 512], BF16, "ExternalOutput")
    with ExitStack() as es:
        P = Prog(nc, es)
        c = make_ctx(nc, es, P, nslots=4)
        c.vecs = c.sb("vecs", [128, 64 * depth], F32)
        c.r_vecs = Res("vecs")
        c.h = c.sb("h", [128, KD, Sc], F32)
        c.rh = Res("h")
        c.ych = P.chan()
        c.och = P.chan()
        P.dma("sp", P.chan(), lambda e: e.dma_start(out=c.vecs[:], in_=vecs), writes=[c.r_vecs])
        P.dma("sp", P.chan(), lambda e: e.dma_start(out=c.h[:], in_=xin.rearrange("(kc p) t -> p kc t", p=128)), writes=[c.rh])

        hn_ch = [P.chan() for _ in range(NQ)]
        pid_cache = {}
        build_gdn_consts(c, TT)

        def hn_exchange(tsl):
            hn_ = c.hn
            for s0 in range(0, T, 256):
                q = (tsl.start + s0) // 256
                P.dma("sp", hn_ch[q], lambda e, q=q, s0=s0, hn_=hn_: e.dma_start(out=hn_loc[q].rearrange("(kc p) t -> p kc t", p=128),
                                                                              in_=hn_[:, :, s0:s0 + 256]),
                      reads=[c.rhn], writes=[r_hn_loc[q]])
                P.coll(P.cc_chan("hn%d" % q), lambda e, q=q: e.collective_compute("AllGather", ALU.bypass, replica_groups=GROUPS,
                                                                               ins=[hn_loc[q].opt()], outs=[hn_all[q].opt()]),
                       reads=[r_hn_loc[q]], writes=[r_hn_all[q]])

        with ExitStack() as es2:
            _pass_bufs(c, es2, T, "a")
            for p0 in range(0, Sc, T):
                emit_tok_pass(c, dict(g_next=0, nyc=0, rpc_y=128), slice(p0, p0 + T), T, hn_dst=hn_exchange, out_dst=None, do_main=False)
            P.barrier()
        preps = {}
        gpreps = {}
        for i in range(depth):
            d = W[i]
            kind, ychan = d["kind"], d["ychan"]
            final = (i == depth - 1)
            y_loc, y_all, r_yl, r_ya = ybuf[kind]
            tasks = []
            if kind == "gdn" and i not in gpreps:
                gpreps[i] = prep_gdn_weights(c, d["gw"])
            preps[i] = prep_layer_weights(c, d["w_xq"], d["w_kv"], d["w_out"], d["w_fi"], d["w_fo"], 32, 88 if kind == "lru" else 128, tasks)
            if i + 1 < depth and W[i + 1]["kind"] == "gdn":
                gpreps[i + 1] = prep_gdn_weights(c, W[i + 1]["gw"], tasks)
            NTB = S // TT
            ncalls = [NTB * 8 if kind == "lru" else NTB * (TT // 64 - 1)]

            def progress(gate, tasks=tasks, ncalls=ncalls):
                k = -(-len(tasks) // max(1, ncalls[0]))
                ncalls[0] -= 1
                for _ in range(k):
                    if tasks:
                        pr_, bi_ = tasks.pop(0)
                        prep_issue(c, pr_, bi_, gate)
            c.progress = progress

            def hn_src(t):
                r, q = t // NQ, t % NQ
                return hn_all[q][r * D:(r + 1) * D, :].rearrange("(kc p) t -> p kc t", p=128)

            def y_dst(t, r0, r1, y_loc=y_loc):
                return y_loc[t // 2][r0:r1, (t % 2) * 256:(t % 2 + 1) * 256]

            ystores = [[] for _ in range(NYK)]

            def tile_done(t, gate, y_loc=y_loc, y_all=y_all, ystores=ystores, r_ya=r_ya):
                if t % 2 == 1:
                    k = t // 2
                    r_yl = ystores
                    P.coll(P.cc_chan("y%d" % k), lambda e, k=k: e.collective_compute("AllGather", ALU.bypass, replica_groups=GROUPS,
                                                                                  ins=[y_loc[k].opt()], outs=[y_all[k].opt()]),
                           reads=r_yl[k], writes=[r_ya[k]])

            def ry_d(t, ystores=ystores):
                r = Res("ys")
                ystores[t // 2].append(r)
                return [r]
            if kind == "lru":
                emit_lru(c, S, TT, hn_src, d["wx"], d["wg"], d["ga"], d["gx"], d["lvec"], y_dst, tile_done, rhn_d=r_hn_all, ry_d=ry_d)
            else:
                emit_gdn(c, S, TT, hn_src, gpreps[i], d["cvec"], d["gvec"], d["pvec"], y_dst, tile_done, rhn_d=r_hn_all, ry_d=ry_d)
            if debug and i == 0:
                dch = P.chan()
                P.dma("sp", dch, lambda e: e.dma_start(out=dbg_hn, in_=hn_all), reads=r_hn_all)
                P.dma("sp", dch, lambda e: e.dma_start(out=dbg_y, in_=y_all), reads=r_ya)
                P.dma("sp", dch, lambda e: e.dma_start(out=dbg_yl, in_=y_loc), reads=r_yl)
            c.progress = None
            while tasks:
                pr_, bi_ = tasks.pop(0)
                prep_issue(c, pr_, bi_)
            with ExitStack() as es3:
                c.KT = es3.enter_context(nc.sbuf_tensor("sb_KT%d" % i, [128, KD, MEM], BF16)); c.rKT = Res("KT")
                c.V = es3.enter_context(nc.sbuf_tensor("sb_V%d" % i, [128, 2, D], BF16)); c.rV = Res("V")
                rpc_y = 88 if kind == "lru" else 128
                lay = dict(g_mix=64 * i, g_mem=64 * i + 16, g_ffn=64 * i + 32, g_next=64 * i + 48, nyc=32, rpc_y=rpc_y, y_res=r_ya)
                lay.update(preps[i])

                def y_ap(tsl, e, y_all=y_all, rpc_y=rpc_y):
                    if "cidx" not in pid_cache:
                        pid_cache["cidx"] = e.partition_id() % 4
                    cidx = pid_cache["cidx"]
                    ya4 = y_all.rearrange("(c q) r t -> c q r t", c=4)
                    return ya4[bass.ds(cidx, 1), tsl.start // 512].rearrange("o (kc p) t -> p (o kc) t", p=rpc_y)
                lay["y_ap"] = y_ap
                emit_kv(c, lay, mem, lay["p_kv"], lay["g_mem"])
                with ExitStack() as es2:
                    _pass_bufs(c, es2, T, "c%d" % i)
                    for p0 in range(0, Sc, T):
                        tsl = slice(p0, p0 + T)
                        emit_tok_pass(c, lay, tsl, T, hn_dst=None if final else hn_exchange,
                                      out_dst=(lambda tsl: outd[:, tsl].rearrange("(kc p) t -> p kc t", p=128)) if final else None)
                    P.barrier()
        P.emit()
    return nc


_PROGS = {}


def _prog(key, fn):
    if key not in _PROGS:
        _PROGS[key] = fn()
    return _PROGS[key]


def _vecs(I, i, gnext):
    v = np.zeros((128, 64), np.float32)
    for k, g in enumerate([I["norm_mix_g"][i], I["norm_mem_g"][i], I["norm_ffn_g"][i], gnext]):
        v[:, 16 * k:16 * (k + 1)] = np.asarray(g, np.float32).reshape(16, 128).T
    return v


def _lru_inputs(I, j, g, hn):
    ch = np.arange(g * 704, (g + 1) * 704)
    cw = I["lru_conv_w"][j]
    lvec = np.stack([cw[0, ch], cw[1, ch], cw[2, ch], cw[3, ch], I["lru_conv_b"][j][ch], I["lru_gate_a_b"][j][ch],
                     I["lru_gate_x_b"][j][ch], I["lru_lambda"][j][ch]], -1)
    lvec = np.ascontiguousarray(lvec.reshape(8, 88, 8).transpose(1, 0, 2))
    return {"hn": hn, "wx": np.ascontiguousarray(I["lru_w_in"][j][:, ch]), "wg": np.ascontiguousarray(I["lru_w_in"][j][:, D_RNN + ch]),
            "ga": np.ascontiguousarray(I["lru_gate_a_w"][j][4 * g:4 * g + 4]), "gx": np.ascontiguousarray(I["lru_gate_x_w"][j][4 * g:4 * g + 4]),
            "lvec": lvec}


def _gdn_inputs(I, j, g, hn):
    Wi = I["gdn_w_in"][j]
    hq = np.arange(4 * g * 128, (4 * g + 4) * 128)
    hv = np.arange(8 * g * 128, (8 * g + 8) * 128)
    cols = np.concatenate([hq, GQK + hq, 2 * GQK + hv, 2 * GQK + GV + hv, 2 * GQK + 2 * GV + np.arange(8 * g, 8 * g + 8),
                           2 * GQK + 2 * GV + 32 + np.arange(8 * g, 8 * g + 8)])
    W = np.ascontiguousarray(Wi[:, cols])
    cch = np.concatenate([hq, GQK + hq, 2 * GQK + hv])
    cvec = np.ascontiguousarray(I["gdn_conv_w"][j][:, cch].T.reshape(16, 128, 4).transpose(1, 0, 2))
    gvec = np.ascontiguousarray(np.asarray(I["gdn_norm_g"][j], np.float32).reshape(128, 1))
    pvec = np.ascontiguousarray(np.stack([I["gdn_a_log"][j][8 * g:8 * g + 8], I["gdn_dt_bias"][j][8 * g:8 * g + 8]], -1).astype(np.float32))
    return {"hn": hn, "w": W, "cvec": cvec, "gvec": gvec, "pvec": pvec}


def kernel_unfused(I, NB=2, S=4096, depth=4):
    I = {k: np.asarray(v) for k, v in I.items()}
    Sc = S // 4
    T = min(512, Sc)
    TT = min(512, S)
    ncores = 4 * NB
    cores = list(range(ncores))
    x = I["x"]
    ncA = _prog(("tokA", Sc, T), lambda: build_tok_program(Sc, T, "lru", True, False, do_main=False))
    hs = [np.ascontiguousarray(x[i // 4, (i % 4) * Sc:(i % 4 + 1) * Sc, :].T) for i in cores]
    res = run_bass_kernel_spmd(ncA, [{"hin": hs[i], "vecs": _vecs(I, 0, I["norm_mix_g"][0])} for i in cores], core_ids=cores).results
    hn = [np.concatenate([res[b * 4 + c]["hnn"] for c in range(4)], axis=1) for b in range(NB)]
    out = None
    for i in range(depth):
        j = i // 2
        kind = "lru" if i % 2 == 0 else "gdn"
        final = (i == depth - 1)
        if kind == "lru":
            ncB = _prog(("lru", S, TT), lambda: build_lru_program(S, TT))
            ins = [_lru_inputs(I, j, c % 4, hn[c // 4]) for c in cores]
        else:
            ncB = _prog(("gdn", S, TT), lambda: build_gdn_program(S, TT))
            ins = [_gdn_inputs(I, j, c % 4, hn[c // 4]) for c in cores]
        res = run_bass_kernel_spmd(ncB, ins, core_ids=cores).results
        y = [np.concatenate([res[b * 4 + g]["y"] for g in range(4)], axis=0) for b in range(NB)]
        ncC = _prog(("tokC", Sc, T, kind, final), lambda: build_tok_program(Sc, T, kind, False, final))
        w_in = I["lru_w_in"][j] if kind == "lru" else I["gdn_w_in"][j]
        w_xq = np.ascontiguousarray(w_in[:, -D:])
        w_out = I["lru_w_out"][j] if kind == "lru" else I["gdn_w_out"][j]
        gnext = I["final_norm_g"] if final else I["norm_mix_g"][i + 1]
        vecs = _vecs(I, i, gnext)
        ins = []
        for c in cores:
            b, q = c // 4, c % 4
            ins.append({"hin": hs[c], "vecs": vecs, "y": np.ascontiguousarray(y[b][:, q * Sc:(q + 1) * Sc]), "mem": I["mem"][b],
                        "w_xq": w_xq, "w_kv": I["mem_kv_w"][i], "w_out": w_out, "w_fi": I["ffn_w_in"][i], "w_fo": I["ffn_w_out"][i]})
        res = run_bass_kernel_spmd(ncC, ins, core_ids=cores).results
        if final:
            out = np.zeros((NB, S, D), np.float32)
            for c in cores:
                out[c // 4, (c % 4) * Sc:(c % 4 + 1) * Sc, :] = res[c]["out"].T
        else:
            hs = [res[c]["hout"] for c in cores]
            hn = [np.concatenate([res[b * 4 + c]["hnn"] for c in range(4)], axis=1) for b in range(NB)]
    return out


def kernel_fused(I, NB=2, S=4096, depth=4, debug=False):
    I = {k: np.asarray(v) for k, v in I.items()}
    Sc = S // 4
    ncores = 4 * NB
    cores = list(range(ncores))
    nc = _prog(("fused", S, depth, NB, debug), lambda: build_fused_program(S, depth, NB, debug))
    vecs = np.concatenate([_vecs(I, i, I["final_norm_g"] if i == depth - 1 else I["norm_mix_g"][i + 1]) for i in range(depth)], axis=1)
    shared = {}
    for i in range(depth):
        j = i // 2
        kind = "lru" if i % 2 == 0 else "gdn"
        w_in = I["lru_w_in"][j] if kind == "lru" else I["gdn_w_in"][j]
        shared["w_xq%d" % i] = np.ascontiguousarray(w_in[:, -D:])
        shared["w_kv%d" % i] = I["mem_kv_w"][i]
        shared["w_out%d" % i] = I["lru_w_out"][j] if kind == "lru" else I["gdn_w_out"][j]
        shared["w_fi%d" % i] = I["ffn_w_in"][i]
        shared["w_fo%d" % i] = I["ffn_w_out"][i]
    ins = []
    for cidx in cores:
        b, g = cidx // 4, cidx % 4
        m = dict(shared)
        m["x"] = np.ascontiguousarray(I["x"][b, g * Sc:(g + 1) * Sc, :].T)
        m["mem"] = I["mem"][b]
        m["vecs"] = vecs
        for i in range(depth):
            j = i // 2
            if i % 2 == 0:
                li = _lru_inputs(I, j, g, None)
                for k in ("wx", "wg", "ga", "gx", "lvec"):
                    m["%s%d" % (k, i)] = li[k]
            else:
                gi = _gdn_inputs(I, j, g, None)
                m["gw%d" % i] = gi["w"]
                for k in ("cvec", "gvec", "pvec"):
                    m["%s%d" % (k, i)] = gi[k]
        ins.append(m)
    res = run_bass_kernel_spmd(nc, ins, core_ids=cores).results
    out = np.zeros((NB, S, D), np.float32)
    for cidx in cores:
        out[cidx // 4, (cidx % 4) * Sc:(cidx % 4 + 1) * Sc, :] = res[cidx]["out"].T
    if debug:
        return out, res
    return out


def kernel(**inputs):
    return kernel_fused(inputs)
```
